# Optimizing a Trainium2 kernel written in Bass

```python
import numpy as np
import jax
import jax.numpy as jnp
from jax import lax

D_MODEL = 1024
BATCH = 1
SEQ = 16384
DEPTH = 1

N_META = 16
GRID_W = 64
EPS = 1e-6

MLA_HEADS = 8
MLA_NOPE = 128
MLA_ROPE = 64
MLA_V = 128
MLA_Q_RANK = 256
MLA_KV_RANK = 256
MLA_WIDTH = MLA_HEADS * MLA_V
ROPE_BASE = 10000.0
Q_BLOCK = 128

NA_HEADS = 16
NA_HEAD_DIM = 64
NA_WIDTH = NA_HEADS * NA_HEAD_DIM
NA_WIN_R = 8
NA_WIN_C = 16

D_INNER = MLA_WIDTH + NA_WIDTH
IN_SIZES = (MLA_Q_RANK, MLA_KV_RANK, MLA_ROPE, MLA_WIDTH, NA_WIDTH, NA_WIDTH, NA_WIDTH, NA_WIDTH)
D_IN_PROJ = sum(IN_SIZES)

kernel_name = "hybrid_mla_neighbourhood_attention_encoder_layer"


def _rmsnorm(x, w):
    xf = x.astype(jnp.float32)
    y = xf * lax.rsqrt(jnp.mean(xf * xf, axis=-1, keepdims=True) + EPS)
    return (y * w.astype(jnp.float32)).astype(x.dtype)


def _rope(x, cos, sin):
    x1, x2 = jnp.split(x.astype(jnp.float32), 2, axis=-1)
    return jnp.concatenate([x1 * cos - x2 * sin, x2 * cos + x1 * sin], axis=-1).astype(x.dtype)


def _mla(q_lat, kv_lat, k_pe, w_uq, w_ukv, q_lat_w, kv_lat_w, qn_w, qpe_w, kn_w, kpe_w):
    B, L, _ = q_lat.shape
    q = (_rmsnorm(q_lat, q_lat_w) @ w_uq).reshape(B, L, MLA_HEADS, MLA_NOPE + MLA_ROPE)
    kv = (_rmsnorm(kv_lat, kv_lat_w) @ w_ukv).reshape(B, L, MLA_HEADS, MLA_NOPE + MLA_V)
    q_nope, q_pe = q[..., :MLA_NOPE], q[..., MLA_NOPE:]
    k_nope, v = kv[..., :MLA_NOPE], kv[..., MLA_NOPE:]
    q_nope = _rmsnorm(q_nope, qn_w)
    k_nope = _rmsnorm(k_nope, kn_w)
    pos = jnp.arange(L, dtype=jnp.float32)
    inv_freq = ROPE_BASE ** (-(jnp.arange(0, MLA_ROPE, 2, dtype=jnp.float32) / MLA_ROPE))
    ang = pos[:, None] * inv_freq[None, :]
    cos, sin = jnp.cos(ang), jnp.sin(ang)
    q_pe = _rope(_rmsnorm(q_pe, qpe_w), cos[None, :, None, :], sin[None, :, None, :])
    k_pe = _rope(_rmsnorm(k_pe, kpe_w), cos[None], sin[None])
    scale = (MLA_NOPE + MLA_ROPE) ** -0.5

    n_blk = -(-L // Q_BLOCK)
    pad = n_blk * Q_BLOCK - L
    qn = jnp.pad(q_nope, ((0, 0), (0, pad), (0, 0), (0, 0)))
    qn = qn.reshape(B, n_blk, Q_BLOCK, MLA_HEADS, MLA_NOPE).transpose(1, 0, 2, 3, 4)
    qp = jnp.pad(q_pe, ((0, 0), (0, pad), (0, 0), (0, 0)))
    qp = qp.reshape(B, n_blk, Q_BLOCK, MLA_HEADS, MLA_ROPE).transpose(1, 0, 2, 3, 4)

    def block(args):
        qn_b, qp_b = args
        s = (jnp.einsum('bqhd,bkhd->bhqk', qn_b, k_nope)
             + jnp.einsum('bqhr,bkr->bhqk', qp_b, k_pe))
        p = jax.nn.softmax(s.astype(jnp.float32) * scale, axis=-1).astype(v.dtype)
        return jnp.einsum('bhqk,bkhd->bqhd', p, v)

    o = lax.map(block, (qn, qp))
    o = o.transpose(1, 0, 2, 3, 4).reshape(B, n_blk * Q_BLOCK, MLA_WIDTH)
    return o[:, :L]


def _na(q, k, v, q_w, k_w, rel_bias, meta_bias):
    B, L, _ = q.shape
    n_real = L - N_META
    rows = n_real // GRID_W
    win_r = min(NA_WIN_R, rows)
    n_win = win_r * NA_WIN_C
    scale = NA_HEAD_DIM ** -0.5
    q = _rmsnorm(q.reshape(B, L, NA_HEADS, NA_HEAD_DIM), q_w)
    k = _rmsnorm(k.reshape(B, L, NA_HEADS, NA_HEAD_DIM), k_w)
    v = v.reshape(B, L, NA_HEADS, NA_HEAD_DIM)
    q_meta, k_meta, v_meta = q[:, :N_META], k[:, :N_META], v[:, :N_META]
    q_grid = q[:, N_META:].reshape(B, rows, GRID_W, NA_HEADS, NA_HEAD_DIM)
    k_grid = k[:, N_META:].reshape(B, rows, GRID_W, NA_HEADS, NA_HEAD_DIM)
    v_grid = v[:, N_META:].reshape(B, rows, GRID_W, NA_HEADS, NA_HEAD_DIM)

    cols = np.arange(GRID_W)
    c0 = np.clip(cols - NA_WIN_C // 2, 0, GRID_W - NA_WIN_C)
    col_idx_np = c0[:, None] + np.arange(NA_WIN_C)[None, :]
    col_idx = jnp.asarray(col_idx_np, dtype=jnp.int32)
    dc_idx = jnp.asarray(col_idx_np - cols[:, None] + (NA_WIN_C - 1), dtype=jnp.int32)
    meta_b = meta_bias.astype(jnp.float32)[None, :, None, :]

    def row_block(args):
        r, q_row = args
        r0 = jnp.clip(r - win_r // 2, 0, rows - win_r)
        k_rows = lax.dynamic_slice_in_dim(k_grid, r0, win_r, axis=1)
        v_rows = lax.dynamic_slice_in_dim(v_grid, r0, win_r, axis=1)
        k_win = k_rows[:, :, col_idx]
        v_win = v_rows[:, :, col_idx]
        dr_idx = r0 + jnp.arange(win_r, dtype=jnp.int32) - r + (NA_WIN_R - 1)
        bias = rel_bias[:, dr_idx[None, :, None], dc_idx[:, None, :]]
        s_win = (jnp.einsum('bqhd,brqchd->bhqrc', q_row, k_win).astype(jnp.float32) * scale
                 + bias.astype(jnp.float32)[None])
        s_meta = jnp.einsum('bqhd,bmhd->bhqm', q_row, k_meta).astype(jnp.float32) * scale + meta_b
        s = jnp.concatenate([s_win.reshape(B, NA_HEADS, GRID_W, n_win), s_meta], axis=-1)
        p = jax.nn.softmax(s, axis=-1).astype(v.dtype)
        p_win = p[..., :n_win].reshape(B, NA_HEADS, GRID_W, win_r, NA_WIN_C)
        p_meta = p[..., n_win:]
        return (jnp.einsum('bhqrc,brqchd->bqhd', p_win, v_win)
                + jnp.einsum('bhqm,bmhd->bqhd', p_meta, v_meta))

    o_grid = lax.map(row_block, (jnp.arange(rows, dtype=jnp.int32), q_grid.transpose(1, 0, 2, 3, 4)))
    o_real = o_grid.transpose(1, 0, 2, 3, 4).reshape(B, n_real, NA_WIDTH)

    s_mm = jnp.einsum('bqhd,bmhd->bhqm', q_meta, k_meta).astype(jnp.float32) * scale + meta_b
    p_mm = jax.nn.softmax(s_mm, axis=-1).astype(v.dtype)
    o_meta = jnp.einsum('bhqm,bmhd->bqhd', p_mm, v_meta).reshape(B, N_META, NA_WIDTH)
    return jnp.concatenate([o_meta, o_real], axis=1)


def setup_inputs(seed: int = 0) -> dict:
    key = jax.random.key(seed)
    ks = jax.random.split(key, 20)
    f32 = jnp.float32

    def nrm(k, shape, scale):
        return jax.random.normal(k, shape, f32) * scale

    def gain(k, shape):
        return 1.0 + 0.05 * jax.random.normal(k, shape, f32)

    return {
        "x": jax.random.normal(ks[0], (BATCH, SEQ, D_MODEL), f32),
        "meta_tokens": nrm(ks[1], (N_META, D_MODEL), 1.0),
        "norm_w": gain(ks[2], (DEPTH, D_MODEL)),
        "w_in": nrm(ks[3], (DEPTH, D_MODEL, D_IN_PROJ), D_MODEL ** -0.5),
        "q_lat_norm_w": gain(ks[4], (DEPTH, MLA_Q_RANK)),
        "kv_lat_norm_w": gain(ks[5], (DEPTH, MLA_KV_RANK)),
        "w_uq": nrm(ks[6], (DEPTH, MLA_Q_RANK, MLA_HEADS * (MLA_NOPE + MLA_ROPE)), MLA_Q_RANK ** -0.5),
        "w_ukv": nrm(ks[7], (DEPTH, MLA_KV_RANK, MLA_HEADS * (MLA_NOPE + MLA_V)), MLA_KV_RANK ** -0.5),
        "mla_qn_w": gain(ks[8], (DEPTH, MLA_NOPE)),
        "mla_qpe_w": gain(ks[9], (DEPTH, MLA_ROPE)),
        "mla_kn_w": gain(ks[10], (DEPTH, MLA_NOPE)),
        "mla_kpe_w": gain(ks[11], (DEPTH, MLA_ROPE)),
        "na_q_norm_w": gain(ks[12], (DEPTH, NA_HEAD_DIM)),
        "na_k_norm_w": gain(ks[13], (DEPTH, NA_HEAD_DIM)),
        "na_rel_bias": nrm(ks[14], (DEPTH, NA_HEADS, 2 * NA_WIN_R - 1, 2 * NA_WIN_C - 1), 0.5),
        "na_meta_bias": nrm(ks[15], (DEPTH, NA_HEADS, N_META), 0.5),
        "w_out": nrm(ks[16], (DEPTH, D_INNER, D_MODEL), D_INNER ** -0.5),
    }


def reference(x, meta_tokens, norm_w, w_in, q_lat_norm_w, kv_lat_norm_w, w_uq, w_ukv,
              mla_qn_w, mla_qpe_w, mla_kn_w, mla_kpe_w, na_q_norm_w, na_k_norm_w,
              na_rel_bias, na_meta_bias, w_out):
    B = x.shape[0]
    meta = jnp.broadcast_to(meta_tokens.astype(x.dtype)[None], (B, N_META, D_MODEL))
    h_res = jnp.concatenate([meta, x], axis=1)
    split_at = [int(s) for s in np.cumsum(IN_SIZES)[:-1]]
    for l in range(DEPTH):
        h = _rmsnorm(h_res, norm_w[l])
        proj = h @ w_in[l]
        q_lat, kv_lat, k_pe, g_mla, na_q, na_k, na_v, g_na = jnp.split(proj, split_at, axis=-1)
        y_mla = _mla(q_lat, kv_lat, k_pe, w_uq[l], w_ukv[l], q_lat_norm_w[l], kv_lat_norm_w[l],
                     mla_qn_w[l], mla_qpe_w[l], mla_kn_w[l], mla_kpe_w[l]) * jax.nn.silu(g_mla)
        y_na = _na(na_q, na_k, na_v, na_q_norm_w[l], na_k_norm_w[l],
                   na_rel_bias[l], na_meta_bias[l]) * jax.nn.silu(g_na)
        h_res = h_res + jnp.concatenate([y_mla, y_na], axis=-1) @ w_out[l]
    return h_res[:, N_META:]
```

```python
import numpy as np
from contextlib import ExitStack
import concourse.bass as bass
import concourse.mybir as mybir
from concourse.bass_utils import run_bass_kernel_spmd

F32 = mybir.dt.float32
BF16 = mybir.dt.bfloat16
AF = mybir.ActivationFunctionType
ALU = mybir.AluOpType
AX = mybir.AxisListType

NCORES = 8
D = 1024
SEQ = 16384
NMETA = 16
TK = SEQ + NMETA
NT = 129
OWN = 2048
WROWS = 40
WTOK = WROWS * 64
WALL = WTOK + NMETA
H = 8
NH = 16
EPS = 1e-6
GRID_W = 64
ROWS = 256
NEG = -30000.0
MLA_SCALE = 192 ** -0.5
NA_SCALE = 0.125


class Prog:
    CE = ("pe", "act", "dve", "pool")

    def __init__(self):
        self.ops = {e: [] for e in ("pe", "act", "dve", "pool", "sp")}
        self.sems = {}
        self.cnt = {}
        self.waited = {e: {} for e in self.ops}
        self.pending = {e: [] for e in self.ops}
        self.dma_sems = []

    def _push(self, eng, fn, waits, inc, n):
        ws = list(self.pending[eng]) + [w for w in waits if w is not None]
        self.pending[eng] = []
        mx = {}
        for (s, v) in ws:
            if v > mx.get(s, 0):
                mx[s] = v
        fw = []
        for s, v in mx.items():
            if self.waited[eng].get(s, 0) >= v:
                continue
            self.waited[eng][s] = v
            fw.append((s, v))
        tok = None
        if inc is not None:
            self.cnt[inc] = self.cnt.get(inc, 0) + n
            tok = (inc, self.cnt[inc])
        self.ops[eng].append((fn, tuple(fw), inc, n))
        return tok

    def op(self, eng, fn, waits=(), inc=True):
        return self._push(eng, fn, waits, eng if inc else None, 1)

    def dma(self, eng, sem, fn, waits=()):
        if sem not in self.dma_sems:
            self.dma_sems.append(sem)
        return self._push(eng, fn, waits, sem, 16)

    def barrier(self):
        toks = [(s, v) for s, v in self.cnt.items()]
        for e in self.pending:
            self.pending[e] = list(toks)

    def replay(self, eng, e):
        for fn, waits, inc, n in self.ops[eng]:
            for (s, v) in waits:
                e.wait_ge(self.sems[s], v)
            ins = fn(e)
            if inc is not None:
                ins.then_inc(self.sems[inc], n)


class Arena:
    def __init__(self, t, cols):
        self.t, self.cols, self.off = t, cols, 0

    def reset(self):
        self.off = 0

    def take(self, n):
        a = self.off
        self.off += (n + 7) // 8 * 8
        assert self.off <= self.cols, (self.off, self.cols)
        return self.t[:, a:a + n]


class Banks:
    def __init__(self, aps):
        self.ap = aps
        self.free = [[] for _ in aps]

    def waits(self, i):
        return list(self.free[i])

    def release(self, i, toks):
        self.free[i] = [t for t in toks if t is not None]


def na_valid(c, i, j):
    r = 32 * c + i
    kr = 32 * c - 4 + j
    r0 = min(max(r - 4, 0), ROWS - 8)
    return (0 <= kr < ROWS) and (r0 <= kr < r0 + 8)


def na_pieces():
    out = []
    for m in range(20):
        rows = [i for i in range(32) if any(na_valid(c, i, j) for c in range(NCORES) for j in (2 * m, 2 * m + 1))]
        lo, hi = min(rows) & ~1, max(rows) | 1
        assert 0 <= lo - 2 * m + 11 and hi - 2 * m + 11 <= 15, (m, lo, hi)
        pcs = []
        r = lo
        while r <= hi:
            r2 = min(r + 8, hi + 1)
            pcs.append((r * 64, r2 * 64))
            r = r2
        out.append(pcs)
    return out


def build_program(dbg=False, stop_after=None):
    nc = bass.Bass("TRN2", target_bir_lowering=False)

    def din(name, shape, dt=F32):
        return nc.dram_tensor(name, list(shape), dt, kind="ExternalInput").ap()

    xT = din("xT", [D, TK])
    xTw = din("xTw", [D, WALL])
    xown = din("xown", [OWN, D])
    w_in = din("w_in", [D, 5696])
    w_uq = din("w_uq", [256, 1536])
    w_ukv = din("w_ukv", [256, 2048])
    w_out = din("w_out", [2048, D])
    normw = din("normw", [128, 8])
    qlw = din("qlw", [128, 2])
    kvlw = din("kvlw", [128, 2])
    qnw = din("qnw", [1, 128])
    knw = din("knw", [1, 128])
    qpew = din("qpew", [1, 64])
    kpew = din("kpew", [1, 64])
    naqw = din("naqw", [1, 64])
    nakw = din("nakw", [1, 64])
    csk = din("csk", [TK, 128])
    csq = din("csq", [OWN, 128])
    ebias = din("ebias", [NH, 128, 1024])
    qmask = din("qmask", [WROWS, OWN])
    konehot = din("konehot", [WROWS, WTOK])
    metab = din("metab", [NMETA, NH])
    out = nc.dram_tensor("out", [OWN, D], F32, kind="ExternalOutput").ap()

    skind = "ExternalOutput" if dbg else "Internal"

    def scratch(name, shape, dt=BF16):
        return nc.dram_tensor(name, list(shape), dt, kind=skind).ap()

    KT_s = scratch("KT_s", [H, 128, TK])
    KPE_s = scratch("KPE_s", [64, TK])
    V_s = scratch("V_s", [H, 128, NT, 129])
    QT_s = scratch("QT_s", [H, 128, OWN])
    QPT_s = scratch("QPT_s", [H, 64, OWN])
    GM_s = scratch("GM_s", [OWN, 1024])
    GN_s = scratch("GN_s", [OWN, 1024])
    NQT_s = scratch("NQT_s", [NH, 64, OWN])
    NKT_s = scratch("NKT_s", [NH, 64, WALL])
    NV_s = scratch("NV_s", [21 * 128, NH, 65])
    YT_s = scratch("YT_s", [2048, OWN])

    P = Prog()
    with ExitStack() as es:
        AB_COLS = 73728
        AF_COLS = 14336
        abt = es.enter_context(nc.sbuf_tensor("arena_bf", [128, AB_COLS], BF16))
        aft = es.enter_context(nc.sbuf_tensor("arena_f", [128, AF_COLS], F32))
        ident = es.enter_context(nc.sbuf_tensor("ident", [128, 128], BF16))
        identf = es.enter_context(nc.sbuf_tensor("identf", [128, 128], F32))
        ones = es.enter_context(nc.sbuf_tensor("ones", [128, 8], BF16))
        consts = es.enter_context(nc.sbuf_tensor("consts", [128, 1024], F32))
        pb = [es.enter_context(nc.psum_tensor("pb%d" % i, [128, 512], F32)) for i in range(8)]
        ab = Arena(abt, AB_COLS)
        af = Arena(aft, AF_COLS)
        B = Banks([p[:, :] for p in pb])

        qkw_t = consts[:, 0:128]
        qpew_t = consts[:, 384:448]
        kpew_t = consts[:, 448:512]
        naqw_t = consts[:, 512:576]
        nakw_t = consts[:, 576:640]
        normw_t = consts[:, 640:648]
        qlw_t = consts[:, 648:650]
        kvlw_t = consts[:, 650:652]
        metab_t = consts[:, 656:672]

        P.op("pool", lambda e: e.memset(identf[:], 0.0))
        i1 = P.op("pool", lambda e: e.affine_select(out=identf[:], in_=identf[:], pattern=[[-1, 128]],
                                                    compare_op=ALU.not_equal, fill=1.0, base=0, channel_multiplier=1))
        P.op("dve", lambda e: e.tensor_copy(out=ident[:], in_=identf[:]), waits=[i1])
        P.op("dve", lambda e: e.memset(ones[:], 1.0))
        lt = None
        for (dst, src) in [(consts[:, 128:256], qnw), (consts[:, 256:384], knw), (qpew_t, qpew), (kpew_t, kpew),
                           (naqw_t, naqw), (nakw_t, nakw)]:
            lt = P.dma("sp", "ld0", lambda e, dst=dst, src=src: e.dma_start(out=dst, in_=src.partition_broadcast(128)))
        for (dst, src) in [(normw_t, normw), (qlw_t, qlw), (kvlw_t, kvlw)]:
            lt = P.dma("sp", "ld0", lambda e, dst=dst, src=src: e.dma_start(out=dst, in_=src))
        lt = P.dma("sp", "ld0", lambda e: e.dma_start(out=metab_t[0:16, :], in_=metab))
        P.op("dve", lambda e: e.tensor_tensor(out=qkw_t, in0=consts[:, 128:256], in1=consts[:, 256:384], op=ALU.mult), waits=[lt])
        P.barrier()

        wst_state = {"k": 0, "free": [[], []]}

        def load_w(dst3, src2d, nch, c0, c1, rowscale, wst):
            last = None
            for ch in range(nch):
                for a in range(c0, c1, 2048):
                    b_ = min(a + 2048, c1)
                    k = wst_state["k"] % 2
                    wst_state["k"] += 1
                    stg = wst[k][:, 0:b_ - a]
                    t = P.dma("sp", "lw%d" % k, lambda e, stg=stg, ch=ch, a=a, b_=b_: e.dma_start(
                        out=stg, in_=src2d[ch * 128:(ch + 1) * 128, a:b_]), waits=wst_state["free"][k])
                    dsl = dst3[:, ch, a - c0:b_ - c0]
                    if rowscale is not None:
                        if k == 0:
                            t2 = P.op("dve", lambda e, dsl=dsl, stg=stg, ch=ch: e.tensor_scalar(
                                out=dsl, in0=stg, scalar1=rowscale[:, ch:ch + 1], scalar2=None, op0=ALU.mult), waits=[t])
                        else:
                            t2 = P.op("act", lambda e, dsl=dsl, stg=stg, ch=ch: e.activation(
                                out=dsl, in_=stg, func=AF.Copy, scale=rowscale[:, ch:ch + 1]), waits=[t])
                    else:
                        if k == 0:
                            t2 = P.op("dve", lambda e, dsl=dsl, stg=stg: e.tensor_copy(out=dsl, in_=stg), waits=[t])
                        else:
                            t2 = P.op("act", lambda e, dsl=dsl, stg=stg: e.activation(out=dsl, in_=stg, func=AF.Copy), waits=[t])
                    wst_state["free"][k] = [t2]
                    last = t2
            return last

        def mm(out_ap, lhsT, rhs, start, stop, waits=(), inc=False, skip=False):
            if skip:
                return P.op("pe", lambda e: e.matmul(out_ap, lhsT=lhsT, rhs=rhs, start=start, stop=stop, skip_group_check=True),
                            waits=waits, inc=inc)
            return P.op("pe", lambda e: e.matmul(out_ap, lhsT=lhsT, rhs=rhs, start=start, stop=stop), waits=waits, inc=inc)

        scr_free = {}

        def headnorm(src3, n, nh, hd, outp, sq, tmp, stt, pre, wtile, rope_cs, waits, tmp2=None, key=None):
            waits = list(waits) + scr_free.get(key, [])
            tok = _headnorm(src3, n, nh, hd, outp, sq, tmp, stt, pre, wtile, rope_cs, waits, tmp2)
            scr_free[key] = [tok]
            return tok

        def _headnorm(src3, n, nh, hd, outp, sq, tmp, stt, pre, wtile, rope_cs, waits, tmp2=None):
            ss = stt[:n, 0:nh]
            sd = stt[:n, nh:2 * nh]
            rr = stt[:n, 2 * nh:3 * nh]
            a = P.op("act", lambda e: e.activation(out=sq, in_=src3, func=AF.Square), waits=waits)
            d = P.op("dve", lambda e: e.tensor_reduce(out=ss, in_=sq, axis=AX.X, op=ALU.add), waits=[a])
            if pre is not None:
                d = P.op("dve", lambda e: e.tensor_scalar(out=ss, in0=ss, scalar1=pre[1], scalar2=None, op0=ALU.mult), waits=[d])
            a = P.op("act", lambda e: e.activation(out=sd, in_=ss, func=AF.Sqrt, bias=EPS, scale=1.0 / hd), waits=[d])
            d = P.op("dve", lambda e: e.reciprocal(out=rr, in_=sd), waits=[a])
            if pre is not None:
                d = P.op("dve", lambda e: e.tensor_scalar(out=rr, in0=rr, scalar1=pre[0], scalar2=None, op0=ALU.mult), waits=[d])
            rb = rr.unsqueeze(2).to_broadcast([n, nh, hd])
            if wtile is None and rope_cs is None:
                return P.op("dve", lambda e: e.tensor_tensor(out=outp, in0=src3, in1=rb, op=ALU.mult), waits=[d])
            d = P.op("dve", lambda e: e.tensor_tensor(out=tmp, in0=src3, in1=rb, op=ALU.mult), waits=[d])
            wb = wtile[:n, :].unsqueeze(1).to_broadcast([n, nh, hd])
            if rope_cs is None:
                return P.op("dve", lambda e: e.tensor_tensor(out=outp, in0=tmp, in1=wb, op=ALU.mult), waits=[d])
            d = P.op("dve", lambda e: e.tensor_tensor(out=tmp, in0=tmp, in1=wb, op=ALU.mult), waits=[d])
            hh = hd // 2
            cc = rope_cs[:n, 0:hd].unsqueeze(1).to_broadcast([n, nh, hd])
            nsin = rope_cs[:n, hd:hd + hh].unsqueeze(1).to_broadcast([n, nh, hh])
            psin = rope_cs[:n, hd + hh:2 * hd].unsqueeze(1).to_broadcast([n, nh, hh])
            d1 = P.op("dve", lambda e: e.tensor_tensor(out=tmp2[:, :, 0:hh], in0=tmp[:, :, hh:hd], in1=nsin, op=ALU.mult), waits=[d])
            d2 = P.op("dve", lambda e: e.tensor_tensor(out=tmp2[:, :, hh:hd], in0=tmp[:, :, 0:hh], in1=psin, op=ALU.mult), waits=[d])
            d3 = P.op("dve", lambda e: e.tensor_tensor(out=tmp, in0=tmp, in1=cc, op=ALU.mult), waits=[d2])
            return P.op("dve", lambda e: e.tensor_tensor(out=outp, in0=tmp, in1=tmp2, op=ALU.add), waits=[d3])

        def rstd_from_ss(ss_ps, n, stt, waits):
            a = P.op("act", lambda e: e.activation(out=stt[:n, 0:1], in_=ss_ps, func=AF.Sqrt, bias=EPS, scale=1.0 / D), waits=waits)
            d = P.op("dve", lambda e: e.reciprocal(out=stt[:n, 1:2], in_=stt[:n, 0:1]), waits=[a])
            d = P.op("dve", lambda e: e.tensor_tensor(out=stt[:n, 2:3], in0=stt[:n, 1:2], in1=stt[:n, 1:2], op=ALU.mult), waits=[d])
            return (stt[:n, 1:2], stt[:n, 2:3]), d

        ab.reset(); af.reset()
        wst = [af.take(2048), af.take(2048)]
        xt = [af.take(4096).rearrange("p (c t) -> p c t", c=8) for _ in range(2)]
        cs = [af.take(512).rearrange("p (j f) -> p j f", j=4) for _ in range(2)]
        stA = [af.take(16) for _ in range(2)]
        stK = [af.take(64) for _ in range(2)]
        sqA = af.take(512)
        tmpA = af.take(64)
        tmpB = af.take(64)
        wkv = ab.take(8 * 320).rearrange("p (c f) -> p c f", c=8)
        wukv = ab.take(2 * 2048).rearrange("p (c f) -> p c f", c=2)
        xb = [ab.take(4096).rearrange("p (c t) -> p c t", c=8) for _ in range(2)]
        xsq = [ab.take(4096).rearrange("p (c t) -> p c t", c=8) for _ in range(2)]
        cn = [ab.take(256) for _ in range(2)]
        kpe_b = [ab.take(64) for _ in range(2)]
        cTt = [ab.take(256).rearrange("p (c t) -> p c t", c=2) for _ in range(2)]
        khat = [ab.take(1024).rearrange("p (h d) -> p h d", h=8) for _ in range(2)]
        KTst = [ab.take(4096).rearrange("p (h t) -> p h t", h=8) for _ in range(2)]
        Vst = [ab.take(8 * 4 * 129).rearrange("p (h j e) -> p h j e", h=8, j=4) for _ in range(2)]
        kpest = [ab.take(512) for _ in range(2)]

        load_w(wkv, w_in, 8, 256, 576, normw_t, wst)
        tw = load_w(wukv, w_ukv, 2, 0, 2048, kvlw_t, wst)
        for b_ in range(2):
            P.op("pool", lambda e, b_=b_: e.memset(Vst[b_][:, :, :, :], 1.0))

        xTv = xT.rearrange("(c p) t -> p c t", p=128)
        KTv = KT_s.rearrange("h d t -> d h t")
        Vv = V_s.rearrange("h p j e -> p h j e")
        NGA = 33
        gfree_x = [[], []]
        gfree_xb = [[], []]
        gfree_st = [[], []]
        tcount = 0
        for g in range(NGA):
            b_ = g % 2
            t0 = g * 512
            ng = 512 if g < 32 else 16
            ntile = 4 if g < 32 else 1
            l1 = P.dma("sp", "lA%d" % b_, lambda e, b_=b_, t0=t0, ng=ng: e.dma_start(out=xt[b_][:, :, 0:ng], in_=xTv[:, :, t0:t0 + ng]),
                       waits=gfree_x[b_])
            if g < 32:
                l2 = P.dma("sp", "lA%d" % b_, lambda e, b_=b_, t0=t0: e.dma_start(
                    out=cs[b_][:, :, :], in_=csk[t0:t0 + 512, :].rearrange("(j p) f -> p j f", p=128)))
            else:
                l2 = P.dma("sp", "lA%d" % b_, lambda e, b_=b_, t0=t0: e.dma_start(out=cs[b_][0:16, 0, :], in_=csk[t0:t0 + 16, :]))
            c1 = P.op("act", lambda e, b_=b_, ng=ng: e.activation(out=xb[b_][:, :, 0:ng], in_=xt[b_][:, :, 0:ng], func=AF.Copy),
                      waits=[l2] + gfree_xb[b_])
            c2 = P.op("pool", lambda e, b_=b_, ng=ng: e.tensor_tensor(out=xsq[b_][:, :, 0:ng], in0=xt[b_][:, :, 0:ng],
                                                                      in1=xt[b_][:, :, 0:ng], op=ALU.mult), waits=[l2] + gfree_xb[b_])
            last_pe = None
            last_rope = None
            stage_toks = []
            for j in range(ntile):
                n = 128 if g < 32 else 16
                tp = tcount % 2
                tcount += 1
                sl = slice(j * 128, j * 128 + n)
                pk = B.ap[tp]
                for c in range(8):
                    mm(pk[:n, 0:320], xb[b_][:, c, sl], wkv[:, c, :], c == 0, c == 7, waits=([c1] + B.waits(tp)) if c == 0 else ())
                for c in range(8):
                    tk = mm(pk[:n, 384:385], xsq[b_][:, c, sl], ones[:, 0:1], c == 0, c == 7, waits=[c2] if c == 0 else (), inc=(c == 7))
                last_pe = tk
                st = stA[tp]
                (r1, r1sq), d = rstd_from_ss(pk[:n, 384:385], n, st, [tk])
                dc = headnorm(pk[:n, 0:256].rearrange("p (h d) -> p h d", h=1), n, 1, 256, cn[tp][:n, :].rearrange("p (h d) -> p h d", h=1),
                              sqA[:n, 0:256].rearrange("p (h d) -> p h d", h=1), None, st[:, 4:8], (r1, r1sq), None, None, [tk, d], key="s0")
                dk = headnorm(pk[:n, 256:320].rearrange("p (h d) -> p h d", h=1), n, 1, 64, kpe_b[tp][:n, :].rearrange("p (h d) -> p h d", h=1),
                              sqA[:n, 256:320].rearrange("p (h d) -> p h d", h=1), tmpA[:n, :].rearrange("p (h d) -> p h d", h=1),
                              st[:, 8:12], (r1, r1sq), kpew_t, cs[b_][:, j, :], [tk, d, l2, dc],
                              tmp2=tmpB[:n, :].rearrange("p (h d) -> p h d", h=1), key="s1")
                B.release(tp, [dc, dk])
                last_rope = dk
                pt = B.ap[2]
                mm(pt[:, 0:n], cn[tp][:n, 0:128], ident[:n, :n], True, True, waits=[dc, dk] + B.waits(2))
                mm(pt[:, 128:128 + n], cn[tp][:n, 128:256], ident[:n, :n], True, True)
                tt = mm(pt[0:64, 256:256 + n], kpe_b[tp][:n, 0:64], ident[:n, :n], True, True, inc=True)
                e1 = P.op("dve", lambda e, tp=tp, n=n, pt=pt: e.tensor_copy(
                    out=cTt[tp][:, :, 0:n], in_=pt[:, 0:256].rearrange("p (c t) -> p c t", c=2)[:, :, 0:n]), waits=[tt])
                e2 = P.op("act", lambda e, b_=b_, sl=sl, n=n, pt=pt: e.activation(out=kpest[b_][0:64, sl], in_=pt[0:64, 256:256 + n], func=AF.Copy),
                          waits=[tt, e1] + (gfree_st[b_] if j == 0 else []))
                B.release(2, [e1, e2])
                stage_toks.append(e2)
                stk = stK[tp]
                kh = khat[tp]
                tr_toks = []
                for gp in range(4):
                    q = 3 + (gp % 2)
                    pu = B.ap[q]
                    mm(pu[:n, :], cTt[tp][:, 0, 0:n], wukv[:, 0, gp * 512:(gp + 1) * 512], True, False, waits=[e1] + B.waits(q))
                    tu = mm(pu[:n, :], cTt[tp][:, 1, 0:n], wukv[:, 1, gp * 512:(gp + 1) * 512], False, True, inc=True)
                    pu4 = pu[:n, :].rearrange("p (h two d) -> p h two d", h=2, two=2)
                    dkk = headnorm(pu4[:, :, 0, :], n, 2, 128, kh[:n, 2 * gp:2 * gp + 2, :], sqA[:n, 0:256].rearrange("p (h d) -> p h d", h=2),
                                   None, stk[:, 8 * gp:8 * gp + 8], None, None, None, [tu], key="s0")
                    av = P.op("act", lambda e, b_=b_, n=n, gp=gp, j=j, pu4=pu4: e.activation(
                        out=Vst[b_][:n, 2 * gp:2 * gp + 2, j, 0:128], in_=pu4[:, :, 1, :], func=AF.Copy),
                        waits=[tu, dkk] + (gfree_st[b_] if (j == 0 and gp == 0) else []))
                    B.release(q, [dkk, av])
                    stage_toks.append(av)
                    q2 = 5 + (gp % 2)
                    p2 = B.ap[q2]
                    mm(p2[:, 0:n], kh[:n, 2 * gp, :], ident[:n, :n], True, True, waits=[dkk] + B.waits(q2))
                    t2 = mm(p2[:, 128:128 + n], kh[:n, 2 * gp + 1, :], ident[:n, :n], True, True, inc=True)
                    tr_toks.append(t2)
                    e3 = P.op("dve", lambda e, b_=b_, gp=gp, sl=sl, n=n, p2=p2: e.tensor_copy(
                        out=KTst[b_][:, 2 * gp:2 * gp + 2, sl], in_=p2[:, 0:256].rearrange("p (h t) -> p h t", h=2)[:, :, 0:n]),
                        waits=[t2] + (gfree_st[b_] if (j == 0 and gp == 0) else []))
                    B.release(q2, [e3])
                    stage_toks.append(e3)
                last_pe = tr_toks[-1]
            gfree_x[b_] = [c1, c2, last_rope]
            gfree_xb[b_] = [last_pe]
            nrow = 128 if g < 32 else 16
            s1 = P.dma("pool", "sA%d" % b_, lambda e, b_=b_, t0=t0, ng=ng: e.dma_start(out=KTv[:, :, t0:t0 + ng], in_=KTst[b_][:, :, 0:ng]),
                       waits=stage_toks)
            s2 = P.dma("pool", "sA%d" % b_, lambda e, b_=b_, g=g, ntile=ntile, nrow=nrow: e.dma_start(
                out=Vv[0:nrow, :, 4 * g:4 * g + ntile, :], in_=Vst[b_][0:nrow, :, 0:ntile, :]))
            s3 = P.dma("pool", "sA%d" % b_, lambda e, b_=b_, t0=t0, ng=ng: e.dma_start(out=KPE_s[:, t0:t0 + ng], in_=kpest[b_][0:64, 0:ng]))
            gfree_st[b_] = [s3]
        P.barrier()

        if stop_after != "A":
            ab.reset(); af.reset()
            wst = [af.take(2048), af.take(2048)]
            wst_state["free"] = [[], []]
            xtB = [af.take(1024).rearrange("p (c t) -> p c t", c=8) for _ in range(2)]
            csB = [af.take(128) for _ in range(2)]
            stB = [af.take(16) for _ in range(2)]
            stH = [af.take(64) for _ in range(8)]
            sqB = [af.take(512) for _ in range(2)]
            tmB = [af.take(512) for _ in range(2)]
            tm2 = [af.take(512) for _ in range(2)]
            wq = ab.take(8 * 256).rearrange("p (c f) -> p c f", c=8)
            wgm = ab.take(8 * 1024).rearrange("p (c f) -> p c f", c=8)
            wnq = ab.take(8 * 1024).rearrange("p (c f) -> p c f", c=8)
            wnk = ab.take(8 * 1024).rearrange("p (c f) -> p c f", c=8)
            wnv = ab.take(8 * 1024).rearrange("p (c f) -> p c f", c=8)
            wgn = ab.take(8 * 1024).rearrange("p (c f) -> p c f", c=8)
            wuq = ab.take(2 * 1536).rearrange("p (c f) -> p c f", c=2)
            xbB = [ab.take(1024).rearrange("p (c t) -> p c t", c=8) for _ in range(2)]
            xsqB = [ab.take(1024).rearrange("p (c t) -> p c t", c=8) for _ in range(2)]
            nkh = [ab.take(1024) for _ in range(2)]
            nqh = nkh
            NKst = [ab.take(8 * 128).rearrange("p (h t) -> p h t", h=8) for _ in range(2)]
            NQst = [ab.take(8 * 128).rearrange("p (h t) -> p h t", h=8) for _ in range(2)]
            NVst = [ab.take(16 * 65).rearrange("p (h e) -> p h e", h=16) for _ in range(2)]
            qlb = [ab.take(256) for _ in range(2)]
            qlT = [ab.take(256).rearrange("p (c t) -> p c t", c=2) for _ in range(2)]
            qnb = [ab.take(1024).rearrange("p (h d) -> p h d", h=8) for _ in range(2)]
            qpb = [ab.take(512).rearrange("p (h d) -> p h d", h=8) for _ in range(2)]
            QnSt = [ab.take(8 * 128).rearrange("p (h t) -> p h t", h=8) for _ in range(2)]
            QpSt = [ab.take(4 * 128).rearrange("p (h t) -> p h t", h=4) for _ in range(2)]
            GMst = [ab.take(1024) for _ in range(2)]
            GNst = [ab.take(1024) for _ in range(2)]
            load_w(wq, w_in, 8, 0, 256, normw_t, wst)
            load_w(wgm, w_in, 8, 576, 1600, normw_t, wst)
            load_w(wnq, w_in, 8, 1600, 2624, normw_t, wst)
            load_w(wnk, w_in, 8, 2624, 3648, normw_t, wst)
            load_w(wnv, w_in, 8, 3648, 4672, normw_t, wst)
            load_w(wgn, w_in, 8, 4672, 5696, normw_t, wst)
            load_w(wuq, w_uq, 2, 0, 1536, qlw_t, wst)
            for b_ in range(2):
                P.op("pool", lambda e, b_=b_: e.memset(NVst[b_][:, :, :], 1.0))
            xTwv = xTw.rearrange("(c p) t -> p c t", p=128)
            NKTv = NKT_s.rearrange("(hp two) d t -> (two d) hp t", two=2)
            NQTv = NQT_s.rearrange("(hp two) d t -> (two d) hp t", two=2)
            QTnv = QT_s.rearrange("h d t -> d h t")
            QTpv = QPT_s.rearrange("(hp two) r t -> (two r) hp t", two=2)
            free_x = [[], []]
            free_xb = [[], []]
            free_st = [[], []]
            free_cs = [[], []]
            bk = {"i": 0}

            def nextbank():
                i = bk["i"] % 8
                bk["i"] += 1
                return i

            def proj(n, tp, wmat, col0, ncols, c1tok):
                q = nextbank()
                t = None
                for c in range(8):
                    t = mm(B.ap[q][:n, 0:ncols], xbB[tp][:, c, 0:n], wmat[:, c, col0:col0 + ncols], c == 0, c == 7,
                           waits=([c1tok] + B.waits(q)) if c == 0 else (), inc=(c == 7))
                return q, t

            def transposes_out(src_bf, n, nblk, stage, st_waits, src_waits):
                toks = []
                for k0 in range(0, nblk, 4):
                    q = nextbank()
                    kk = min(4, nblk - k0)
                    t = None
                    for k in range(kk):
                        t = mm(B.ap[q][:, k * 128:k * 128 + n], src_bf[:n, (k0 + k) * 128:(k0 + k + 1) * 128], ident[:n, :n], True, True,
                               waits=(B.waits(q) + list(src_waits)) if k == 0 else (), inc=(k == kk - 1))
                    ev = P.op("act", lambda e, q=q, k0=k0, kk=kk, n=n: e.activation(
                        out=stage[:, k0:k0 + kk, 0:n], in_=B.ap[q][:, 0:kk * 128].rearrange("p (h t) -> p h t", h=kk)[:, :, 0:n], func=AF.Copy),
                        waits=[t] + (st_waits if k0 == 0 else []))
                    B.release(q, [ev])
                    toks.append(ev)
                return toks

            nwt = int(stop_after[2:]) if (stop_after or "").startswith("Bn") else 21
            for wt in range(nwt):
                tp = wt % 2
                n = 128 if wt < 20 else 16
                own = 2 <= wt < 18
                ti = wt - 2
                tok0 = wt * 128
                l1 = P.dma("sp", "lB%d" % tp, lambda e, tp=tp, n=n, tok0=tok0: e.dma_start(out=xtB[tp][:, :, 0:n], in_=xTwv[:, :, tok0:tok0 + n]),
                           waits=free_x[tp])
                l2 = None
                lb = l1
                if own:
                    l2 = P.dma("sp", "lB%d" % tp, lambda e, tp=tp, ti=ti: e.dma_start(out=csB[tp][:, :], in_=csq[ti * 128:(ti + 1) * 128, :]),
                               waits=free_cs[tp])
                    lb = l2
                c1 = P.op("act", lambda e, tp=tp, n=n: e.activation(out=xbB[tp][:, :, 0:n], in_=xtB[tp][:, :, 0:n], func=AF.Copy),
                          waits=[lb] + free_xb[tp])
                c2 = P.op("pool", lambda e, tp=tp, n=n: e.tensor_tensor(out=xsqB[tp][:, :, 0:n], in0=xtB[tp][:, :, 0:n], in1=xtB[tp][:, :, 0:n],
                                                                        op=ALU.mult), waits=[lb] + free_xb[tp])
                qs = nextbank()
                tk = None
                for c in range(8):
                    tk = mm(B.ap[qs][:n, 0:1], xsqB[tp][:, c, 0:n], ones[:, 0:1], c == 0, c == 7, waits=([c2] + B.waits(qs)) if c == 0 else (),
                            inc=(c == 7))
                st = stB[tp]
                (r1, r1sq), d = rstd_from_ss(B.ap[qs][:n, 0:1], n, st, [tk])
                B.release(qs, [d])
                stage_toks = []
                for half in range(2):
                    q, t = proj(n, tp, wnk, half * 512, 512, c1)
                    dk = headnorm(B.ap[q][:n, :].rearrange("p (h d) -> p h d", h=8), n, 8, 64,
                                  nkh[tp][:n, half * 512:(half + 1) * 512].rearrange("p (h d) -> p h d", h=8),
                                  sqB[half][:n, :].rearrange("p (h d) -> p h d", h=8), tmB[half][:n, :].rearrange("p (h d) -> p h d", h=8),
                                  stH[half], (r1, r1sq), nakw_t, None, [t, d], key="h%d" % half)
                    B.release(q, [dk])
                evs = transposes_out(nkh[tp], n, 8, NKst[tp], free_st[tp], [dk])
                stage_toks += evs
                for half in range(2):
                    q, t = proj(n, tp, wnv, half * 512, 512, c1)
                    av = P.op("act", lambda e, q=q, n=n, tp=tp, half=half, r1=r1: e.activation(
                        out=NVst[tp][:n, half * 8:(half + 1) * 8, 0:64], in_=B.ap[q][:n, :].rearrange("p (h d) -> p h d", h=8),
                        func=AF.Copy, scale=r1), waits=[t, d] + free_st[tp])
                    B.release(q, [av])
                    stage_toks.append(av)
                last_pe_x = None
                if own:
                    q, t = proj(n, tp, wq, 0, 256, c1)
                    dq = headnorm(B.ap[q][:n, 0:256].rearrange("p (h d) -> p h d", h=1), n, 1, 256, qlb[tp][:n, :].rearrange("p (h d) -> p h d", h=1),
                                  sqB[0][:n, 0:256].rearrange("p (h d) -> p h d", h=1), None, stH[2], (r1, r1sq), None, None, [t, d], key="h0")
                    B.release(q, [dq])
                    q = nextbank()
                    mm(B.ap[q][:, 0:128], qlb[tp][:, 0:128], ident[:, :], True, True, waits=[dq] + B.waits(q))
                    t = mm(B.ap[q][:, 128:256], qlb[tp][:, 128:256], ident[:, :], True, True, inc=True)
                    eq = P.op("dve", lambda e, q=q, tp=tp: e.tensor_copy(out=qlT[tp][:, :, :], in_=B.ap[q][:, 0:256].rearrange("p (c t) -> p c t", c=2)),
                              waits=[t])
                    B.release(q, [eq])
                    for gq in range(4):
                        q = nextbank()
                        mm(B.ap[q][:, 0:384], qlT[tp][:, 0, :], wuq[:, 0, gq * 384:(gq + 1) * 384], True, False, waits=[eq] + B.waits(q))
                        t = mm(B.ap[q][:, 0:384], qlT[tp][:, 1, :], wuq[:, 1, gq * 384:(gq + 1) * 384], False, True, inc=True)
                        v3 = B.ap[q][:, 0:384].rearrange("p (h d) -> p h d", h=2)
                        d1 = headnorm(v3[:, :, 0:128], 128, 2, 128, qnb[tp][:, 2 * gq:2 * gq + 2, :], sqB[0][:, 0:256].rearrange("p (h d) -> p h d", h=2),
                                      tmB[0][:, 0:256].rearrange("p (h d) -> p h d", h=2), stH[3], None, qkw_t, None, [t], key="h0")
                        d2 = headnorm(v3[:, :, 128:192], 128, 2, 64, qpb[tp][:, 2 * gq:2 * gq + 2, :], sqB[1][:, 0:128].rearrange("p (h d) -> p h d", h=2),
                                      tmB[1][:, 0:128].rearrange("p (h d) -> p h d", h=2), stH[4], None, qpew_t, csB[tp], [t, l2, d1],
                                      tmp2=tm2[1][:, 0:128].rearrange("p (h d) -> p h d", h=2), key="h1")
                        B.release(q, [d1, d2])
                    free_cs[tp] = [d2]
                    evs = transposes_out(qnb[tp][:, :, :].rearrange("p h d -> p (h d)"), 128, 8, QnSt[tp], free_st[tp], [d1, d2])
                    stage_toks += evs
                    evs = transposes_out(qpb[tp][:, :, :].rearrange("p h d -> p (h d)"), 128, 4, QpSt[tp], free_st[tp], [d1, d2])
                    stage_toks += evs
                    for (wmat, gst) in ((wgm, GMst), (wgn, GNst)):
                        for half in range(2):
                            q, t = proj(n, tp, wmat, half * 512, 512, c1)
                            ag = P.op("act", lambda e, q=q, tp=tp, half=half, gst=gst, r1=r1: e.activation(
                                out=gst[tp][:, half * 512:(half + 1) * 512], in_=B.ap[q][:, :], func=AF.Silu, scale=r1),
                                waits=[t, d] + free_st[tp])
                            B.release(q, [ag])
                            stage_toks.append(ag)
                    for half in range(2):
                        q, t = proj(n, tp, wnq, half * 512, 512, c1)
                        dk = headnorm(B.ap[q][:, :].rearrange("p (h d) -> p h d", h=8), 128, 8, 64,
                                      nqh[tp][:, half * 512:(half + 1) * 512].rearrange("p (h d) -> p h d", h=8),
                                      sqB[half][:, :].rearrange("p (h d) -> p h d", h=8), tmB[half][:, :].rearrange("p (h d) -> p h d", h=8),
                                      stH[5 + half], (r1, r1sq), naqw_t, None, [t, d], key="h%d" % half)
                        B.release(q, [dk])
                        last_pe_x = t
                    evs = transposes_out(nqh[tp], 128, 8, NQst[tp], free_st[tp], [dk])
                    stage_toks += evs
                else:
                    last_pe_x = t
                free_x[tp] = [c1, c2]
                free_xb[tp] = [last_pe_x] if last_pe_x is not None else []
                s = P.dma("pool", "sB%d" % tp, lambda e, tp=tp, n=n, tok0=tok0: e.dma_start(out=NKTv[:, :, tok0:tok0 + n], in_=NKst[tp][:, :, 0:n]),
                          waits=stage_toks)
                s = P.dma("pool", "sB%d" % tp, lambda e, tp=tp, n=n, tok0=tok0: e.dma_start(out=NV_s[tok0:tok0 + n, :, :], in_=NVst[tp][0:n, :, :]))
                if own:
                    q0 = ti * 128
                    s = P.dma("pool", "sB%d" % tp, lambda e, tp=tp, q0=q0: e.dma_start(out=QTnv[:, :, q0:q0 + 128], in_=QnSt[tp][:, :, :]))
                    s = P.dma("pool", "sB%d" % tp, lambda e, tp=tp, q0=q0: e.dma_start(out=QTpv[:, :, q0:q0 + 128], in_=QpSt[tp][:, :, :]))
                    s = P.dma("pool", "sB%d" % tp, lambda e, tp=tp, q0=q0: e.dma_start(out=NQTv[:, :, q0:q0 + 128], in_=NQst[tp][:, :, :]))
                    s = P.dma("pool", "sB%d" % tp, lambda e, tp=tp, q0=q0: e.dma_start(out=GM_s[q0:q0 + 128, :], in_=GMst[tp][:, :]))
                    s = P.dma("pool", "sB%d" % tp, lambda e, tp=tp, q0=q0: e.dma_start(out=GN_s[q0:q0 + 128, :], in_=GNst[tp][:, :]))
                free_st[tp] = [s]
            P.barrier()

        if stop_after not in ("A", "B") and not (stop_after or "").startswith("Bn"):
            ab.reset(); af.reset()
            NKA = [ab.take(WALL) for _ in range(2)]
            NQA = [ab.take(OWN) for _ in range(2)]
            NVall = ab.take(21 * 16 * 65).rearrange("p (j h e) -> p j h e", j=21, h=16)
            GNall = ab.take(16 * 1024).rearrange("p (j f) -> p j f", j=16)
            PTc = [ab.take(512) for _ in range(3)]
            ypair = ab.take(16 * 128).rearrange("p (u f) -> p u f", u=16)
            YTst = ab.take(OWN)
            EB = [af.take(1024) for _ in range(2)]
            Ef = [af.take(512) for _ in range(2)]
            rdn = af.take(16)
            mstage = af.take(WTOK)
            pieces = na_pieces()
            lA = P.dma("sp", "m0", lambda e: e.dma_start(out=mstage[64:104, 0:WTOK], in_=konehot))
            for b_ in range(2):
                P.op("dve", lambda e, b_=b_: e.memset(NKA[b_][64:104, WTOK:WALL], 0.0))
                lk = P.op("dve", lambda e, b_=b_: e.tensor_copy(out=NKA[b_][64:104, 0:WTOK], in_=mstage[64:104, 0:WTOK]), waits=[lA])
            lB = P.dma("sp", "m1", lambda e: e.dma_start(out=mstage[64:104, 0:OWN], in_=qmask), waits=[lk])
            for b_ in range(2):
                lq = P.op("dve", lambda e, b_=b_: e.tensor_copy(out=NQA[b_][64:104, :], in_=mstage[64:104, 0:OWN]), waits=[lB])
            lv = P.dma("sp", "m2", lambda e: e.dma_start(out=NVall[:, :, :, :], in_=NV_s.rearrange("(j p) h e -> p j h e", p=128)))
            lg = P.dma("sp", "m3", lambda e: e.dma_start(out=GNall[:, :, :], in_=GN_s.rearrange("(j p) f -> p j f", p=128)))
            accb = [0, 1, 2]

            def acc_ap(u):
                return B.ap[accb[u // 7]][:, (u % 7) * 65:(u % 7) * 65 + 65]

            free_nk = [[], []]
            free_eb = [[], []]
            free_pt = [[], [], []]
            free_ef = [[], []]
            free_yp = []
            free_yst = []
            ucnt = 0
            for h in range(NH):
                b_ = h % 2
                lk = P.dma("sp", "lC%d" % b_, lambda e, b_=b_, h=h: e.dma_start(out=NKA[b_][0:64, :], in_=NKT_s[h]), waits=free_nk[b_])
                lq2 = P.dma("sp", "lC%d" % b_, lambda e, b_=b_, h=h: e.dma_start(out=NQA[b_][0:64, :], in_=NQT_s[h]))
                le = P.dma("sp", "lC%d" % b_, lambda e, b_=b_, h=h: e.dma_start(out=EB[b_][:, :], in_=ebias[h]), waits=free_eb[b_])
                ae = P.op("act", lambda e, b_=b_: e.activation(out=EB[b_][:, :], in_=EB[b_][:, :], func=AF.Exp), waits=[le])
                z = None
                for k in range(3):
                    z = P.op("dve", lambda e, k=k: e.memset(B.ap[accb[k]][:, 0:455], 0.0), waits=B.waits(accb[k]))
                pv_last = None
                first = True
                for m in range(21):
                    nk = 128 if m < 20 else 16
                    pcs = pieces[m] if m < 20 else [(0, 512), (512, 1024), (1024, 1536), (1536, 2048)]
                    for (q0, q1) in pcs:
                        nq = q1 - q0
                        sb_ = 3 + (ucnt % 2)
                        pS = B.ap[sb_]
                        pi = ucnt % 3
                        fi = ucnt % 2
                        ucnt += 1
                        w0 = B.waits(sb_) + ([le, lq, lv, z] if first else [])
                        first = False
                        if m < 20:
                            ts = mm(pS[:, 0:nq], NKA[b_][0:104, m * 128:(m + 1) * 128], NQA[b_][0:104, q0:q1], True, True, waits=w0, inc=True)
                            a1 = P.op("act", lambda e, fi=fi, nq=nq, pS=pS: e.activation(out=Ef[fi][:, 0:nq], in_=pS[:, 0:nq], func=AF.Exp, scale=NA_SCALE),
                                      waits=[ts] + free_ef[fi])
                            rel0 = (q0 // 64) - 2 * m + 11
                            d1 = P.op("dve", lambda e, fi=fi, pi=pi, nq=nq, rel0=rel0, b_=b_: e.tensor_tensor(
                                out=PTc[pi][:, 0:nq], in0=Ef[fi][:, 0:nq], in1=EB[b_][:, rel0 * 64:rel0 * 64 + nq], op=ALU.mult),
                                waits=[a1, ae] + free_pt[pi])
                            B.release(sb_, [a1])
                            free_ef[fi] = [d1]
                            ready = d1
                        else:
                            ts = mm(pS[0:16, 0:nq], NKA[b_][0:64, WTOK:WALL], NQA[b_][0:64, q0:q1], True, True, waits=w0, inc=True)
                            a1 = P.op("act", lambda e, pi=pi, nq=nq, pS=pS, h=h: e.activation(
                                out=PTc[pi][0:16, 0:nq], in_=pS[0:16, 0:nq], func=AF.Exp, scale=NA_SCALE, bias=metab_t[0:16, h:h + 1]),
                                waits=[ts] + free_pt[pi])
                            B.release(sb_, [a1])
                            ready = a1
                        for u in range(q0 // 128, q1 // 128):
                            o = u * 128 - q0
                            pv_last = mm(acc_ap(u), PTc[pi][0:nk, o:o + 128], NVall[0:nk, m, h, :], False, False,
                                         waits=[ready] if u == q0 // 128 else (), inc=(u == q1 // 128 - 1), skip=True)
                        free_pt[pi] = [pv_last]
                free_nk[b_] = [pv_last]
                free_eb[b_] = [pv_last]
                ev = None
                for u in range(16):
                    a = acc_ap(u)
                    r = P.op("dve", lambda e, a=a, u=u: e.reciprocal(out=rdn[:, u:u + 1], in_=a[:, 64:65]), waits=[pv_last] + (free_yp if (u == 0 and h % 2 == 0) else []))
                    ev = P.op("dve", lambda e, a=a, u=u, h=h: e.scalar_tensor_tensor(
                        out=ypair[:, u, (h % 2) * 64:(h % 2) * 64 + 64], in0=a[:, 0:64], scalar=rdn[:, u:u + 1],
                        in1=GNall[:, u, h * 64:(h + 1) * 64], op0=ALU.mult, op1=ALU.mult), waits=[r, lg])
                for k in range(3):
                    B.release(accb[k], [ev])
                if h % 2 == 1:
                    hp = h // 2
                    tt = None
                    evs = []
                    for u4 in range(4):
                        q = 5 + (u4 % 2)
                        for k in range(4):
                            u = u4 * 4 + k
                            tt = mm(B.ap[q][:, k * 128:(k + 1) * 128], ypair[:, u, :], ident[:, :], True, True,
                                    waits=([ev] + B.waits(q)) if k == 0 else (), inc=(k == 3))
                        e2 = P.op("act", lambda e, q=q, u4=u4: e.activation(out=YTst[:, u4 * 512:(u4 + 1) * 512], in_=B.ap[q][:, :], func=AF.Copy),
                                  waits=[tt] + (free_yst if u4 == 0 else []))
                        B.release(q, [e2])
                        evs.append(e2)
                    free_yp = [tt]
                    s = P.dma("pool", "st2", lambda e, hp=hp: e.dma_start(out=YT_s[1024 + hp * 128:1024 + (hp + 1) * 128, :], in_=YTst[:, :]), waits=evs)
                    free_yst = [s]
            P.barrier()

        if stop_after not in ("A", "B", "C") and not (stop_after or "").startswith("Bn"):
            ab.reset(); af.reset()
            KTh = ab.take(TK)
            kpeT = ab.take(TK)
            Vh = ab.take(NT * 129).rearrange("p (j e) -> p j e", j=NT)
            Qn = [ab.take(OWN) for _ in range(2)]
            Qp = [ab.take(OWN) for _ in range(2)]
            GMh = [ab.take(16 * 128).rearrange("p (j f) -> p j f", j=16) for _ in range(2)]
            PTd = [ab.take(512) for _ in range(4)]
            ybf = [ab.take(128) for _ in range(2)]
            YTd = [ab.take(1024) for _ in range(2)]
            rdd = af.take(16)
            lp = P.dma("sp", "m0", lambda e: e.dma_start(out=kpeT[0:64, :], in_=KPE_s))
            NGK = 8
            gtiles = [list(range(16 * g, 16 * g + 16)) for g in range(NGK)]
            gtiles[7].append(128)
            accb = [0, 1, 2]

            def accd(i):
                return B.ap[accb[i // 3]][:, (i % 3) * 129:(i % 3) * 129 + 129]

            kv_free = [[] for _ in range(NGK)]
            kv_ld = [None] * NGK
            q_free = [[], []]
            free_ptd = [[], [], [], []]
            free_yb = [[], []]
            free_ytd = [[], []]
            gm_free = [[], []]
            GMv = GM_s.rearrange("(j p) f -> p j f", p=128)
            ucnt = 0
            pcount = 0
            for h in range(H):
                hb = h % 2
                lqn = P.dma("sp", "lD%d" % hb, lambda e, hb=hb, h=h: e.dma_start(out=Qn[hb][:, :], in_=QT_s[h]), waits=q_free[hb])
                lqp = P.dma("sp", "lD%d" % hb, lambda e, hb=hb, h=h: e.dma_start(out=Qp[hb][0:64, :], in_=QPT_s[h]))
                lgm = P.dma("sp", "lD%d" % hb, lambda e, hb=hb, h=h: e.dma_start(out=GMh[hb][:, :, :], in_=GMv[:, :, h * 128:(h + 1) * 128]), waits=gm_free[hb])
                for g in range(NGK):
                    ta, tb = gtiles[g][0], gtiles[g][-1] + 1
                    t0, t1 = ta * 128, min(tb * 128, TK)
                    P.dma("sp", "kv%d" % g, lambda e, h=h, t0=t0, t1=t1: e.dma_start(out=KTh[:, t0:t1], in_=KT_s[h, :, t0:t1]), waits=kv_free[g])
                    kv_ld[g] = P.dma("sp", "kv%d" % g, lambda e, h=h, ta=ta, tb=tb: e.dma_start(out=Vh[:, ta:tb, :], in_=V_s[h, :, ta:tb, :]))
                for qc in range(2):
                    z = None
                    for k in range(3):
                        z = P.op("dve", lambda e, k=k: e.memset(B.ap[accb[k]][:, 0:387], 0.0), waits=B.waits(accb[k]))
                    units = [(j, s) for j in range(NT) for s in range(2)]
                    pend = []

                    def issue_pv(item):
                        (j, s, pi, ex, nk) = item
                        t = None
                        for i in range(4):
                            t = mm(accd(4 * s + i), PTd[pi][0:nk, i * 128:(i + 1) * 128], Vh[0:nk, j, :], False, False,
                                   waits=[ex, z] if i == 0 else (), inc=(i == 3), skip=True)
                        free_ptd[pi] = [t]
                        if qc == 1 and s == 1 and (j % 16 == 15 or j == 128) and not (j == 127):
                            kv_free[min(j // 16, 7)] = [t]
                        return t

                    pv_last = None
                    for (j, s) in units:
                        nk = 128 if j < 128 else 16
                        g = min(j // 16, 7)
                        sb_ = 3 + (ucnt % 3)
                        pi = ucnt % 4
                        ucnt += 1
                        pS = B.ap[sb_]
                        q0 = qc * 1024 + s * 512
                        w0 = B.waits(sb_)
                        if s == 0 and (j % 16 == 0) and j < 128:
                            w0 = w0 + [kv_ld[g]]
                        if j == 0 and s == 0:
                            w0 = w0 + [lgm, lp]
                        mm(pS[0:nk, :], KTh[:, j * 128:j * 128 + nk], Qn[hb][:, q0:q0 + 512], True, False, waits=w0)
                        ts = mm(pS[0:nk, :], kpeT[0:64, j * 128:j * 128 + nk], Qp[hb][0:64, q0:q0 + 512], False, True, inc=True)
                        ex = P.op("act", lambda e, pi=pi, nk=nk, pS=pS: e.activation(out=PTd[pi][0:nk, :], in_=pS[0:nk, :], func=AF.Exp, scale=MLA_SCALE),
                                  waits=[ts] + free_ptd[pi])
                        B.release(sb_, [ex])
                        pend.append((j, s, pi, ex, nk))
                        if len(pend) > 2:
                            pv_last = issue_pv(pend.pop(0))
                    while pend:
                        pv_last = issue_pv(pend.pop(0))
                    if qc == 1:
                        q_free[hb] = [pv_last]
                    for i in range(8):
                        a = accd(i)
                        u = qc * 8 + i
                        yb = ybf[i % 2]
                        r = P.op("dve", lambda e, a=a, i=i: e.reciprocal(out=rdd[:, i:i + 1], in_=a[:, 128:129]), waits=[pv_last])
                        ev = P.op("dve", lambda e, a=a, i=i, u=u, yb=yb, hb=hb: e.scalar_tensor_tensor(
                            out=yb[:, :], in0=a[:, 0:128], scalar=rdd[:, i:i + 1], in1=GMh[hb][:, u, :], op0=ALU.mult, op1=ALU.mult),
                            waits=[r, lgm] + free_yb[i % 2])
                        q = 6 + (i // 4) % 2
                        if i % 4 == 0:
                            wq_ = B.waits(q)
                        tt = mm(B.ap[q][:, (i % 4) * 128:(i % 4 + 1) * 128], yb[:, :], ident[:, :], True, True,
                                waits=[ev] + (wq_ if i % 4 == 0 else []), inc=True)
                        free_yb[i % 2] = [tt]
                        if i % 4 == 3:
                            yk = pcount % 2
                            e2 = P.op("act", lambda e, q=q, yk=yk, i=i: e.activation(
                                out=YTd[yk][:, (i // 4) * 512:(i // 4 + 1) * 512], in_=B.ap[q][:, :], func=AF.Copy),
                                waits=[tt] + (free_ytd[yk] if i == 3 else []))
                            B.release(q, [e2])
                            if i == 7:
                                s_ = P.dma("pool", "sD%d" % yk, lambda e, yk=yk, h=h, qc=qc: e.dma_start(
                                    out=YT_s[h * 128:(h + 1) * 128, qc * 1024:(qc + 1) * 1024], in_=YTd[yk][:, :]), waits=[e2])
                                free_ytd[yk] = [s_]
                    for k in range(3):
                        B.release(accb[k], [ev])
                    if qc == 1:
                        gm_free[hb] = [ev]
                    pcount += 1
            P.barrier()

        if stop_after is None:
            ab.reset(); af.reset()
            wst = [af.take(2048), af.take(2048)]
            wst_state["free"] = [[], []]
            wo = ab.take(16 * 1024).rearrange("p (c f) -> p c f", c=16)
            yT = ab.take(16 * OWN).rearrange("p (c t) -> p c t", c=16)
            xo = [af.take(1024) for _ in range(2)]
            oo = [af.take(1024) for _ in range(2)]
            ly = P.dma("sp", "m1", lambda e: e.dma_start(out=yT[:, :, :], in_=YT_s.rearrange("(c p) t -> p c t", p=128)))
            tw = load_w(wo, w_out, 16, 0, 1024, None, wst)
            free_xo = [[], []]
            free_oo = [[], []]
            fin = None
            for ti in range(16):
                tp = ti % 2
                lx = P.dma("sp", "lE%d" % tp, lambda e, tp=tp, ti=ti: e.dma_start(out=xo[tp][:, :], in_=xown[ti * 128:(ti + 1) * 128, :]), waits=free_xo[tp])
                for half in range(2):
                    q = (ti * 2 + half) % 8
                    t = None
                    for c in range(16):
                        t = mm(B.ap[q][:, :], yT[:, c, ti * 128:(ti + 1) * 128], wo[:, c, half * 512:(half + 1) * 512], c == 0, c == 15,
                               waits=([ly, tw] + B.waits(q)) if c == 0 else (), inc=(c == 15))
                    a = P.op("dve", lambda e, q=q, tp=tp, half=half: e.tensor_tensor(
                        out=oo[tp][:, half * 512:(half + 1) * 512], in0=B.ap[q][:, :], in1=xo[tp][:, half * 512:(half + 1) * 512], op=ALU.add),
                        waits=[t, lx] + (free_oo[tp] if half == 0 else []))
                    B.release(q, [a])
                free_xo[tp] = [a]
                fin = P.dma("pool", "fin%d" % tp, lambda e, tp=tp, ti=ti: e.dma_start(out=out[ti * 128:(ti + 1) * 128, :], in_=oo[tp][:, :]), waits=[a])
                free_oo[tp] = [fin]

        fin_waits = [(s, P.cnt[s]) for s in P.dma_sems if s in P.cnt]
        P._push("pool", lambda e: e.memset(identf[:, 0:1], 0.0), fin_waits + [(s, v) for s, v in P.cnt.items() if s in Prog.CE], None, 1)

        for nme in P.cnt:
            P.sems[nme] = es.enter_context(nc.semaphore(nme))
        block = es.enter_context(nc.Block())

        @block.sync
        def _(e):
            P.replay("sp", e)

        @block.tensor
        def _(e):
            P.replay("pe", e)

        @block.scalar
        def _(e):
            P.replay("act", e)

        @block.vector
        def _(e):
            P.replay("dve", e)

        @block.gpsimd
        def _(e):
            P.replay("pool", e)
    return nc


def _rope_table(pos):
    inv_freq = (10000.0 ** (-(np.arange(0, 64, 2, dtype=np.float32) / 64))).astype(np.float32)
    ang = pos.astype(np.float32)[:, None] * inv_freq[None, :]
    c, s = np.cos(ang).astype(np.float32), np.sin(ang).astype(np.float32)
    return np.ascontiguousarray(np.concatenate([c, c, -s, s], axis=1))


def prepare_inputs(x, meta_tokens, norm_w, w_in, q_lat_norm_w, kv_lat_norm_w, w_uq, w_ukv,
                   mla_qn_w, mla_qpe_w, mla_kn_w, mla_kpe_w, na_q_norm_w, na_k_norm_w,
                   na_rel_bias, na_meta_bias, w_out):
    f = lambda a: np.ascontiguousarray(np.asarray(a, dtype=np.float32))
    x = f(x)[0]
    meta = f(meta_tokens)
    xall = np.concatenate([x, meta], axis=0)
    xT = np.ascontiguousarray(xall.T)
    posk = np.concatenate([np.arange(SEQ) + NMETA, np.arange(NMETA)])
    csk = _rope_table(posk)
    rb = f(na_rel_bias)[0]
    cols = np.arange(GRID_W)
    c0 = np.clip(cols - 8, 0, GRID_W - 16)
    eb = np.full((NH, 2, 64, 16, 64), NEG, np.float32)
    for jj in range(2):
        for irel in range(16):
            dr = jj + 7 - irel
            if not (-7 <= dr <= 7):
                continue
            for cq in range(64):
                ck = np.arange(c0[cq], c0[cq] + 16)
                eb[:, jj, ck, irel, cq] = rb[:, dr + 7, ck - cq + 15]
    eb = np.ascontiguousarray(eb.reshape(NH, 128, 1024))
    konehot = np.zeros((WROWS, WTOK), np.float32)
    for j in range(WROWS):
        konehot[j, j * 64:(j + 1) * 64] = 1.0
    shared = {
        "xT": xT, "w_in": f(w_in)[0], "w_uq": f(w_uq)[0], "w_ukv": f(w_ukv)[0], "w_out": f(w_out)[0],
        "normw": np.ascontiguousarray(f(norm_w)[0].reshape(8, 128).T),
        "qlw": np.ascontiguousarray(f(q_lat_norm_w)[0].reshape(2, 128).T),
        "kvlw": np.ascontiguousarray(f(kv_lat_norm_w)[0].reshape(2, 128).T),
        "qnw": f(mla_qn_w)[0][None, :], "knw": f(mla_kn_w)[0][None, :],
        "qpew": f(mla_qpe_w)[0][None, :], "kpew": f(mla_kpe_w)[0][None, :],
        "naqw": f(na_q_norm_w)[0][None, :], "nakw": f(na_k_norm_w)[0][None, :],
        "csk": csk, "ebias": eb, "konehot": konehot,
        "metab": np.ascontiguousarray(f(na_meta_bias)[0].T),
    }
    in_maps = []
    for c in range(NCORES):
        xw = np.zeros((WALL, D), np.float32)
        for j in range(WROWS):
            gr = 32 * c - 4 + j
            if 0 <= gr < ROWS:
                xw[j * 64:(j + 1) * 64] = x[gr * 64:(gr + 1) * 64]
        xw[WTOK:] = meta
        qm = np.zeros((WROWS, OWN), np.float32)
        for i in range(32):
            for j in range(WROWS):
                if not na_valid(c, i, j):
                    qm[j, i * 64:(i + 1) * 64] = NEG
        m = dict(shared)
        m["xTw"] = np.ascontiguousarray(xw.T)
        m["xown"] = np.ascontiguousarray(x[c * OWN:(c + 1) * OWN])
        m["csq"] = _rope_table(np.arange(c * OWN, (c + 1) * OWN) + NMETA)
        m["qmask"] = qm
        in_maps.append(m)
    return in_maps


def kernel(**inputs):
    in_maps = prepare_inputs(**inputs)
    nc = build_program()
    res = run_bass_kernel_spmd(nc, in_maps, core_ids=list(range(NCORES)))
    outs = [np.asarray(r["out"], dtype=np.float32) for r in res.results]
    return np.concatenate(outs, axis=0)[None, :, :]
```

```python
import numpy as np
from contextlib import ExitStack
import concourse.bass as bass
import concourse.mybir as mybir
from concourse.bass_utils import run_bass_kernel_spmd

F32 = mybir.dt.float32
BF16 = mybir.dt.bfloat16
AF = mybir.ActivationFunctionType
ALU = mybir.AluOpType
AX = mybir.AxisListType

NCORES = 8
D = 1024
SEQ = 16384
NMETA = 16
TK = SEQ + NMETA
NT = 129
OWN = 2048
WROWS = 40
WTOK = WROWS * 64
WALL = WTOK + NMETA
H = 8
NH = 16
EPS = 1e-6
GRID_W = 64
ROWS = 256
NEG = -30000.0
MLA_SCALE = 192 ** -0.5
NA_SCALE = 0.125


class Prog:
    CE = ("pe", "act", "dve", "pool")

    def __init__(self):
        self.ops = {e: [] for e in ("pe", "act", "dve", "pool", "sp")}
        self.sems = {}
        self.cnt = {}
        self.waited = {e: {} for e in self.ops}
        self.pending = {e: [] for e in self.ops}
        self.dma_sems = []

    def _push(self, eng, fn, waits, inc, n):
        ws = list(self.pending[eng]) + [w for w in waits if w is not None]
        self.pending[eng] = []
        mx = {}
        for (s, v) in ws:
            if v > mx.get(s, 0):
                mx[s] = v
        fw = []
        for s, v in mx.items():
            if self.waited[eng].get(s, 0) >= v:
                continue
            self.waited[eng][s] = v
            fw.append((s, v))
        tok = None
        if inc is not None:
            self.cnt[inc] = self.cnt.get(inc, 0) + n
            tok = (inc, self.cnt[inc])
        self.ops[eng].append((fn, tuple(fw), inc, n))
        return tok

    def op(self, eng, fn, waits=(), inc=True):
        return self._push(eng, fn, waits, eng if inc else None, 1)

    def dma(self, eng, sem, fn, waits=()):
        if sem not in self.dma_sems:
            self.dma_sems.append(sem)
        return self._push(eng, fn, waits, sem, 16)

    def barrier(self):
        toks = [(s, v) for s, v in self.cnt.items()]
        for e in self.pending:
            self.pending[e] = list(toks)

    def replay(self, eng, e):
        for fn, waits, inc, n in self.ops[eng]:
            for (s, v) in waits:
                e.wait_ge(self.sems[s], v)
            ins = fn(e)
            if inc is not None:
                ins.then_inc(self.sems[inc], n)


class Arena:
    def __init__(self, t, cols):
        self.t, self.cols, self.off = t, cols, 0

    def reset(self):
        self.off = 0

    def take(self, n):
        a = self.off
        self.off += (n + 7) // 8 * 8
        assert self.off <= self.cols, (self.off, self.cols)
        return self.t[:, a:a + n]


class Banks:
    def __init__(self, aps):
        self.ap = aps
        self.free = [[] for _ in aps]

    def waits(self, i):
        return list(self.free[i])

    def release(self, i, toks):
        self.free[i] = [t for t in toks if t is not None]


def na_valid(c, i, j):
    r = 32 * c + i
    kr = 32 * c - 4 + j
    r0 = min(max(r - 4, 0), ROWS - 8)
    return (0 <= kr < ROWS) and (r0 <= kr < r0 + 8)


def na_pieces():
    out = []
    for m in range(20):
        rows = [i for i in range(32) if any(na_valid(c, i, j) for c in range(NCORES) for j in (2 * m, 2 * m + 1))]
        lo, hi = min(rows) & ~1, max(rows) | 1
        assert 0 <= lo - 2 * m + 11 and hi - 2 * m + 11 <= 15, (m, lo, hi)
        pcs = []
        r = lo
        while r <= hi:
            r2 = min(r + 8, hi + 1)
            pcs.append((r * 64, r2 * 64))
            r = r2
        out.append(pcs)
    return out


def build_program(dbg=False, stop_after=None):
    nc = bass.Bass("TRN2", target_bir_lowering=False)

    def din(name, shape, dt=F32):
        return nc.dram_tensor(name, list(shape), dt, kind="ExternalInput").ap()

    xT = din("xT", [D, TK])
    xTw = din("xTw", [D, WALL])
    xown = din("xown", [OWN, D])
    w_in = din("w_in", [D, 5696])
    w_uq = din("w_uq", [256, 1536])
    w_ukv = din("w_ukv", [256, 2048])
    w_out = din("w_out", [2048, D])
    normw = din("normw", [128, 8])
    qlw = din("qlw", [128, 2])
    kvlw = din("kvlw", [128, 2])
    qnw = din("qnw", [1, 128])
    knw = din("knw", [1, 128])
    qpew = din("qpew", [1, 64])
    kpew = din("kpew", [1, 64])
    naqw = din("naqw", [1, 64])
    nakw = din("nakw", [1, 64])
    csk = din("csk", [TK, 128])
    csq = din("csq", [OWN, 128])
    ebias = din("ebias", [NH, 128, 1024])
    qmask = din("qmask", [WROWS, OWN])
    konehot = din("konehot", [WROWS, WTOK])
    metab = din("metab", [NMETA, NH])
    out = nc.dram_tensor("out", [OWN, D], F32, kind="ExternalOutput").ap()

    skind = "ExternalOutput" if dbg else "Internal"

    def scratch(name, shape, dt=BF16):
        return nc.dram_tensor(name, list(shape), dt, kind=skind).ap()

    KT_s = scratch("KT_s", [H, 128, TK])
    KPE_s = scratch("KPE_s", [64, TK])
    V_s = scratch("V_s", [H, 128, NT, 129])
    QT_s = scratch("QT_s", [H, 128, OWN])
    QPT_s = scratch("QPT_s", [H, 64, OWN])
    GM_s = scratch("GM_s", [OWN, 1024])
    GN_s = scratch("GN_s", [OWN, 1024])
    NQT_s = scratch("NQT_s", [NH, 64, OWN])
    NKT_s = scratch("NKT_s", [NH, 64, WALL])
    NV_s = scratch("NV_s", [21 * 128, NH, 65])
    YT_s = scratch("YT_s", [2048, OWN])

    P = Prog()
    with ExitStack() as es:
        AB_COLS = 73728
        AF_COLS = 14336
        abt = es.enter_context(nc.sbuf_tensor("arena_bf", [128, AB_COLS], BF16))
        aft = es.enter_context(nc.sbuf_tensor("arena_f", [128, AF_COLS], F32))
        ident = es.enter_context(nc.sbuf_tensor("ident", [128, 128], BF16))
        identf = es.enter_context(nc.sbuf_tensor("identf", [128, 128], F32))
        ones = es.enter_context(nc.sbuf_tensor("ones", [128, 8], BF16))
        consts = es.enter_context(nc.sbuf_tensor("consts", [128, 1024], F32))
        pb = [es.enter_context(nc.psum_tensor("pb%d" % i, [128, 512], F32)) for i in range(8)]
        ab = Arena(abt, AB_COLS)
        af = Arena(aft, AF_COLS)
        B = Banks([p[:, :] for p in pb])

        qkw_t = consts[:, 0:128]
        qpew_t = consts[:, 384:448]
        kpew_t = consts[:, 448:512]
        naqw_t = consts[:, 512:576]
        nakw_t = consts[:, 576:640]
        normw_t = consts[:, 640:648]
        qlw_t = consts[:, 648:650]
        kvlw_t = consts[:, 650:652]
        metab_t = consts[:, 656:672]

        P.op("pool", lambda e: e.memset(identf[:], 0.0))
        i1 = P.op("pool", lambda e: e.affine_select(out=identf[:], in_=identf[:], pattern=[[-1, 128]],
                                                    compare_op=ALU.not_equal, fill=1.0, base=0, channel_multiplier=1))
        P.op("dve", lambda e: e.tensor_copy(out=ident[:], in_=identf[:]), waits=[i1])
        P.op("dve", lambda e: e.memset(ones[:], 1.0))
        lt = None
        for (dst, src) in [(consts[:, 128:256], qnw), (consts[:, 256:384], knw), (qpew_t, qpew), (kpew_t, kpew),
                           (naqw_t, naqw), (nakw_t, nakw)]:
            lt = P.dma("sp", "ld0", lambda e, dst=dst, src=src: e.dma_start(out=dst, in_=src.partition_broadcast(128)))
        for (dst, src) in [(normw_t, normw), (qlw_t, qlw), (kvlw_t, kvlw)]:
            lt = P.dma("sp", "ld0", lambda e, dst=dst, src=src: e.dma_start(out=dst, in_=src))
        lt = P.dma("sp", "ld0", lambda e: e.dma_start(out=metab_t[0:16, :], in_=metab))
        P.op("dve", lambda e: e.tensor_tensor(out=qkw_t, in0=consts[:, 128:256], in1=consts[:, 256:384], op=ALU.mult), waits=[lt])
        P.barrier()

        wst_state = {"k": 0, "free": [[], []]}

        def load_w(dst3, src2d, nch, c0, c1, rowscale, wst):
            last = None
            for ch in range(nch):
                for a in range(c0, c1, 2048):
                    b_ = min(a + 2048, c1)
                    k = wst_state["k"] % 2
                    wst_state["k"] += 1
                    stg = wst[k][:, 0:b_ - a]
                    t = P.dma("sp", "lw%d" % k, lambda e, stg=stg, ch=ch, a=a, b_=b_: e.dma_start(
                        out=stg, in_=src2d[ch * 128:(ch + 1) * 128, a:b_]), waits=wst_state["free"][k])
                    dsl = dst3[:, ch, a - c0:b_ - c0]
                    if rowscale is not None:
                        if k == 0:
                            t2 = P.op("dve", lambda e, dsl=dsl, stg=stg, ch=ch: e.tensor_scalar(
                                out=dsl, in0=stg, scalar1=rowscale[:, ch:ch + 1], scalar2=None, op0=ALU.mult), waits=[t])
                        else:
                            t2 = P.op("act", lambda e, dsl=dsl, stg=stg, ch=ch: e.activation(
                                out=dsl, in_=stg, func=AF.Copy, scale=rowscale[:, ch:ch + 1]), waits=[t])
                    else:
                        if k == 0:
                            t2 = P.op("dve", lambda e, dsl=dsl, stg=stg: e.tensor_copy(out=dsl, in_=stg), waits=[t])
                        else:
                            t2 = P.op("act", lambda e, dsl=dsl, stg=stg: e.activation(out=dsl, in_=stg, func=AF.Copy), waits=[t])
                    wst_state["free"][k] = [t2]
                    last = t2
            return last

        def mm(out_ap, lhsT, rhs, start, stop, waits=(), inc=False, skip=False):
            if skip:
                return P.op("pe", lambda e: e.matmul(out_ap, lhsT=lhsT, rhs=rhs, start=start, stop=stop, skip_group_check=True),
                            waits=waits, inc=inc)
            return P.op("pe", lambda e: e.matmul(out_ap, lhsT=lhsT, rhs=rhs, start=start, stop=stop), waits=waits, inc=inc)

        scr_free = {}

        def headnorm(src3, n, nh, hd, outp, sq, tmp, stt, pre, wtile, rope_cs, waits, tmp2=None, key=None):
            waits = list(waits) + scr_free.get(key, [])
            tok = _headnorm(src3, n, nh, hd, outp, sq, tmp, stt, pre, wtile, rope_cs, waits, tmp2)
            scr_free[key] = [tok]
            return tok

        def _headnorm(src3, n, nh, hd, outp, sq, tmp, stt, pre, wtile, rope_cs, waits, tmp2=None):
            ss = stt[:n, 0:nh]
            sd = stt[:n, nh:2 * nh]
            rr = stt[:n, 2 * nh:3 * nh]
            a = P.op("act", lambda e: e.activation(out=sq, in_=src3, func=AF.Square), waits=waits)
            d = P.op("dve", lambda e: e.tensor_reduce(out=ss, in_=sq, axis=AX.X, op=ALU.add), waits=[a])
            if pre is not None:
                d = P.op("dve", lambda e: e.tensor_scalar(out=ss, in0=ss, scalar1=pre[1], scalar2=None, op0=ALU.mult), waits=[d])
            a = P.op("act", lambda e: e.activation(out=sd, in_=ss, func=AF.Sqrt, bias=EPS, scale=1.0 / hd), waits=[d])
            d = P.op("dve", lambda e: e.reciprocal(out=rr, in_=sd), waits=[a])
            if pre is not None:
                d = P.op("dve", lambda e: e.tensor_scalar(out=rr, in0=rr, scalar1=pre[0], scalar2=None, op0=ALU.mult), waits=[d])
            rb = rr.unsqueeze(2).to_broadcast([n, nh, hd])
            if wtile is None and rope_cs is None:
                return P.op("dve", lambda e: e.tensor_tensor(out=outp, in0=src3, in1=rb, op=ALU.mult), waits=[d])
            d = P.op("dve", lambda e: e.tensor_tensor(out=tmp, in0=src3, in1=rb, op=ALU.mult), waits=[d])
            wb = wtile[:n, :].unsqueeze(1).to_broadcast([n, nh, hd])
            if rope_cs is None:
                return P.op("dve", lambda e: e.tensor_tensor(out=outp, in0=tmp, in1=wb, op=ALU.mult), waits=[d])
            d = P.op("dve", lambda e: e.tensor_tensor(out=tmp, in0=tmp, in1=wb, op=ALU.mult), waits=[d])
            hh = hd // 2
            cc = rope_cs[:n, 0:hd].unsqueeze(1).to_broadcast([n, nh, hd])
            nsin = rope_cs[:n, hd:hd + hh].unsqueeze(1).to_broadcast([n, nh, hh])
            psin = rope_cs[:n, hd + hh:2 * hd].unsqueeze(1).to_broadcast([n, nh, hh])
            d1 = P.op("dve", lambda e: e.tensor_tensor(out=tmp2[:, :, 0:hh], in0=tmp[:, :, hh:hd], in1=nsin, op=ALU.mult), waits=[d])
            d2 = P.op("dve", lambda e: e.tensor_tensor(out=tmp2[:, :, hh:hd], in0=tmp[:, :, 0:hh], in1=psin, op=ALU.mult), waits=[d])
            d3 = P.op("dve", lambda e: e.tensor_tensor(out=tmp, in0=tmp, in1=cc, op=ALU.mult), waits=[d2])
            return P.op("dve", lambda e: e.tensor_tensor(out=outp, in0=tmp, in1=tmp2, op=ALU.add), waits=[d3])

        def rstd_from_ss(ss_ps, n, stt, waits):
            a = P.op("act", lambda e: e.activation(out=stt[:n, 0:1], in_=ss_ps, func=AF.Sqrt, bias=EPS, scale=1.0 / D), waits=waits)
            d = P.op("dve", lambda e: e.reciprocal(out=stt[:n, 1:2], in_=stt[:n, 0:1]), waits=[a])
            d = P.op("dve", lambda e: e.tensor_tensor(out=stt[:n, 2:3], in0=stt[:n, 1:2], in1=stt[:n, 1:2], op=ALU.mult), waits=[d])
            return (stt[:n, 1:2], stt[:n, 2:3]), d

        ab.reset(); af.reset()
        wst = [af.take(2048), af.take(2048)]
        xt = [af.take(4096).rearrange("p (c t) -> p c t", c=8) for _ in range(2)]
        cs = [af.take(512).rearrange("p (j f) -> p j f", j=4) for _ in range(2)]
        stA = [af.take(16) for _ in range(2)]
        stK = [af.take(64) for _ in range(2)]
        sqA = af.take(512)
        tmpA = af.take(64)
        tmpB = af.take(64)
        wkv = ab.take(8 * 320).rearrange("p (c f) -> p c f", c=8)
        wukv = ab.take(2 * 2048).rearrange("p (c f) -> p c f", c=2)
        xb = [ab.take(4096).rearrange("p (c t) -> p c t", c=8) for _ in range(2)]
        xsq = [ab.take(4096).rearrange("p (c t) -> p c t", c=8) for _ in range(2)]
        cn = [ab.take(256) for _ in range(2)]
        kpe_b = [ab.take(64) for _ in range(2)]
        cTt = [ab.take(256).rearrange("p (c t) -> p c t", c=2) for _ in range(2)]
        khat = [ab.take(1024).rearrange("p (h d) -> p h d", h=8) for _ in range(2)]
        KTst = [ab.take(4096).rearrange("p (h t) -> p h t", h=8) for _ in range(2)]
        Vst = [ab.take(8 * 4 * 129).rearrange("p (h j e) -> p h j e", h=8, j=4) for _ in range(2)]
        kpest = [ab.take(512) for _ in range(2)]

        load_w(wkv, w_in, 8, 256, 576, normw_t, wst)
        tw = load_w(wukv, w_ukv, 2, 0, 2048, kvlw_t, wst)
        for b_ in range(2):
            P.op("pool", lambda e, b_=b_: e.memset(Vst[b_][:, :, :, :], 1.0))

        xTv = xT.rearrange("(c p) t -> p c t", p=128)
        KTv = KT_s.rearrange("h d t -> d h t")
        Vv = V_s.rearrange("h p j e -> p h j e")
        NGA = 33
        gfree_x = [[], []]
        gfree_xb = [[], []]
        gfree_st = [[], []]
        tcount = 0
        for g in range(NGA):
            b_ = g % 2
            t0 = g * 512
            ng = 512 if g < 32 else 16
            ntile = 4 if g < 32 else 1
            l1 = P.dma("sp", "lA%d" % b_, lambda e, b_=b_, t0=t0, ng=ng: e.dma_start(out=xt[b_][:, :, 0:ng], in_=xTv[:, :, t0:t0 + ng]),
                       waits=gfree_x[b_])
            if g < 32:
                l2 = P.dma("sp", "lA%d" % b_, lambda e, b_=b_, t0=t0: e.dma_start(
                    out=cs[b_][:, :, :], in_=csk[t0:t0 + 512, :].rearrange("(j p) f -> p j f", p=128)))
            else:
                l2 = P.dma("sp", "lA%d" % b_, lambda e, b_=b_, t0=t0: e.dma_start(out=cs[b_][0:16, 0, :], in_=csk[t0:t0 + 16, :]))
            c1 = P.op("act", lambda e, b_=b_, ng=ng: e.activation(out=xb[b_][:, :, 0:ng], in_=xt[b_][:, :, 0:ng], func=AF.Copy),
                      waits=[l2] + gfree_xb[b_])
            c2 = P.op("pool", lambda e, b_=b_, ng=ng: e.tensor_tensor(out=xsq[b_][:, :, 0:ng], in0=xt[b_][:, :, 0:ng],
                                                                      in1=xt[b_][:, :, 0:ng], op=ALU.mult), waits=[l2] + gfree_xb[b_])
            last_pe = None
            last_rope = None
            stage_toks = []
            for j in range(ntile):
                n = 128 if g < 32 else 16
                tp = tcount % 2
                tcount += 1
                sl = slice(j * 128, j * 128 + n)
                pk = B.ap[tp]
                for c in range(8):
                    mm(pk[:n, 0:320], xb[b_][:, c, sl], wkv[:, c, :], c == 0, c == 7, waits=([c1] + B.waits(tp)) if c == 0 else ())
                for c in range(8):
                    tk = mm(pk[:n, 384:385], xsq[b_][:, c, sl], ones[:, 0:1], c == 0, c == 7, waits=[c2] if c == 0 else (), inc=(c == 7))
                last_pe = tk
                st = stA[tp]
                (r1, r1sq), d = rstd_from_ss(pk[:n, 384:385], n, st, [tk])
                dc = headnorm(pk[:n, 0:256].rearrange("p (h d) -> p h d", h=1), n, 1, 256, cn[tp][:n, :].rearrange("p (h d) -> p h d", h=1),
                              sqA[:n, 0:256].rearrange("p (h d) -> p h d", h=1), None, st[:, 4:8], (r1, r1sq), None, None, [tk, d], key="s0")
                dk = headnorm(pk[:n, 256:320].rearrange("p (h d) -> p h d", h=1), n, 1, 64, kpe_b[tp][:n, :].rearrange("p (h d) -> p h d", h=1),
                              sqA[:n, 256:320].rearrange("p (h d) -> p h d", h=1), tmpA[:n, :].rearrange("p (h d) -> p h d", h=1),
                              st[:, 8:12], (r1, r1sq), kpew_t, cs[b_][:, j, :], [tk, d, l2, dc],
                              tmp2=tmpB[:n, :].rearrange("p (h d) -> p h d", h=1), key="s1")
                B.release(tp, [dc, dk])
                last_rope = dk
                pt = B.ap[2]
                mm(pt[:, 0:n], cn[tp][:n, 0:128], ident[:n, :n], True, True, waits=[dc, dk] + B.waits(2))
                mm(pt[:, 128:128 + n], cn[tp][:n, 128:256], ident[:n, :n], True, True)
                tt = mm(pt[0:64, 256:256 + n], kpe_b[tp][:n, 0:64], ident[:n, :n], True, True, inc=True)
                e1 = P.op("dve", lambda e, tp=tp, n=n, pt=pt: e.tensor_copy(
                    out=cTt[tp][:, :, 0:n], in_=pt[:, 0:256].rearrange("p (c t) -> p c t", c=2)[:, :, 0:n]), waits=[tt])
                e2 = P.op("act", lambda e, b_=b_, sl=sl, n=n, pt=pt: e.activation(out=kpest[b_][0:64, sl], in_=pt[0:64, 256:256 + n], func=AF.Copy),
                          waits=[tt, e1] + (gfree_st[b_] if j == 0 else []))
                B.release(2, [e1, e2])
                stage_toks.append(e2)
                stk = stK[tp]
                kh = khat[tp]
                tr_toks = []
                for gp in range(4):
                    q = 3 + (gp % 2)
                    pu = B.ap[q]
                    mm(pu[:n, :], cTt[tp][:, 0, 0:n], wukv[:, 0, gp * 512:(gp + 1) * 512], True, False, waits=[e1] + B.waits(q))
                    tu = mm(pu[:n, :], cTt[tp][:, 1, 0:n], wukv[:, 1, gp * 512:(gp + 1) * 512], False, True, inc=True)
                    pu4 = pu[:n, :].rearrange("p (h two d) -> p h two d", h=2, two=2)
                    dkk = headnorm(pu4[:, :, 0, :], n, 2, 128, kh[:n, 2 * gp:2 * gp + 2, :], sqA[:n, 0:256].rearrange("p (h d) -> p h d", h=2),
                                   None, stk[:, 8 * gp:8 * gp + 8], None, None, None, [tu], key="s0")
                    av = P.op("act", lambda e, b_=b_, n=n, gp=gp, j=j, pu4=pu4: e.activation(
                        out=Vst[b_][:n, 2 * gp:2 * gp + 2, j, 0:128], in_=pu4[:, :, 1, :], func=AF.Copy),
                        waits=[tu, dkk] + (gfree_st[b_] if (j == 0 and gp == 0) else []))
                    B.release(q, [dkk, av])
                    stage_toks.append(av)
                    q2 = 5 + (gp % 2)
                    p2 = B.ap[q2]
                    mm(p2[:, 0:n], kh[:n, 2 * gp, :], ident[:n, :n], True, True, waits=[dkk] + B.waits(q2))
                    t2 = mm(p2[:, 128:128 + n], kh[:n, 2 * gp + 1, :], ident[:n, :n], True, True, inc=True)
                    tr_toks.append(t2)
                    e3 = P.op("dve", lambda e, b_=b_, gp=gp, sl=sl, n=n, p2=p2: e.tensor_copy(
                        out=KTst[b_][:, 2 * gp:2 * gp + 2, sl], in_=p2[:, 0:256].rearrange("p (h t) -> p h t", h=2)[:, :, 0:n]),
                        waits=[t2] + (gfree_st[b_] if (j == 0 and gp == 0) else []))
                    B.release(q2, [e3])
                    stage_toks.append(e3)
                last_pe = tr_toks[-1]
            gfree_x[b_] = [c1, c2, last_rope]
            gfree_xb[b_] = [last_pe]
            nrow = 128 if g < 32 else 16
            s1 = P.dma("pool", "sA%d" % b_, lambda e, b_=b_, t0=t0, ng=ng: e.dma_start(out=KTv[:, :, t0:t0 + ng], in_=KTst[b_][:, :, 0:ng]),
                       waits=stage_toks)
            s2 = P.dma("pool", "sA%d" % b_, lambda e, b_=b_, g=g, ntile=ntile, nrow=nrow: e.dma_start(
                out=Vv[0:nrow, :, 4 * g:4 * g + ntile, :], in_=Vst[b_][0:nrow, :, 0:ntile, :]))
            s3 = P.dma("pool", "sA%d" % b_, lambda e, b_=b_, t0=t0, ng=ng: e.dma_start(out=KPE_s[:, t0:t0 + ng], in_=kpest[b_][0:64, 0:ng]))
            gfree_st[b_] = [s3]
        P.barrier()

        if stop_after != "A":
            ab.reset(); af.reset()
            wst = [af.take(2048), af.take(2048)]
            wst_state["free"] = [[], []]
            xtB = [af.take(1024).rearrange("p (c t) -> p c t", c=8) for _ in range(2)]
            csB = [af.take(128) for _ in range(2)]
            stB = [af.take(16) for _ in range(2)]
            stH = [af.take(64) for _ in range(8)]
            sqB = [af.take(512) for _ in range(2)]
            tmB = [af.take(512) for _ in range(2)]
            tm2 = [af.take(512) for _ in range(2)]
            wq = ab.take(8 * 256).rearrange("p (c f) -> p c f", c=8)
            wgm = ab.take(8 * 1024).rearrange("p (c f) -> p c f", c=8)
            wnq = ab.take(8 * 1024).rearrange("p (c f) -> p c f", c=8)
            wnk = ab.take(8 * 1024).rearrange("p (c f) -> p c f", c=8)
            wnv = ab.take(8 * 1024).rearrange("p (c f) -> p c f", c=8)
            wgn = ab.take(8 * 1024).rearrange("p (c f) -> p c f", c=8)
            wuq = ab.take(2 * 1536).rearrange("p (c f) -> p c f", c=2)
            xbB = [ab.take(1024).rearrange("p (c t) -> p c t", c=8) for _ in range(2)]
            xsqB = [ab.take(1024).rearrange("p (c t) -> p c t", c=8) for _ in range(2)]
            nkh = [ab.take(1024) for _ in range(2)]
            nqh = nkh
            NKst = [ab.take(8 * 128).rearrange("p (h t) -> p h t", h=8) for _ in range(2)]
            NQst = [ab.take(8 * 128).rearrange("p (h t) -> p h t", h=8) for _ in range(2)]
            NVst = [ab.take(16 * 65).rearrange("p (h e) -> p h e", h=16) for _ in range(2)]
            qlb = [ab.take(256) for _ in range(2)]
            qlT = [ab.take(256).rearrange("p (c t) -> p c t", c=2) for _ in range(2)]
            qnb = [ab.take(1024).rearrange("p (h d) -> p h d", h=8) for _ in range(2)]
            qpb = [ab.take(512).rearrange("p (h d) -> p h d", h=8) for _ in range(2)]
            QnSt = [ab.take(8 * 128).rearrange("p (h t) -> p h t", h=8) for _ in range(2)]
            QpSt = [ab.take(4 * 128).rearrange("p (h t) -> p h t", h=4) for _ in range(2)]
            GMst = [ab.take(1024) for _ in range(2)]
            GNst = [ab.take(1024) for _ in range(2)]
            load_w(wq, w_in, 8, 0, 256, normw_t, wst)
            load_w(wgm, w_in, 8, 576, 1600, normw_t, wst)
            load_w(wnq, w_in, 8, 1600, 2624, normw_t, wst)
            load_w(wnk, w_in, 8, 2624, 3648, normw_t, wst)
            load_w(wnv, w_in, 8, 3648, 4672, normw_t, wst)
            load_w(wgn, w_in, 8, 4672, 5696, normw_t, wst)
            load_w(wuq, w_uq, 2, 0, 1536, qlw_t, wst)
            for b_ in range(2):
                P.op("pool", lambda e, b_=b_: e.memset(NVst[b_][:, :, :], 1.0))
            xTwv = xTw.rearrange("(c p) t -> p c t", p=128)
            NKTv = NKT_s.rearrange("(hp two) d t -> (two d) hp t", two=2)
            NQTv = NQT_s.rearrange("(hp two) d t -> (two d) hp t", two=2)
            QTnv = QT_s.rearrange("h d t -> d h t")
            QTpv = QPT_s.rearrange("(hp two) r t -> (two r) hp t", two=2)
            free_x = [[], []]
            free_xb = [[], []]
            free_st = [[], []]
            free_cs = [[], []]
            bk = {"i": 0}

            def nextbank():
                i = bk["i"] % 8
                bk["i"] += 1
                return i

            def proj(n, tp, wmat, col0, ncols, c1tok):
                q = nextbank()
                t = None
                for c in range(8):
                    t = mm(B.ap[q][:n, 0:ncols], xbB[tp][:, c, 0:n], wmat[:, c, col0:col0 + ncols], c == 0, c == 7,
                           waits=([c1tok] + B.waits(q)) if c == 0 else (), inc=(c == 7))
                return q, t

            def transposes_out(src_bf, n, nblk, stage, st_waits, src_waits):
                toks = []
                for k0 in range(0, nblk, 4):
                    q = nextbank()
                    kk = min(4, nblk - k0)
                    t = None
                    for k in range(kk):
                        t = mm(B.ap[q][:, k * 128:k * 128 + n], src_bf[:n, (k0 + k) * 128:(k0 + k + 1) * 128], ident[:n, :n], True, True,
                               waits=(B.waits(q) + list(src_waits)) if k == 0 else (), inc=(k == kk - 1))
                    ev = P.op("act", lambda e, q=q, k0=k0, kk=kk, n=n: e.activation(
                        out=stage[:, k0:k0 + kk, 0:n], in_=B.ap[q][:, 0:kk * 128].rearrange("p (h t) -> p h t", h=kk)[:, :, 0:n], func=AF.Copy),
                        waits=[t] + (st_waits if k0 == 0 else []))
                    B.release(q, [ev])
                    toks.append(ev)
                return toks

            nwt = int(stop_after[2:]) if (stop_after or "").startswith("Bn") else 21
            for wt in range(nwt):
                tp = wt % 2
                n = 128 if wt < 20 else 16
                own = 2 <= wt < 18
                ti = wt - 2
                tok0 = wt * 128
                l1 = P.dma("sp", "lB%d" % tp, lambda e, tp=tp, n=n, tok0=tok0: e.dma_start(out=xtB[tp][:, :, 0:n], in_=xTwv[:, :, tok0:tok0 + n]),
                           waits=free_x[tp])
                l2 = None
                lb = l1
                if own:
                    l2 = P.dma("sp", "lB%d" % tp, lambda e, tp=tp, ti=ti: e.dma_start(out=csB[tp][:, :], in_=csq[ti * 128:(ti + 1) * 128, :]),
                               waits=free_cs[tp])
                    lb = l2
                c1 = P.op("act", lambda e, tp=tp, n=n: e.activation(out=xbB[tp][:, :, 0:n], in_=xtB[tp][:, :, 0:n], func=AF.Copy),
                          waits=[lb] + free_xb[tp])
                c2 = P.op("pool", lambda e, tp=tp, n=n: e.tensor_tensor(out=xsqB[tp][:, :, 0:n], in0=xtB[tp][:, :, 0:n], in1=xtB[tp][:, :, 0:n],
                                                                        op=ALU.mult), waits=[lb] + free_xb[tp])
                qs = nextbank()
                tk = None
                for c in range(8):
                    tk = mm(B.ap[qs][:n, 0:1], xsqB[tp][:, c, 0:n], ones[:, 0:1], c == 0, c == 7, waits=([c2] + B.waits(qs)) if c == 0 else (),
                            inc=(c == 7))
                st = stB[tp]
                (r1, r1sq), d = rstd_from_ss(B.ap[qs][:n, 0:1], n, st, [tk])
                B.release(qs, [d])
                stage_toks = []
                for half in range(2):
                    q, t = proj(n, tp, wnk, half * 512, 512, c1)
                    dk = headnorm(B.ap[q][:n, :].rearrange("p (h d) -> p h d", h=8), n, 8, 64,
                                  nkh[tp][:n, half * 512:(half + 1) * 512].rearrange("p (h d) -> p h d", h=8),
                                  sqB[half][:n, :].rearrange("p (h d) -> p h d", h=8), tmB[half][:n, :].rearrange("p (h d) -> p h d", h=8),
                                  stH[half], (r1, r1sq), nakw_t, None, [t, d], key="h%d" % half)
                    B.release(q, [dk])
                evs = transposes_out(nkh[tp], n, 8, NKst[tp], free_st[tp], [dk])
                stage_toks += evs
                for half in range(2):
                    q, t = proj(n, tp, wnv, half * 512, 512, c1)
                    av = P.op("act", lambda e, q=q, n=n, tp=tp, half=half, r1=r1: e.activation(
                        out=NVst[tp][:n, half * 8:(half + 1) * 8, 0:64], in_=B.ap[q][:n, :].rearrange("p (h d) -> p h d", h=8),
                        func=AF.Copy, scale=r1), waits=[t, d] + free_st[tp])
                    B.release(q, [av])
                    stage_toks.append(av)
                last_pe_x = None
                if own:
                    q, t = proj(n, tp, wq, 0, 256, c1)
                    dq = headnorm(B.ap[q][:n, 0:256].rearrange("p (h d) -> p h d", h=1), n, 1, 256, qlb[tp][:n, :].rearrange("p (h d) -> p h d", h=1),
                                  sqB[0][:n, 0:256].rearrange("p (h d) -> p h d", h=1), None, stH[2], (r1, r1sq), None, None, [t, d], key="h0")
                    B.release(q, [dq])
                    q = nextbank()
                    mm(B.ap[q][:, 0:128], qlb[tp][:, 0:128], ident[:, :], True, True, waits=[dq] + B.waits(q))
                    t = mm(B.ap[q][:, 128:256], qlb[tp][:, 128:256], ident[:, :], True, True, inc=True)
                    eq = P.op("dve", lambda e, q=q, tp=tp: e.tensor_copy(out=qlT[tp][:, :, :], in_=B.ap[q][:, 0:256].rearrange("p (c t) -> p c t", c=2)),
                              waits=[t])
                    B.release(q, [eq])
                    for gq in range(4):
                        q = nextbank()
                        mm(B.ap[q][:, 0:384], qlT[tp][:, 0, :], wuq[:, 0, gq * 384:(gq + 1) * 384], True, False, waits=[eq] + B.waits(q))
                        t = mm(B.ap[q][:, 0:384], qlT[tp][:, 1, :], wuq[:, 1, gq * 384:(gq + 1) * 384], False, True, inc=True)
                        v3 = B.ap[q][:, 0:384].rearrange("p (h d) -> p h d", h=2)
                        d1 = headnorm(v3[:, :, 0:128], 128, 2, 128, qnb[tp][:, 2 * gq:2 * gq + 2, :], sqB[0][:, 0:256].rearrange("p (h d) -> p h d", h=2),
                                      tmB[0][:, 0:256].rearrange("p (h d) -> p h d", h=2), stH[3], None, qkw_t, None, [t], key="h0")
                        d2 = headnorm(v3[:, :, 128:192], 128, 2, 64, qpb[tp][:, 2 * gq:2 * gq + 2, :], sqB[1][:, 0:128].rearrange("p (h d) -> p h d", h=2),
                                      tmB[1][:, 0:128].rearrange("p (h d) -> p h d", h=2), stH[4], None, qpew_t, csB[tp], [t, l2, d1],
                                      tmp2=tm2[1][:, 0:128].rearrange("p (h d) -> p h d", h=2), key="h1")
                        B.release(q, [d1, d2])
                    free_cs[tp] = [d2]
                    evs = transposes_out(qnb[tp][:, :, :].rearrange("p h d -> p (h d)"), 128, 8, QnSt[tp], free_st[tp], [d1, d2])
                    stage_toks += evs
                    evs = transposes_out(qpb[tp][:, :, :].rearrange("p h d -> p (h d)"), 128, 4, QpSt[tp], free_st[tp], [d1, d2])
                    stage_toks += evs
                    for (wmat, gst) in ((wgm, GMst), (wgn, GNst)):
                        for half in range(2):
                            q, t = proj(n, tp, wmat, half * 512, 512, c1)
                            ag = P.op("act", lambda e, q=q, tp=tp, half=half, gst=gst, r1=r1: e.activation(
                                out=gst[tp][:, half * 512:(half + 1) * 512], in_=B.ap[q][:, :], func=AF.Silu, scale=r1),
                                waits=[t, d] + free_st[tp])
                            B.release(q, [ag])
                            stage_toks.append(ag)
                    for half in range(2):
                        q, t = proj(n, tp, wnq, half * 512, 512, c1)
                        dk = headnorm(B.ap[q][:, :].rearrange("p (h d) -> p h d", h=8), 128, 8, 64,
                                      nqh[tp][:, half * 512:(half + 1) * 512].rearrange("p (h d) -> p h d", h=8),
                                      sqB[half][:, :].rearrange("p (h d) -> p h d", h=8), tmB[half][:, :].rearrange("p (h d) -> p h d", h=8),
                                      stH[5 + half], (r1, r1sq), naqw_t, None, [t, d], key="h%d" % half)
                        B.release(q, [dk])
                        last_pe_x = t
                    evs = transposes_out(nqh[tp], 128, 8, NQst[tp], free_st[tp], [dk])
                    stage_toks += evs
                else:
                    last_pe_x = t
                free_x[tp] = [c1, c2]
                free_xb[tp] = [last_pe_x] if last_pe_x is not None else []
                s = P.dma("pool", "sB%d" % tp, lambda e, tp=tp, n=n, tok0=tok0: e.dma_start(out=NKTv[:, :, tok0:tok0 + n], in_=NKst[tp][:, :, 0:n]),
                          waits=stage_toks)
                s = P.dma("pool", "sB%d" % tp, lambda e, tp=tp, n=n, tok0=tok0: e.dma_start(out=NV_s[tok0:tok0 + n, :, :], in_=NVst[tp][0:n, :, :]))
                if own:
                    q0 = ti * 128
                    s = P.dma("pool", "sB%d" % tp, lambda e, tp=tp, q0=q0: e.dma_start(out=QTnv[:, :, q0:q0 + 128], in_=QnSt[tp][:, :, :]))
                    s = P.dma("pool", "sB%d" % tp, lambda e, tp=tp, q0=q0: e.dma_start(out=QTpv[:, :, q0:q0 + 128], in_=QpSt[tp][:, :, :]))
                    s = P.dma("pool", "sB%d" % tp, lambda e, tp=tp, q0=q0: e.dma_start(out=NQTv[:, :, q0:q0 + 128], in_=NQst[tp][:, :, :]))
                    s = P.dma("pool", "sB%d" % tp, lambda e, tp=tp, q0=q0: e.dma_start(out=GM_s[q0:q0 + 128, :], in_=GMst[tp][:, :]))
                    s = P.dma("pool", "sB%d" % tp, lambda e, tp=tp, q0=q0: e.dma_start(out=GN_s[q0:q0 + 128, :], in_=GNst[tp][:, :]))
                free_st[tp] = [s]
            P.barrier()

        if stop_after not in ("A", "B") and not (stop_after or "").startswith("Bn"):
            ab.reset(); af.reset()
            NKA = [ab.take(WALL) for _ in range(2)]
            NQA = [ab.take(OWN) for _ in range(2)]
            NVall = ab.take(21 * 16 * 65).rearrange("p (j h e) -> p j h e", j=21, h=16)
            GNall = ab.take(16 * 1024).rearrange("p (j f) -> p j f", j=16)
            PTc = [ab.take(512) for _ in range(4)]
            ypair = ab.take(16 * 128).rearrange("p (u f) -> p u f", u=16)
            YTst = ab.take(OWN)
            EB = [af.take(1024) for _ in range(2)]
            Ef = [af.take(512) for _ in range(2)]
            rdn = af.take(16)
            mstage = af.take(WTOK)
            pieces = na_pieces()
            lA = P.dma("sp", "m0", lambda e: e.dma_start(out=mstage[64:104, 0:WTOK], in_=konehot))
            for b_ in range(2):
                P.op("dve", lambda e, b_=b_: e.memset(NKA[b_][64:128, :], 0.0))
                P.op("dve", lambda e, b_=b_: e.memset(NQA[b_][64:128, :], 0.0))
                lk = P.op("dve", lambda e, b_=b_: e.tensor_copy(out=NKA[b_][64:104, 0:WTOK], in_=mstage[64:104, 0:WTOK]), waits=[lA])
            lB = P.dma("sp", "m1", lambda e: e.dma_start(out=mstage[64:104, 0:OWN], in_=qmask), waits=[lk])
            for b_ in range(2):
                lq = P.op("dve", lambda e, b_=b_: e.tensor_copy(out=NQA[b_][64:104, :], in_=mstage[64:104, 0:OWN]), waits=[lB])
            lv = P.dma("sp", "m2", lambda e: e.dma_start(out=NVall[:, :, :, :], in_=NV_s.rearrange("(j p) h e -> p j h e", p=128)))
            lg = P.dma("sp", "m3", lambda e: e.dma_start(out=GNall[:, :, :], in_=GN_s.rearrange("(j p) f -> p j f", p=128)))
            accb = [0, 1, 2]

            def acc_ap(u):
                return B.ap[accb[u // 7]][:, (u % 7) * 65:(u % 7) * 65 + 65]

            free_nk = [[], []]
            free_eb = [[], []]
            free_pt = [[], [], [], []]
            free_ef = [[], []]
            free_yp = []
            free_yst = []
            ucnt = 0
            for h in range(NH):
                b_ = h % 2
                lk = P.dma("sp", "lC%d" % b_, lambda e, b_=b_, h=h: e.dma_start(out=NKA[b_][0:64, :], in_=NKT_s[h]), waits=free_nk[b_])
                lq2 = P.dma("sp", "lC%d" % b_, lambda e, b_=b_, h=h: e.dma_start(out=NQA[b_][0:64, :], in_=NQT_s[h]))
                le = P.dma("sp", "lC%d" % b_, lambda e, b_=b_, h=h: e.dma_start(out=EB[b_][:, :], in_=ebias[h]), waits=free_eb[b_])
                ae = P.op("act", lambda e, b_=b_: e.activation(out=EB[b_][:, :], in_=EB[b_][:, :], func=AF.Exp), waits=[le])
                z = None
                for k in range(3):
                    z = P.op("dve", lambda e, k=k: e.memset(B.ap[accb[k]][:, 0:455], 0.0), waits=B.waits(accb[k]))
                pv_last = None
                first = True
                pendc = []

                def issue_pv_c(item):
                    (m_, q0_, q1_, pi_, ready_, nk_) = item
                    t_ = None
                    for u in range(q0_ // 128, q1_ // 128):
                        o = u * 128 - q0_
                        t_ = mm(acc_ap(u), PTc[pi_][0:nk_, o:o + 128], NVall[0:nk_, m_, h, :], False, False,
                                waits=[ready_] if u == q0_ // 128 else (), inc=(u == q1_ // 128 - 1), skip=True)
                    free_pt[pi_] = [t_]
                    return t_

                for m in range(21):
                    nk = 128 if m < 20 else 16
                    pcs = pieces[m] if m < 20 else [(0, 512), (512, 1024), (1024, 1536), (1536, 2048)]
                    for (q0, q1) in pcs:
                        nq = q1 - q0
                        sb_ = (3, 4, 7)[ucnt % 3]
                        pS = B.ap[sb_]
                        pi = ucnt % 4
                        fi = ucnt % 2
                        ucnt += 1
                        w0 = B.waits(sb_) + ([le, lq, lv, z] if first else [])
                        first = False
                        if m < 20:
                            ts = mm(pS[:, 0:nq], NKA[b_][:, m * 128:(m + 1) * 128], NQA[b_][:, q0:q1], True, True, waits=w0, inc=True)
                            a1 = P.op("act", lambda e, fi=fi, nq=nq, pS=pS: e.activation(out=Ef[fi][:, 0:nq], in_=pS[:, 0:nq], func=AF.Exp, scale=NA_SCALE),
                                      waits=[ts] + free_ef[fi])
                            rel0 = (q0 // 64) - 2 * m + 11
                            d1 = P.op("dve", lambda e, fi=fi, pi=pi, nq=nq, rel0=rel0, b_=b_: e.tensor_tensor(
                                out=PTc[pi][:, 0:nq], in0=Ef[fi][:, 0:nq], in1=EB[b_][:, rel0 * 64:rel0 * 64 + nq], op=ALU.mult),
                                waits=[a1, ae] + free_pt[pi])
                            B.release(sb_, [a1])
                            free_ef[fi] = [d1]
                            ready = d1
                        else:
                            ts = mm(pS[0:16, 0:nq], NKA[b_][:, WTOK:WALL], NQA[b_][:, q0:q1], True, True, waits=w0, inc=True)
                            a1 = P.op("act", lambda e, pi=pi, nq=nq, pS=pS, h=h: e.activation(
                                out=PTc[pi][0:16, 0:nq], in_=pS[0:16, 0:nq], func=AF.Exp, scale=NA_SCALE, bias=metab_t[0:16, h:h + 1]),
                                waits=[ts] + free_pt[pi])
                            B.release(sb_, [a1])
                            ready = a1
                        pendc.append((m, q0, q1, pi, ready, nk))
                        if len(pendc) > 2:
                            pv_last = issue_pv_c(pendc.pop(0))
                while pendc:
                    pv_last = issue_pv_c(pendc.pop(0))
                free_nk[b_] = [pv_last]
                free_eb[b_] = [pv_last]
                ev = None
                for u in range(16):
                    a = acc_ap(u)
                    r = P.op("dve", lambda e, a=a, u=u: e.reciprocal(out=rdn[:, u:u + 1], in_=a[:, 64:65]), waits=[pv_last] + (free_yp if (u == 0 and h % 2 == 0) else []))
                    ev = P.op("dve", lambda e, a=a, u=u, h=h: e.scalar_tensor_tensor(
                        out=ypair[:, u, (h % 2) * 64:(h % 2) * 64 + 64], in0=a[:, 0:64], scalar=rdn[:, u:u + 1],
                        in1=GNall[:, u, h * 64:(h + 1) * 64], op0=ALU.mult, op1=ALU.mult), waits=[r, lg])
                for k in range(3):
                    B.release(accb[k], [ev])
                if h % 2 == 1:
                    hp = h // 2
                    tt = None
                    evs = []
                    for u4 in range(4):
                        q = 5 + (u4 % 2)
                        for k in range(4):
                            u = u4 * 4 + k
                            tt = mm(B.ap[q][:, k * 128:(k + 1) * 128], ypair[:, u, :], ident[:, :], True, True,
                                    waits=([ev] + B.waits(q)) if k == 0 else (), inc=(k == 3))
                        e2 = P.op("act", lambda e, q=q, u4=u4: e.activation(out=YTst[:, u4 * 512:(u4 + 1) * 512], in_=B.ap[q][:, :], func=AF.Copy),
                                  waits=[tt] + (free_yst if u4 == 0 else []))
                        B.release(q, [e2])
                        evs.append(e2)
                    free_yp = [tt]
                    s = P.dma("pool", "st2", lambda e, hp=hp: e.dma_start(out=YT_s[1024 + hp * 128:1024 + (hp + 1) * 128, :], in_=YTst[:, :]), waits=evs)
                    free_yst = [s]
            P.barrier()

        if stop_after not in ("A", "B", "C") and not (stop_after or "").startswith("Bn"):
            ab.reset(); af.reset()
            KTh = ab.take(TK)
            kpeT = ab.take(TK)
            Vh = ab.take(NT * 129).rearrange("p (j e) -> p j e", j=NT)
            Qn = [ab.take(OWN) for _ in range(2)]
            Qp = [ab.take(OWN) for _ in range(2)]
            GMh = [ab.take(16 * 128).rearrange("p (j f) -> p j f", j=16) for _ in range(2)]
            PTd = [ab.take(512) for _ in range(4)]
            ybf = [ab.take(128) for _ in range(2)]
            YTd = [ab.take(1024) for _ in range(2)]
            rdd = af.take(16)
            lp = P.dma("sp", "m0", lambda e: e.dma_start(out=kpeT[0:64, :], in_=KPE_s))
            zp = P.op("pool", lambda e: e.memset(kpeT[64:128, :], 0.0))
            for hb_ in range(2):
                zp = P.op("pool", lambda e, hb_=hb_: e.memset(Qp[hb_][64:128, :], 0.0))
            NGK = 8
            gtiles = [list(range(16 * g, 16 * g + 16)) for g in range(NGK)]
            gtiles[7].append(128)
            accb = [0, 1, 2]

            def accd(i):
                return B.ap[accb[i // 3]][:, (i % 3) * 129:(i % 3) * 129 + 129]

            kv_free = [[] for _ in range(NGK)]
            kv_ld = [None] * NGK
            q_free = [[], []]
            free_ptd = [[], [], [], []]
            free_yb = [[], []]
            free_ytd = [[], []]
            gm_free = [[], []]
            GMv = GM_s.rearrange("(j p) f -> p j f", p=128)
            ucnt = 0
            pcount = 0
            for h in range(H):
                hb = h % 2
                lqn = P.dma("sp", "lD%d" % hb, lambda e, hb=hb, h=h: e.dma_start(out=Qn[hb][:, :], in_=QT_s[h]), waits=q_free[hb])
                lqp = P.dma("sp", "lD%d" % hb, lambda e, hb=hb, h=h: e.dma_start(out=Qp[hb][0:64, :], in_=QPT_s[h]))
                lgm = P.dma("sp", "lD%d" % hb, lambda e, hb=hb, h=h: e.dma_start(out=GMh[hb][:, :, :], in_=GMv[:, :, h * 128:(h + 1) * 128]), waits=gm_free[hb])
                for g in range(NGK):
                    ta, tb = gtiles[g][0], gtiles[g][-1] + 1
                    t0, t1 = ta * 128, min(tb * 128, TK)
                    P.dma("sp", "kv%d" % g, lambda e, h=h, t0=t0, t1=t1: e.dma_start(out=KTh[:, t0:t1], in_=KT_s[h, :, t0:t1]), waits=kv_free[g])
                    kv_ld[g] = P.dma("sp", "kv%d" % g, lambda e, h=h, ta=ta, tb=tb: e.dma_start(out=Vh[:, ta:tb, :], in_=V_s[h, :, ta:tb, :]))
                for qc in range(2):
                    z = None
                    for k in range(3):
                        z = P.op("dve", lambda e, k=k: e.memset(B.ap[accb[k]][:, 0:387], 0.0), waits=B.waits(accb[k]))
                    units = [(j, s) for j in range(NT) for s in range(2)]
                    pend = []

                    def issue_pv(item):
                        (j, s, pi, ex, nk) = item
                        t = None
                        for i in range(4):
                            t = mm(accd(4 * s + i), PTd[pi][0:nk, i * 128:(i + 1) * 128], Vh[0:nk, j, :], False, False,
                                   waits=[ex, z] if i == 0 else (), inc=(i == 3), skip=True)
                        free_ptd[pi] = [t]
                        if qc == 1 and s == 1 and (j % 16 == 15 or j == 128) and not (j == 127):
                            kv_free[min(j // 16, 7)] = [t]
                        return t

                    pv_last = None
                    for (j, s) in units:
                        nk = 128 if j < 128 else 16
                        g = min(j // 16, 7)
                        sb_ = 3 + (ucnt % 3)
                        pi = ucnt % 4
                        ucnt += 1
                        pS = B.ap[sb_]
                        q0 = qc * 1024 + s * 512
                        w0 = B.waits(sb_)
                        if s == 0 and (j % 16 == 0) and j < 128:
                            w0 = w0 + [kv_ld[g]]
                        if j == 0 and s == 0:
                            w0 = w0 + [lgm, lp, zp]
                        mm(pS[0:nk, :], KTh[:, j * 128:j * 128 + nk], Qn[hb][:, q0:q0 + 512], True, False, waits=w0)
                        ts = mm(pS[0:nk, :], kpeT[:, j * 128:j * 128 + nk], Qp[hb][:, q0:q0 + 512], False, True, inc=True)
                        ex = P.op("act", lambda e, pi=pi, nk=nk, pS=pS: e.activation(out=PTd[pi][0:nk, :], in_=pS[0:nk, :], func=AF.Exp, scale=MLA_SCALE),
                                  waits=[ts] + free_ptd[pi])
                        B.release(sb_, [ex])
                        pend.append((j, s, pi, ex, nk))
                        if len(pend) > 2:
                            pv_last = issue_pv(pend.pop(0))
                    while pend:
                        pv_last = issue_pv(pend.pop(0))
                    if qc == 1:
                        q_free[hb] = [pv_last]
                    for i in range(8):
                        a = accd(i)
                        u = qc * 8 + i
                        yb = ybf[i % 2]
                        r = P.op("dve", lambda e, a=a, i=i: e.reciprocal(out=rdd[:, i:i + 1], in_=a[:, 128:129]), waits=[pv_last])
                        ev = P.op("dve", lambda e, a=a, i=i, u=u, yb=yb, hb=hb: e.scalar_tensor_tensor(
                            out=yb[:, :], in0=a[:, 0:128], scalar=rdd[:, i:i + 1], in1=GMh[hb][:, u, :], op0=ALU.mult, op1=ALU.mult),
                            waits=[r, lgm] + free_yb[i % 2])
                        q = 6 + (i // 4) % 2
                        if i % 4 == 0:
                            wq_ = B.waits(q)
                        tt = mm(B.ap[q][:, (i % 4) * 128:(i % 4 + 1) * 128], yb[:, :], ident[:, :], True, True,
                                waits=[ev] + (wq_ if i % 4 == 0 else []), inc=True)
                        free_yb[i % 2] = [tt]
                        if i % 4 == 3:
                            yk = pcount % 2
                            e2 = P.op("act", lambda e, q=q, yk=yk, i=i: e.activation(
                                out=YTd[yk][:, (i // 4) * 512:(i // 4 + 1) * 512], in_=B.ap[q][:, :], func=AF.Copy),
                                waits=[tt] + (free_ytd[yk] if i == 3 else []))
                            B.release(q, [e2])
                            if i == 7:
                                s_ = P.dma("pool", "sD%d" % yk, lambda e, yk=yk, h=h, qc=qc: e.dma_start(
                                    out=YT_s[h * 128:(h + 1) * 128, qc * 1024:(qc + 1) * 1024], in_=YTd[yk][:, :]), waits=[e2])
                                free_ytd[yk] = [s_]
                    for k in range(3):
                        B.release(accb[k], [ev])
                    if qc == 1:
                        gm_free[hb] = [ev]
                    pcount += 1
            P.barrier()

        if stop_after is None:
            ab.reset(); af.reset()
            wst = [af.take(2048), af.take(2048)]
            wst_state["free"] = [[], []]
            wo = ab.take(16 * 1024).rearrange("p (c f) -> p c f", c=16)
            yT = ab.take(16 * OWN).rearrange("p (c t) -> p c t", c=16)
            xo = [af.take(1024) for _ in range(2)]
            oo = [af.take(1024) for _ in range(2)]
            ly = P.dma("sp", "m1", lambda e: e.dma_start(out=yT[:, :, :], in_=YT_s.rearrange("(c p) t -> p c t", p=128)))
            tw = load_w(wo, w_out, 16, 0, 1024, None, wst)
            free_xo = [[], []]
            free_oo = [[], []]
            fin = None
            for ti in range(16):
                tp = ti % 2
                lx = P.dma("sp", "lE%d" % tp, lambda e, tp=tp, ti=ti: e.dma_start(out=xo[tp][:, :], in_=xown[ti * 128:(ti + 1) * 128, :]), waits=free_xo[tp])
                for half in range(2):
                    q = (ti * 2 + half) % 8
                    t = None
                    for c in range(16):
                        t = mm(B.ap[q][:, :], yT[:, c, ti * 128:(ti + 1) * 128], wo[:, c, half * 512:(half + 1) * 512], c == 0, c == 15,
                               waits=([ly, tw] + B.waits(q)) if c == 0 else (), inc=(c == 15))
                    a = P.op("dve", lambda e, q=q, tp=tp, half=half: e.tensor_tensor(
                        out=oo[tp][:, half * 512:(half + 1) * 512], in0=B.ap[q][:, :], in1=xo[tp][:, half * 512:(half + 1) * 512], op=ALU.add),
                        waits=[t, lx] + (free_oo[tp] if half == 0 else []))
                    B.release(q, [a])
                free_xo[tp] = [a]
                fin = P.dma("pool", "fin%d" % tp, lambda e, tp=tp, ti=ti: e.dma_start(out=out[ti * 128:(ti + 1) * 128, :], in_=oo[tp][:, :]), waits=[a])
                free_oo[tp] = [fin]

        fin_waits = [(s, P.cnt[s]) for s in P.dma_sems if s in P.cnt]
        P._push("pool", lambda e: e.memset(identf[:, 0:1], 0.0), fin_waits + [(s, v) for s, v in P.cnt.items() if s in Prog.CE], None, 1)

        for nme in P.cnt:
            P.sems[nme] = es.enter_context(nc.semaphore(nme))
        block = es.enter_context(nc.Block())

        @block.sync
        def _(e):
            P.replay("sp", e)

        @block.tensor
        def _(e):
            P.replay("pe", e)

        @block.scalar
        def _(e):
            P.replay("act", e)

        @block.vector
        def _(e):
            P.replay("dve", e)

        @block.gpsimd
        def _(e):
            P.replay("pool", e)
    return nc


def _rope_table(pos):
    inv_freq = (10000.0 ** (-(np.arange(0, 64, 2, dtype=np.float32) / 64))).astype(np.float32)
    ang = pos.astype(np.float32)[:, None] * inv_freq[None, :]
    c, s = np.cos(ang).astype(np.float32), np.sin(ang).astype(np.float32)
    return np.ascontiguousarray(np.concatenate([c, c, -s, s], axis=1))


def prepare_inputs(x, meta_tokens, norm_w, w_in, q_lat_norm_w, kv_lat_norm_w, w_uq, w_ukv,
                   mla_qn_w, mla_qpe_w, mla_kn_w, mla_kpe_w, na_q_norm_w, na_k_norm_w,
                   na_rel_bias, na_meta_bias, w_out):
    f = lambda a: np.ascontiguousarray(np.asarray(a, dtype=np.float32))
    x = f(x)[0]
    meta = f(meta_tokens)
    xall = np.concatenate([x, meta], axis=0)
    xT = np.ascontiguousarray(xall.T)
    posk = np.concatenate([np.arange(SEQ) + NMETA, np.arange(NMETA)])
    csk = _rope_table(posk)
    rb = f(na_rel_bias)[0]
    cols = np.arange(GRID_W)
    c0 = np.clip(cols - 8, 0, GRID_W - 16)
    eb = np.full((NH, 2, 64, 16, 64), NEG, np.float32)
    for jj in range(2):
        for irel in range(16):
            dr = jj + 7 - irel
            if not (-7 <= dr <= 7):
                continue
            for cq in range(64):
                ck = np.arange(c0[cq], c0[cq] + 16)
                eb[:, jj, ck, irel, cq] = rb[:, dr + 7, ck - cq + 15]
    eb = np.ascontiguousarray(eb.reshape(NH, 128, 1024))
    konehot = np.zeros((WROWS, WTOK), np.float32)
    for j in range(WROWS):
        konehot[j, j * 64:(j + 1) * 64] = 1.0
    shared = {
        "xT": xT, "w_in": f(w_in)[0], "w_uq": f(w_uq)[0], "w_ukv": f(w_ukv)[0], "w_out": f(w_out)[0],
        "normw": np.ascontiguousarray(f(norm_w)[0].reshape(8, 128).T),
        "qlw": np.ascontiguousarray(f(q_lat_norm_w)[0].reshape(2, 128).T),
        "kvlw": np.ascontiguousarray(f(kv_lat_norm_w)[0].reshape(2, 128).T),
        "qnw": f(mla_qn_w)[0][None, :], "knw": f(mla_kn_w)[0][None, :],
        "qpew": f(mla_qpe_w)[0][None, :], "kpew": f(mla_kpe_w)[0][None, :],
        "naqw": f(na_q_norm_w)[0][None, :], "nakw": f(na_k_norm_w)[0][None, :],
        "csk": csk, "ebias": eb, "konehot": konehot,
        "metab": np.ascontiguousarray(f(na_meta_bias)[0].T),
    }
    in_maps = []
    for c in range(NCORES):
        xw = np.zeros((WALL, D), np.float32)
        for j in range(WROWS):
            gr = 32 * c - 4 + j
            if 0 <= gr < ROWS:
                xw[j * 64:(j + 1) * 64] = x[gr * 64:(gr + 1) * 64]
        xw[WTOK:] = meta
        qm = np.zeros((WROWS, OWN), np.float32)
        for i in range(32):
            for j in range(WROWS):
                if not na_valid(c, i, j):
                    qm[j, i * 64:(i + 1) * 64] = NEG
        m = dict(shared)
        m["xTw"] = np.ascontiguousarray(xw.T)
        m["xown"] = np.ascontiguousarray(x[c * OWN:(c + 1) * OWN])
        m["csq"] = _rope_table(np.arange(c * OWN, (c + 1) * OWN) + NMETA)
        m["qmask"] = qm
        in_maps.append(m)
    return in_maps


def kernel(**inputs):
    in_maps = prepare_inputs(**inputs)
    nc = build_program()
    res = run_bass_kernel_spmd(nc, in_maps, core_ids=list(range(NCORES)))
    outs = [np.asarray(r["out"], dtype=np.float32) for r in res.results]
    return np.concatenate(outs, axis=0)[None, :, :]
```

```python
import numpy as np
from contextlib import ExitStack
import concourse.bass as bass
import concourse.mybir as mybir
from concourse.bass_utils import run_bass_kernel_spmd

F32 = mybir.dt.float32
BF16 = mybir.dt.bfloat16
AF = mybir.ActivationFunctionType
ALU = mybir.AluOpType
AX = mybir.AxisListType

NCORES = 8
D = 1024
SEQ = 16384
NMETA = 16
TK = SEQ + NMETA
NT = 129
OWN = 2048
WROWS = 40
WTOK = WROWS * 64
WALL = WTOK + NMETA
H = 8
NH = 16
EPS = 1e-6
GRID_W = 64
ROWS = 256
NEG = -30000.0
MLA_SCALE = 192 ** -0.5
NA_SCALE = 0.125


class Prog:
    CE = ("pe", "act", "dve", "pool")

    def __init__(self):
        self.ops = {e: [] for e in ("pe", "act", "dve", "pool", "sp")}
        self.sems = {}
        self.cnt = {}
        self.waited = {e: {} for e in self.ops}
        self.pending = {e: [] for e in self.ops}
        self.dma_sems = []

    def _push(self, eng, fn, waits, inc, n):
        ws = list(self.pending[eng]) + [w for w in waits if w is not None]
        self.pending[eng] = []
        mx = {}
        for (s, v) in ws:
            if v > mx.get(s, 0):
                mx[s] = v
        fw = []
        for s, v in mx.items():
            if self.waited[eng].get(s, 0) >= v:
                continue
            self.waited[eng][s] = v
            fw.append((s, v))
        tok = None
        if inc is not None:
            self.cnt[inc] = self.cnt.get(inc, 0) + n
            tok = (inc, self.cnt[inc])
        self.ops[eng].append((fn, tuple(fw), inc, n))
        return tok

    def op(self, eng, fn, waits=(), inc=True):
        return self._push(eng, fn, waits, eng if inc else None, 1)

    def dma(self, eng, sem, fn, waits=()):
        if sem not in self.dma_sems:
            self.dma_sems.append(sem)
        return self._push(eng, fn, waits, sem, 16)

    def barrier(self):
        toks = [(s, v) for s, v in self.cnt.items()]
        for e in self.pending:
            self.pending[e] = list(toks)

    def replay(self, eng, e):
        for fn, waits, inc, n in self.ops[eng]:
            for (s, v) in waits:
                e.wait_ge(self.sems[s], v)
            ins = fn(e)
            if inc is not None:
                ins.then_inc(self.sems[inc], n)


class Arena:
    def __init__(self, t, cols):
        self.t, self.cols, self.off = t, cols, 0

    def reset(self):
        self.off = 0

    def take(self, n):
        a = self.off
        self.off += (n + 7) // 8 * 8
        assert self.off <= self.cols, (self.off, self.cols)
        return self.t[:, a:a + n]


class Banks:
    def __init__(self, aps):
        self.ap = aps
        self.free = [[] for _ in aps]

    def waits(self, i):
        return list(self.free[i])

    def release(self, i, toks):
        self.free[i] = [t for t in toks if t is not None]


def na_valid(c, i, j):
    r = 32 * c + i
    kr = 32 * c - 4 + j
    r0 = min(max(r - 4, 0), ROWS - 8)
    return (0 <= kr < ROWS) and (r0 <= kr < r0 + 8)


def na_pieces():
    out = []
    for m in range(20):
        rows = [i for i in range(32) if any(na_valid(c, i, j) for c in range(NCORES) for j in (2 * m, 2 * m + 1))]
        lo, hi = min(rows) & ~1, max(rows) | 1
        assert 0 <= lo - 2 * m + 11 and hi - 2 * m + 11 <= 15, (m, lo, hi)
        pcs = []
        r = lo
        while r <= hi:
            r2 = min(r + 8, hi + 1)
            pcs.append((r * 64, r2 * 64))
            r = r2
        out.append(pcs)
    return out


def build_program(dbg=False, stop_after=None):
    nc = bass.Bass("TRN2", target_bir_lowering=False)

    def din(name, shape, dt=F32):
        return nc.dram_tensor(name, list(shape), dt, kind="ExternalInput").ap()

    xT = din("xT", [D, TK])
    xTw = din("xTw", [D, WALL])
    xown = din("xown", [OWN, D])
    w_in = din("w_in", [D, 5696])
    w_uq = din("w_uq", [256, 1536])
    w_ukv = din("w_ukv", [256, 2048])
    w_out = din("w_out", [2048, D])
    normw = din("normw", [128, 8])
    qlw = din("qlw", [128, 2])
    kvlw = din("kvlw", [128, 2])
    qnw = din("qnw", [1, 128])
    knw = din("knw", [1, 128])
    qpew = din("qpew", [1, 64])
    kpew = din("kpew", [1, 64])
    naqw = din("naqw", [1, 64])
    nakw = din("nakw", [1, 64])
    csk = din("csk", [TK, 128])
    csq = din("csq", [OWN, 128])
    ebias = din("ebias", [NH, 128, 1024])
    qmask = din("qmask", [WROWS, OWN])
    konehot = din("konehot", [WROWS, WTOK])
    metab = din("metab", [NMETA, NH])
    out = nc.dram_tensor("out", [OWN, D], F32, kind="ExternalOutput").ap()

    skind = "ExternalOutput" if dbg else "Internal"

    def scratch(name, shape, dt=BF16):
        return nc.dram_tensor(name, list(shape), dt, kind=skind).ap()

    KT_s = scratch("KT_s", [H, 128, TK])
    KPE_s = scratch("KPE_s", [64, TK])
    V_s = scratch("V_s", [H, 128, NT, 129])
    QT_s = scratch("QT_s", [H, 128, OWN])
    QPT_s = scratch("QPT_s", [H, 64, OWN])
    GM_s = scratch("GM_s", [OWN, 1024])
    GN_s = scratch("GN_s", [OWN, 1024])
    NQT_s = scratch("NQT_s", [NH, 64, OWN])
    NKT_s = scratch("NKT_s", [NH, 64, WALL])
    NV_s = scratch("NV_s", [21 * 128, NH, 65])
    YT_s = scratch("YT_s", [2048, OWN])

    P = Prog()
    with ExitStack() as es:
        AB_COLS = 73728
        AF_COLS = 14848
        abt = es.enter_context(nc.sbuf_tensor("arena_bf", [128, AB_COLS], BF16))
        aft = es.enter_context(nc.sbuf_tensor("arena_f", [128, AF_COLS], F32))
        ident = es.enter_context(nc.sbuf_tensor("ident", [128, 128], BF16))
        identf = es.enter_context(nc.sbuf_tensor("identf", [128, 128], F32))
        ones = es.enter_context(nc.sbuf_tensor("ones", [128, 8], BF16))
        consts = es.enter_context(nc.sbuf_tensor("consts", [128, 1024], F32))
        pb = [es.enter_context(nc.psum_tensor("pb%d" % i, [128, 512], F32)) for i in range(8)]
        ab = Arena(abt, AB_COLS)
        af = Arena(aft, AF_COLS)
        B = Banks([p[:, :] for p in pb])

        qkw_t = consts[:, 0:128]
        qpew_t = consts[:, 384:448]
        kpew_t = consts[:, 448:512]
        naqw_t = consts[:, 512:576]
        nakw_t = consts[:, 576:640]
        normw_t = consts[:, 640:648]
        qlw_t = consts[:, 648:650]
        kvlw_t = consts[:, 650:652]
        metab_t = consts[:, 656:672]

        P.op("pool", lambda e: e.memset(identf[:], 0.0))
        i1 = P.op("pool", lambda e: e.affine_select(out=identf[:], in_=identf[:], pattern=[[-1, 128]],
                                                    compare_op=ALU.not_equal, fill=1.0, base=0, channel_multiplier=1))
        P.op("dve", lambda e: e.tensor_copy(out=ident[:], in_=identf[:]), waits=[i1])
        P.op("dve", lambda e: e.memset(ones[:], 1.0))
        lt = None
        for (dst, src) in [(consts[:, 128:256], qnw), (consts[:, 256:384], knw), (qpew_t, qpew), (kpew_t, kpew),
                           (naqw_t, naqw), (nakw_t, nakw)]:
            lt = P.dma("sp", "ld0", lambda e, dst=dst, src=src: e.dma_start(out=dst, in_=src.partition_broadcast(128)))
        for (dst, src) in [(normw_t, normw), (qlw_t, qlw), (kvlw_t, kvlw)]:
            lt = P.dma("sp", "ld0", lambda e, dst=dst, src=src: e.dma_start(out=dst, in_=src))
        lt = P.dma("sp", "ld0", lambda e: e.dma_start(out=metab_t[0:16, :], in_=metab))
        P.op("dve", lambda e: e.tensor_tensor(out=qkw_t, in0=consts[:, 128:256], in1=consts[:, 256:384], op=ALU.mult), waits=[lt])
        P.barrier()

        wst_state = {"k": 0, "free": [[], []]}

        def load_w(dst3, src2d, nch, c0, c1, rowscale, wst):
            last = None
            for ch in range(nch):
                for a in range(c0, c1, 2048):
                    b_ = min(a + 2048, c1)
                    k = wst_state["k"] % 2
                    wst_state["k"] += 1
                    stg = wst[k][:, 0:b_ - a]
                    t = P.dma("sp", "lw%d" % k, lambda e, stg=stg, ch=ch, a=a, b_=b_: e.dma_start(
                        out=stg, in_=src2d[ch * 128:(ch + 1) * 128, a:b_]), waits=wst_state["free"][k])
                    dsl = dst3[:, ch, a - c0:b_ - c0]
                    if rowscale is not None:
                        if k == 0:
                            t2 = P.op("dve", lambda e, dsl=dsl, stg=stg, ch=ch: e.tensor_scalar(
                                out=dsl, in0=stg, scalar1=rowscale[:, ch:ch + 1], scalar2=None, op0=ALU.mult), waits=[t])
                        else:
                            t2 = P.op("act", lambda e, dsl=dsl, stg=stg, ch=ch: e.activation(
                                out=dsl, in_=stg, func=AF.Copy, scale=rowscale[:, ch:ch + 1]), waits=[t])
                    else:
                        if k == 0:
                            t2 = P.op("dve", lambda e, dsl=dsl, stg=stg: e.tensor_copy(out=dsl, in_=stg), waits=[t])
                        else:
                            t2 = P.op("act", lambda e, dsl=dsl, stg=stg: e.activation(out=dsl, in_=stg, func=AF.Copy), waits=[t])
                    wst_state["free"][k] = [t2]
                    last = t2
            return last

        def mm(out_ap, lhsT, rhs, start, stop, waits=(), inc=False, skip=False):
            if skip:
                return P.op("pe", lambda e: e.matmul(out_ap, lhsT=lhsT, rhs=rhs, start=start, stop=stop, skip_group_check=True),
                            waits=waits, inc=inc)
            return P.op("pe", lambda e: e.matmul(out_ap, lhsT=lhsT, rhs=rhs, start=start, stop=stop), waits=waits, inc=inc)

        scr_free = {}

        def run(gen):
            try:
                while True:
                    next(gen)
            except StopIteration as ex_:
                return ex_.value

        def interleave(gens):
            gens = list(gens)
            while gens:
                for gi in list(gens):
                    try:
                        next(gi)
                    except StopIteration:
                        gens.remove(gi)

        def headnorm(*a, **k):
            return run(headnorm_g(*a, **k))

        def headnorm_g(src3, n, nh, hd, outp, sq, tmp, stt, pre, wtile, rope_cs, waits, tmp2=None, key=None):
            waits = list(waits) + scr_free.get(key, [])
            ss = stt[:n, 0:nh]
            sd = stt[:n, nh:2 * nh]
            rr = stt[:n, 2 * nh:3 * nh]
            a = P.op("act", lambda e: e.activation(out=sq, in_=src3, func=AF.Square), waits=waits)
            yield
            d = P.op("dve", lambda e: e.tensor_reduce(out=ss, in_=sq, axis=AX.X, op=ALU.add), waits=[a])
            if pre is not None:
                d = P.op("dve", lambda e: e.tensor_scalar(out=ss, in0=ss, scalar1=pre[1], scalar2=None, op0=ALU.mult), waits=[d])
            yield
            a = P.op("act", lambda e: e.activation(out=sd, in_=ss, func=AF.Sqrt, bias=EPS, scale=1.0 / hd), waits=[d])
            yield
            d = P.op("dve", lambda e: e.reciprocal(out=rr, in_=sd), waits=[a])
            if pre is not None:
                d = P.op("dve", lambda e: e.tensor_scalar(out=rr, in0=rr, scalar1=pre[0], scalar2=None, op0=ALU.mult), waits=[d])
            rb = rr.unsqueeze(2).to_broadcast([n, nh, hd])
            if wtile is None and rope_cs is None:
                tok = P.op("dve", lambda e: e.tensor_tensor(out=outp, in0=src3, in1=rb, op=ALU.mult), waits=[d])
                scr_free[key] = [tok]
                yield
                return tok
            d = P.op("dve", lambda e: e.tensor_tensor(out=tmp, in0=src3, in1=rb, op=ALU.mult), waits=[d])
            wb = wtile[:n, :].unsqueeze(1).to_broadcast([n, nh, hd])
            if rope_cs is None:
                tok = P.op("dve", lambda e: e.tensor_tensor(out=outp, in0=tmp, in1=wb, op=ALU.mult), waits=[d])
                scr_free[key] = [tok]
                yield
                return tok
            d = P.op("dve", lambda e: e.tensor_tensor(out=tmp, in0=tmp, in1=wb, op=ALU.mult), waits=[d])
            hh = hd // 2
            cc = rope_cs[:n, 0:hd].unsqueeze(1).to_broadcast([n, nh, hd])
            nsin = rope_cs[:n, hd:hd + hh].unsqueeze(1).to_broadcast([n, nh, hh])
            psin = rope_cs[:n, hd + hh:2 * hd].unsqueeze(1).to_broadcast([n, nh, hh])
            d1 = P.op("dve", lambda e: e.tensor_tensor(out=tmp2[:, :, 0:hh], in0=tmp[:, :, hh:hd], in1=nsin, op=ALU.mult), waits=[d])
            d2 = P.op("dve", lambda e: e.tensor_tensor(out=tmp2[:, :, hh:hd], in0=tmp[:, :, 0:hh], in1=psin, op=ALU.mult), waits=[d])
            d3 = P.op("dve", lambda e: e.tensor_tensor(out=tmp, in0=tmp, in1=cc, op=ALU.mult), waits=[d2])
            tok = P.op("dve", lambda e: e.tensor_tensor(out=outp, in0=tmp, in1=tmp2, op=ALU.add), waits=[d3])
            scr_free[key] = [tok]
            yield
            return tok

        def rstd_from_ss(*a, **k):
            return run(rstd_from_ss_g(*a, **k))

        def rstd_from_ss_g(ss_ps, n, stt, waits):
            a = P.op("act", lambda e: e.activation(out=stt[:n, 0:1], in_=ss_ps, func=AF.Sqrt, bias=EPS, scale=1.0 / D), waits=waits)
            yield
            d = P.op("dve", lambda e: e.reciprocal(out=stt[:n, 1:2], in_=stt[:n, 0:1]), waits=[a])
            d = P.op("dve", lambda e: e.tensor_tensor(out=stt[:n, 2:3], in0=stt[:n, 1:2], in1=stt[:n, 1:2], op=ALU.mult), waits=[d])
            yield
            return (stt[:n, 1:2], stt[:n, 2:3]), d

        ab.reset(); af.reset()
        wst = [af.take(2048), af.take(2048)]
        xt = [af.take(4096).rearrange("p (c t) -> p c t", c=8) for _ in range(2)]
        cs = [af.take(512).rearrange("p (j f) -> p j f", j=4) for _ in range(2)]
        stA = [af.take(16) for _ in range(2)]
        stK = [af.take(64) for _ in range(2)]
        sqA_ = [af.take(320) for _ in range(2)]
        tmpA_ = [af.take(64) for _ in range(2)]
        tmpB_ = [af.take(64) for _ in range(2)]
        wkv = ab.take(8 * 320).rearrange("p (c f) -> p c f", c=8)
        wukv = ab.take(2 * 2048).rearrange("p (c f) -> p c f", c=2)
        xb = [ab.take(4096).rearrange("p (c t) -> p c t", c=8) for _ in range(2)]
        xsq = [ab.take(4096).rearrange("p (c t) -> p c t", c=8) for _ in range(2)]
        cn = [ab.take(256) for _ in range(2)]
        kpe_b = [ab.take(64) for _ in range(2)]
        cTt = [ab.take(256).rearrange("p (c t) -> p c t", c=2) for _ in range(2)]
        khat = [ab.take(1024).rearrange("p (h d) -> p h d", h=8) for _ in range(2)]
        KTst = [ab.take(4096).rearrange("p (h t) -> p h t", h=8) for _ in range(2)]
        Vst = [ab.take(8 * 4 * 129).rearrange("p (h j e) -> p h j e", h=8, j=4) for _ in range(2)]
        kpest = [ab.take(512) for _ in range(2)]

        load_w(wkv, w_in, 8, 256, 576, normw_t, wst)
        tw = load_w(wukv, w_ukv, 2, 0, 2048, kvlw_t, wst)
        for b_ in range(2):
            P.op("pool", lambda e, b_=b_: e.memset(Vst[b_][:, :, :, :], 1.0))

        xTv = xT.rearrange("(c p) t -> p c t", p=128)
        KTv = KT_s.rearrange("h d t -> d h t")
        Vv = V_s.rearrange("h p j e -> p h j e")
        NGA = 33
        gfree_x = [[], []]
        gfree_xb = [[], []]
        gfree_st = [[], []]
        tcount = 0
        for g in range(NGA):
            b_ = g % 2
            t0 = g * 512
            ng = 512 if g < 32 else 16
            ntile = 4 if g < 32 else 1
            l1 = P.dma("sp", "lA%d" % b_, lambda e, b_=b_, t0=t0, ng=ng: e.dma_start(out=xt[b_][:, :, 0:ng], in_=xTv[:, :, t0:t0 + ng]),
                       waits=gfree_x[b_])
            if g < 32:
                l2 = P.dma("sp", "lA%d" % b_, lambda e, b_=b_, t0=t0: e.dma_start(
                    out=cs[b_][:, :, :], in_=csk[t0:t0 + 512, :].rearrange("(j p) f -> p j f", p=128)))
            else:
                l2 = P.dma("sp", "lA%d" % b_, lambda e, b_=b_, t0=t0: e.dma_start(out=cs[b_][0:16, 0, :], in_=csk[t0:t0 + 16, :]))
            c1 = P.op("act", lambda e, b_=b_, ng=ng: e.activation(out=xb[b_][:, :, 0:ng], in_=xt[b_][:, :, 0:ng], func=AF.Copy),
                      waits=[l2] + gfree_xb[b_])
            c2 = P.op("pool", lambda e, b_=b_, ng=ng: e.tensor_tensor(out=xsq[b_][:, :, 0:ng], in0=xt[b_][:, :, 0:ng],
                                                                      in1=xt[b_][:, :, 0:ng], op=ALU.mult), waits=[l2] + gfree_xb[b_])
            res = {"last_pe": None, "last_rope": None, "stage": []}

            def tileA(j, n, tp, b_=b_, l2=l2, c1=c1, c2=c2, res=res):
                sqA, tmpA, tmpB = sqA_[tp], tmpA_[tp], tmpB_[tp]
                sl = slice(j * 128, j * 128 + n)
                pk = B.ap[tp]
                for c in range(8):
                    mm(pk[:n, 0:320], xb[b_][:, c, sl], wkv[:, c, :], c == 0, c == 7, waits=([c1] + B.waits(tp)) if c == 0 else ())
                for c in range(8):
                    tk = mm(pk[:n, 384:385], xsq[b_][:, c, sl], ones[:, 0:1], c == 0, c == 7, waits=[c2] if c == 0 else (), inc=(c == 7))
                yield
                st = stA[tp]
                (r1, r1sq), d = yield from rstd_from_ss_g(pk[:n, 384:385], n, st, [tk])
                dc = yield from headnorm_g(pk[:n, 0:256].rearrange("p (h d) -> p h d", h=1), n, 1, 256, cn[tp][:n, :].rearrange("p (h d) -> p h d", h=1),
                                           sqA[:n, 0:256].rearrange("p (h d) -> p h d", h=1), None, st[:, 4:8], (r1, r1sq), None, None, [tk, d],
                                           key="s0_%d" % tp)
                dk = yield from headnorm_g(pk[:n, 256:320].rearrange("p (h d) -> p h d", h=1), n, 1, 64, kpe_b[tp][:n, :].rearrange("p (h d) -> p h d", h=1),
                                           sqA[:n, 256:320].rearrange("p (h d) -> p h d", h=1), tmpA[:n, :].rearrange("p (h d) -> p h d", h=1),
                                           st[:, 8:12], (r1, r1sq), kpew_t, cs[b_][:, j, :], [tk, d, l2, dc],
                                           tmp2=tmpB[:n, :].rearrange("p (h d) -> p h d", h=1), key="s1_%d" % tp)
                B.release(tp, [dc, dk])
                res["last_rope"] = dk
                ptb = 2 if tp == 0 else 7
                pt = B.ap[ptb]
                mm(pt[:, 0:n], cn[tp][:n, 0:128], ident[:n, :n], True, True, waits=[dc, dk] + B.waits(ptb))
                mm(pt[:, 128:128 + n], cn[tp][:n, 128:256], ident[:n, :n], True, True)
                tt = mm(pt[0:64, 256:256 + n], kpe_b[tp][:n, 0:64], ident[:n, :n], True, True, inc=True)
                yield
                e1 = P.op("dve", lambda e: e.tensor_copy(
                    out=cTt[tp][:, :, 0:n], in_=pt[:, 0:256].rearrange("p (c t) -> p c t", c=2)[:, :, 0:n]), waits=[tt])
                yield
                e2 = P.op("act", lambda e: e.activation(out=kpest[b_][0:64, sl], in_=pt[0:64, 256:256 + n], func=AF.Copy),
                          waits=[tt, e1] + gfree_st[b_])
                B.release(ptb, [e1, e2])
                res["stage"].append(e2)
                stk = stK[tp]
                kh = khat[tp]
                t2 = None
                for gp in range(4):
                    q = 3 + tp
                    pu = B.ap[q]
                    mm(pu[:n, :], cTt[tp][:, 0, 0:n], wukv[:, 0, gp * 512:(gp + 1) * 512], True, False, waits=[e1] + B.waits(q))
                    tu = mm(pu[:n, :], cTt[tp][:, 1, 0:n], wukv[:, 1, gp * 512:(gp + 1) * 512], False, True, inc=True)
                    yield
                    pu4 = pu[:n, :].rearrange("p (h two d) -> p h two d", h=2, two=2)
                    dkk = yield from headnorm_g(pu4[:, :, 0, :], n, 2, 128, kh[:n, 2 * gp:2 * gp + 2, :], sqA[:n, 0:256].rearrange("p (h d) -> p h d", h=2),
                                                None, stk[:, 8 * gp:8 * gp + 8], None, None, None, [tu], key="s0_%d" % tp)
                    av = P.op("act", lambda e, gp=gp, pu4=pu4: e.activation(
                        out=Vst[b_][:n, 2 * gp:2 * gp + 2, j, 0:128], in_=pu4[:, :, 1, :], func=AF.Copy),
                        waits=[tu, dkk] + gfree_st[b_])
                    B.release(q, [dkk, av])
                    res["stage"].append(av)
                    q2 = 5 + tp
                    p2 = B.ap[q2]
                    mm(p2[:, 0:n], kh[:n, 2 * gp, :], ident[:n, :n], True, True, waits=[dkk] + B.waits(q2))
                    t2 = mm(p2[:, 128:128 + n], kh[:n, 2 * gp + 1, :], ident[:n, :n], True, True, inc=True)
                    yield
                    e3 = P.op("dve", lambda e, gp=gp, p2=p2: e.tensor_copy(
                        out=KTst[b_][:, 2 * gp:2 * gp + 2, sl], in_=p2[:, 0:256].rearrange("p (h t) -> p h t", h=2)[:, :, 0:n]),
                        waits=[t2] + gfree_st[b_])
                    B.release(q2, [e3])
                    res["stage"].append(e3)
                    yield
                res["last_pe"] = t2

            if g < 32:
                interleave([tileA(0, 128, 0), tileA(1, 128, 1)])
                interleave([tileA(2, 128, 0), tileA(3, 128, 1)])
            else:
                interleave([tileA(0, 16, 0)])
            last_pe = res["last_pe"]
            last_rope = res["last_rope"]
            stage_toks = res["stage"]
            gfree_x[b_] = [c1, c2, last_rope]
            gfree_xb[b_] = [last_pe]
            nrow = 128 if g < 32 else 16
            s1 = P.dma("pool", "sA%d" % b_, lambda e, b_=b_, t0=t0, ng=ng: e.dma_start(out=KTv[:, :, t0:t0 + ng], in_=KTst[b_][:, :, 0:ng]),
                       waits=stage_toks)
            s2 = P.dma("pool", "sA%d" % b_, lambda e, b_=b_, g=g, ntile=ntile, nrow=nrow: e.dma_start(
                out=Vv[0:nrow, :, 4 * g:4 * g + ntile, :], in_=Vst[b_][0:nrow, :, 0:ntile, :]))
            s3 = P.dma("pool", "sA%d" % b_, lambda e, b_=b_, t0=t0, ng=ng: e.dma_start(out=KPE_s[:, t0:t0 + ng], in_=kpest[b_][0:64, 0:ng]))
            gfree_st[b_] = [s3]
        P.barrier()

        if stop_after != "A":
            ab.reset(); af.reset()
            wst = [af.take(2048), af.take(2048)]
            wst_state["free"] = [[], []]
            xtB = [af.take(1024).rearrange("p (c t) -> p c t", c=8) for _ in range(2)]
            csB = [af.take(128) for _ in range(2)]
            stB = [af.take(16) for _ in range(2)]
            stH = [af.take(64) for _ in range(8)]
            sqB = [af.take(512) for _ in range(2)]
            tmB = [af.take(512) for _ in range(2)]
            tm2 = [af.take(512) for _ in range(2)]
            wq = ab.take(8 * 256).rearrange("p (c f) -> p c f", c=8)
            wgm = ab.take(8 * 1024).rearrange("p (c f) -> p c f", c=8)
            wnq = ab.take(8 * 1024).rearrange("p (c f) -> p c f", c=8)
            wnk = ab.take(8 * 1024).rearrange("p (c f) -> p c f", c=8)
            wnv = ab.take(8 * 1024).rearrange("p (c f) -> p c f", c=8)
            wgn = ab.take(8 * 1024).rearrange("p (c f) -> p c f", c=8)
            wuq = ab.take(2 * 1536).rearrange("p (c f) -> p c f", c=2)
            xbB = [ab.take(1024).rearrange("p (c t) -> p c t", c=8) for _ in range(2)]
            xsqB = [ab.take(1024).rearrange("p (c t) -> p c t", c=8) for _ in range(2)]
            nkh = [ab.take(1024) for _ in range(2)]
            nqh = nkh
            NKst = [ab.take(8 * 128).rearrange("p (h t) -> p h t", h=8) for _ in range(2)]
            NQst = [ab.take(8 * 128).rearrange("p (h t) -> p h t", h=8) for _ in range(2)]
            NVst = [ab.take(16 * 65).rearrange("p (h e) -> p h e", h=16) for _ in range(2)]
            qlb = [ab.take(256) for _ in range(2)]
            qlT = [ab.take(256).rearrange("p (c t) -> p c t", c=2) for _ in range(2)]
            qnb = [ab.take(1024).rearrange("p (h d) -> p h d", h=8) for _ in range(2)]
            qpb = [ab.take(512).rearrange("p (h d) -> p h d", h=8) for _ in range(2)]
            QnSt = [ab.take(8 * 128).rearrange("p (h t) -> p h t", h=8) for _ in range(2)]
            QpSt = [ab.take(4 * 128).rearrange("p (h t) -> p h t", h=4) for _ in range(2)]
            GMst = [ab.take(1024) for _ in range(2)]
            GNst = [ab.take(1024) for _ in range(2)]
            load_w(wq, w_in, 8, 0, 256, normw_t, wst)
            load_w(wgm, w_in, 8, 576, 1600, normw_t, wst)
            load_w(wnq, w_in, 8, 1600, 2624, normw_t, wst)
            load_w(wnk, w_in, 8, 2624, 3648, normw_t, wst)
            load_w(wnv, w_in, 8, 3648, 4672, normw_t, wst)
            load_w(wgn, w_in, 8, 4672, 5696, normw_t, wst)
            load_w(wuq, w_uq, 2, 0, 1536, qlw_t, wst)
            for b_ in range(2):
                P.op("pool", lambda e, b_=b_: e.memset(NVst[b_][:, :, :], 1.0))
            xTwv = xTw.rearrange("(c p) t -> p c t", p=128)
            NKTv = NKT_s.rearrange("(hp two) d t -> (two d) hp t", two=2)
            NQTv = NQT_s.rearrange("(hp two) d t -> (two d) hp t", two=2)
            QTnv = QT_s.rearrange("h d t -> d h t")
            QTpv = QPT_s.rearrange("(hp two) r t -> (two r) hp t", two=2)
            free_x = [[], []]
            free_xb = [[], []]
            free_st = [[], []]
            free_cs = [[], []]
            bk = {"i": 0}

            def nextbank():
                i = bk["i"] % 8
                bk["i"] += 1
                return i

            def proj(n, tp, wmat, col0, ncols, c1tok):
                q = nextbank()
                t = None
                for c in range(8):
                    t = mm(B.ap[q][:n, 0:ncols], xbB[tp][:, c, 0:n], wmat[:, c, col0:col0 + ncols], c == 0, c == 7,
                           waits=([c1tok] + B.waits(q)) if c == 0 else (), inc=(c == 7))
                return q, t

            def transposes_out(src_bf, n, nblk, stage, st_waits, src_waits):
                toks = []
                for k0 in range(0, nblk, 4):
                    q = nextbank()
                    kk = min(4, nblk - k0)
                    t = None
                    for k in range(kk):
                        t = mm(B.ap[q][:, k * 128:k * 128 + n], src_bf[:n, (k0 + k) * 128:(k0 + k + 1) * 128], ident[:n, :n], True, True,
                               waits=(B.waits(q) + list(src_waits)) if k == 0 else (), inc=(k == kk - 1))
                    ev = P.op("act", lambda e, q=q, k0=k0, kk=kk, n=n: e.activation(
                        out=stage[:, k0:k0 + kk, 0:n], in_=B.ap[q][:, 0:kk * 128].rearrange("p (h t) -> p h t", h=kk)[:, :, 0:n], func=AF.Copy),
                        waits=[t] + (st_waits if k0 == 0 else []))
                    B.release(q, [ev])
                    toks.append(ev)
                return toks

            nwt = int(stop_after[2:]) if (stop_after or "").startswith("Bn") else 21
            for wt in range(nwt):
                tp = wt % 2
                n = 128 if wt < 20 else 16
                own = 2 <= wt < 18
                ti = wt - 2
                tok0 = wt * 128
                l1 = P.dma("sp", "lB%d" % tp, lambda e, tp=tp, n=n, tok0=tok0: e.dma_start(out=xtB[tp][:, :, 0:n], in_=xTwv[:, :, tok0:tok0 + n]),
                           waits=free_x[tp])
                l2 = None
                lb = l1
                if own:
                    l2 = P.dma("sp", "lB%d" % tp, lambda e, tp=tp, ti=ti: e.dma_start(out=csB[tp][:, :], in_=csq[ti * 128:(ti + 1) * 128, :]),
                               waits=free_cs[tp])
                    lb = l2
                c1 = P.op("act", lambda e, tp=tp, n=n: e.activation(out=xbB[tp][:, :, 0:n], in_=xtB[tp][:, :, 0:n], func=AF.Copy),
                          waits=[lb] + free_xb[tp])
                c2 = P.op("pool", lambda e, tp=tp, n=n: e.tensor_tensor(out=xsqB[tp][:, :, 0:n], in0=xtB[tp][:, :, 0:n], in1=xtB[tp][:, :, 0:n],
                                                                        op=ALU.mult), waits=[lb] + free_xb[tp])
                qs = nextbank()
                tk = None
                for c in range(8):
                    tk = mm(B.ap[qs][:n, 0:1], xsqB[tp][:, c, 0:n], ones[:, 0:1], c == 0, c == 7, waits=([c2] + B.waits(qs)) if c == 0 else (),
                            inc=(c == 7))
                st = stB[tp]
                (r1, r1sq), d = rstd_from_ss(B.ap[qs][:n, 0:1], n, st, [tk])
                B.release(qs, [d])
                stage_toks = []
                for half in range(2):
                    q, t = proj(n, tp, wnk, half * 512, 512, c1)
                    dk = headnorm(B.ap[q][:n, :].rearrange("p (h d) -> p h d", h=8), n, 8, 64,
                                  nkh[tp][:n, half * 512:(half + 1) * 512].rearrange("p (h d) -> p h d", h=8),
                                  sqB[half][:n, :].rearrange("p (h d) -> p h d", h=8), tmB[half][:n, :].rearrange("p (h d) -> p h d", h=8),
                                  stH[half], (r1, r1sq), nakw_t, None, [t, d], key="h%d" % half)
                    B.release(q, [dk])
                evs = transposes_out(nkh[tp], n, 8, NKst[tp], free_st[tp], [dk])
                stage_toks += evs
                for half in range(2):
                    q, t = proj(n, tp, wnv, half * 512, 512, c1)
                    av = P.op("act", lambda e, q=q, n=n, tp=tp, half=half, r1=r1: e.activation(
                        out=NVst[tp][:n, half * 8:(half + 1) * 8, 0:64], in_=B.ap[q][:n, :].rearrange("p (h d) -> p h d", h=8),
                        func=AF.Copy, scale=r1), waits=[t, d] + free_st[tp])
                    B.release(q, [av])
                    stage_toks.append(av)
                last_pe_x = None
                if own:
                    q, t = proj(n, tp, wq, 0, 256, c1)
                    dq = headnorm(B.ap[q][:n, 0:256].rearrange("p (h d) -> p h d", h=1), n, 1, 256, qlb[tp][:n, :].rearrange("p (h d) -> p h d", h=1),
                                  sqB[0][:n, 0:256].rearrange("p (h d) -> p h d", h=1), None, stH[2], (r1, r1sq), None, None, [t, d], key="h0")
                    B.release(q, [dq])
                    q = nextbank()
                    mm(B.ap[q][:, 0:128], qlb[tp][:, 0:128], ident[:, :], True, True, waits=[dq] + B.waits(q))
                    t = mm(B.ap[q][:, 128:256], qlb[tp][:, 128:256], ident[:, :], True, True, inc=True)
                    eq = P.op("dve", lambda e, q=q, tp=tp: e.tensor_copy(out=qlT[tp][:, :, :], in_=B.ap[q][:, 0:256].rearrange("p (c t) -> p c t", c=2)),
                              waits=[t])
                    B.release(q, [eq])
                    for gq in range(4):
                        q = nextbank()
                        mm(B.ap[q][:, 0:384], qlT[tp][:, 0, :], wuq[:, 0, gq * 384:(gq + 1) * 384], True, False, waits=[eq] + B.waits(q))
                        t = mm(B.ap[q][:, 0:384], qlT[tp][:, 1, :], wuq[:, 1, gq * 384:(gq + 1) * 384], False, True, inc=True)
                        v3 = B.ap[q][:, 0:384].rearrange("p (h d) -> p h d", h=2)
                        d1 = headnorm(v3[:, :, 0:128], 128, 2, 128, qnb[tp][:, 2 * gq:2 * gq + 2, :], sqB[0][:, 0:256].rearrange("p (h d) -> p h d", h=2),
                                      tmB[0][:, 0:256].rearrange("p (h d) -> p h d", h=2), stH[3], None, qkw_t, None, [t], key="h0")
                        d2 = headnorm(v3[:, :, 128:192], 128, 2, 64, qpb[tp][:, 2 * gq:2 * gq + 2, :], sqB[1][:, 0:128].rearrange("p (h d) -> p h d", h=2),
                                      tmB[1][:, 0:128].rearrange("p (h d) -> p h d", h=2), stH[4], None, qpew_t, csB[tp], [t, l2, d1],
                                      tmp2=tm2[1][:, 0:128].rearrange("p (h d) -> p h d", h=2), key="h1")
                        B.release(q, [d1, d2])
                    free_cs[tp] = [d2]
                    evs = transposes_out(qnb[tp][:, :, :].rearrange("p h d -> p (h d)"), 128, 8, QnSt[tp], free_st[tp], [d1, d2])
                    stage_toks += evs
                    evs = transposes_out(qpb[tp][:, :, :].rearrange("p h d -> p (h d)"), 128, 4, QpSt[tp], free_st[tp], [d1, d2])
                    stage_toks += evs
                    for (wmat, gst) in ((wgm, GMst), (wgn, GNst)):
                        for half in range(2):
                            q, t = proj(n, tp, wmat, half * 512, 512, c1)
                            ag = P.op("act", lambda e, q=q, tp=tp, half=half, gst=gst, r1=r1: e.activation(
                                out=gst[tp][:, half * 512:(half + 1) * 512], in_=B.ap[q][:, :], func=AF.Silu, scale=r1),
                                waits=[t, d] + free_st[tp])
                            B.release(q, [ag])
                            stage_toks.append(ag)
                    for half in range(2):
                        q, t = proj(n, tp, wnq, half * 512, 512, c1)
                        dk = headnorm(B.ap[q][:, :].rearrange("p (h d) -> p h d", h=8), 128, 8, 64,
                                      nqh[tp][:, half * 512:(half + 1) * 512].rearrange("p (h d) -> p h d", h=8),
                                      sqB[half][:, :].rearrange("p (h d) -> p h d", h=8), tmB[half][:, :].rearrange("p (h d) -> p h d", h=8),
                                      stH[5 + half], (r1, r1sq), naqw_t, None, [t, d], key="h%d" % half)
                        B.release(q, [dk])
                        last_pe_x = t
                    evs = transposes_out(nqh[tp], 128, 8, NQst[tp], free_st[tp], [dk])
                    stage_toks += evs
                else:
                    last_pe_x = t
                free_x[tp] = [c1, c2]
                free_xb[tp] = [last_pe_x] if last_pe_x is not None else []
                s = P.dma("pool", "sB%d" % tp, lambda e, tp=tp, n=n, tok0=tok0: e.dma_start(out=NKTv[:, :, tok0:tok0 + n], in_=NKst[tp][:, :, 0:n]),
                          waits=stage_toks)
                s = P.dma("pool", "sB%d" % tp, lambda e, tp=tp, n=n, tok0=tok0: e.dma_start(out=NV_s[tok0:tok0 + n, :, :], in_=NVst[tp][0:n, :, :]))
                if own:
                    q0 = ti * 128
                    s = P.dma("pool", "sB%d" % tp, lambda e, tp=tp, q0=q0: e.dma_start(out=QTnv[:, :, q0:q0 + 128], in_=QnSt[tp][:, :, :]))
                    s = P.dma("pool", "sB%d" % tp, lambda e, tp=tp, q0=q0: e.dma_start(out=QTpv[:, :, q0:q0 + 128], in_=QpSt[tp][:, :, :]))
                    s = P.dma("pool", "sB%d" % tp, lambda e, tp=tp, q0=q0: e.dma_start(out=NQTv[:, :, q0:q0 + 128], in_=NQst[tp][:, :, :]))
                    s = P.dma("pool", "sB%d" % tp, lambda e, tp=tp, q0=q0: e.dma_start(out=GM_s[q0:q0 + 128, :], in_=GMst[tp][:, :]))
                    s = P.dma("pool", "sB%d" % tp, lambda e, tp=tp, q0=q0: e.dma_start(out=GN_s[q0:q0 + 128, :], in_=GNst[tp][:, :]))
                free_st[tp] = [s]
            P.barrier()

        if stop_after not in ("A", "B") and not (stop_after or "").startswith("Bn"):
            ab.reset(); af.reset()
            NKA = [ab.take(WALL) for _ in range(2)]
            NQA = [ab.take(OWN) for _ in range(2)]
            NVall = ab.take(21 * 16 * 65).rearrange("p (j h e) -> p j h e", j=21, h=16)
            GNall = ab.take(16 * 1024).rearrange("p (j f) -> p j f", j=16)
            PTc = [ab.take(512) for _ in range(4)]
            ypair = ab.take(16 * 128).rearrange("p (u f) -> p u f", u=16)
            YTst = ab.take(OWN)
            EB = [af.take(1024) for _ in range(2)]
            Ef = [af.take(512) for _ in range(2)]
            rdn = af.take(16)
            mstage = af.take(WTOK)
            pieces = na_pieces()
            lA = P.dma("sp", "m0", lambda e: e.dma_start(out=mstage[64:104, 0:WTOK], in_=konehot))
            for b_ in range(2):
                P.op("dve", lambda e, b_=b_: e.memset(NKA[b_][64:128, :], 0.0))
                P.op("dve", lambda e, b_=b_: e.memset(NQA[b_][64:128, :], 0.0))
                lk = P.op("dve", lambda e, b_=b_: e.tensor_copy(out=NKA[b_][64:104, 0:WTOK], in_=mstage[64:104, 0:WTOK]), waits=[lA])
            lB = P.dma("sp", "m1", lambda e: e.dma_start(out=mstage[64:104, 0:OWN], in_=qmask), waits=[lk])
            for b_ in range(2):
                lq = P.op("dve", lambda e, b_=b_: e.tensor_copy(out=NQA[b_][64:104, :], in_=mstage[64:104, 0:OWN]), waits=[lB])
            lv = P.dma("sp", "m2", lambda e: e.dma_start(out=NVall[:, :, :, :], in_=NV_s.rearrange("(j p) h e -> p j h e", p=128)))
            lg = P.dma("sp", "m3", lambda e: e.dma_start(out=GNall[:, :, :], in_=GN_s.rearrange("(j p) f -> p j f", p=128)))
            accb = [0, 1, 2]

            def acc_ap(u):
                return B.ap[accb[u // 7]][:, (u % 7) * 65:(u % 7) * 65 + 65]

            free_nk = [[], []]
            free_eb = [[], []]
            free_pt = [[], [], [], []]
            free_ef = [[], []]
            free_yp = []
            free_yst = []
            ucnt = 0
            for h in range(NH):
                b_ = h % 2
                lk = P.dma("sp", "lC%d" % b_, lambda e, b_=b_, h=h: e.dma_start(out=NKA[b_][0:64, :], in_=NKT_s[h]), waits=free_nk[b_])
                lq2 = P.dma("sp", "lC%d" % b_, lambda e, b_=b_, h=h: e.dma_start(out=NQA[b_][0:64, :], in_=NQT_s[h]))
                le = P.dma("sp", "lC%d" % b_, lambda e, b_=b_, h=h: e.dma_start(out=EB[b_][:, :], in_=ebias[h]), waits=free_eb[b_])
                ae = P.op("act", lambda e, b_=b_: e.activation(out=EB[b_][:, :], in_=EB[b_][:, :], func=AF.Exp), waits=[le])
                z = None
                for k in range(3):
                    z = P.op("dve", lambda e, k=k: e.memset(B.ap[accb[k]][:, 0:455], 0.0), waits=B.waits(accb[k]))
                pv_last = None
                first = True
                pendc = []

                def issue_pv_c(item):
                    (m_, q0_, q1_, pi_, ready_, nk_) = item
                    t_ = None
                    for u in range(q0_ // 128, q1_ // 128):
                        o = u * 128 - q0_
                        t_ = mm(acc_ap(u), PTc[pi_][0:nk_, o:o + 128], NVall[0:nk_, m_, h, :], False, False,
                                waits=[ready_] if u == q0_ // 128 else (), inc=(u == q1_ // 128 - 1), skip=True)
                    free_pt[pi_] = [t_]
                    return t_

                for m in range(21):
                    nk = 128 if m < 20 else 16
                    pcs = pieces[m] if m < 20 else [(0, 512), (512, 1024), (1024, 1536), (1536, 2048)]
                    for (q0, q1) in pcs:
                        nq = q1 - q0
                        sb_ = (3, 4, 7)[ucnt % 3]
                        pS = B.ap[sb_]
                        pi = ucnt % 4
                        fi = ucnt % 2
                        ucnt += 1
                        w0 = B.waits(sb_) + ([le, lq, lv, z] if first else [])
                        first = False
                        if m < 20:
                            ts = mm(pS[:, 0:nq], NKA[b_][:, m * 128:(m + 1) * 128], NQA[b_][:, q0:q1], True, True, waits=w0, inc=True)
                            a1 = P.op("act", lambda e, fi=fi, nq=nq, pS=pS: e.activation(out=Ef[fi][:, 0:nq], in_=pS[:, 0:nq], func=AF.Exp, scale=NA_SCALE),
                                      waits=[ts] + free_ef[fi])
                            rel0 = (q0 // 64) - 2 * m + 11
                            d1 = P.op("dve", lambda e, fi=fi, pi=pi, nq=nq, rel0=rel0, b_=b_: e.tensor_tensor(
                                out=PTc[pi][:, 0:nq], in0=Ef[fi][:, 0:nq], in1=EB[b_][:, rel0 * 64:rel0 * 64 + nq], op=ALU.mult),
                                waits=[a1, ae] + free_pt[pi])
                            B.release(sb_, [a1])
                            free_ef[fi] = [d1]
                            ready = d1
                        else:
                            ts = mm(pS[0:16, 0:nq], NKA[b_][:, WTOK:WALL], NQA[b_][:, q0:q1], True, True, waits=w0, inc=True)
                            a1 = P.op("act", lambda e, pi=pi, nq=nq, pS=pS, h=h: e.activation(
                                out=PTc[pi][0:16, 0:nq], in_=pS[0:16, 0:nq], func=AF.Exp, scale=NA_SCALE, bias=metab_t[0:16, h:h + 1]),
                                waits=[ts] + free_pt[pi])
                            B.release(sb_, [a1])
                            ready = a1
                        pendc.append((m, q0, q1, pi, ready, nk))
                        if len(pendc) > 2:
                            pv_last = issue_pv_c(pendc.pop(0))
                while pendc:
                    pv_last = issue_pv_c(pendc.pop(0))
                free_nk[b_] = [pv_last]
                free_eb[b_] = [pv_last]
                ev = None
                for u in range(16):
                    a = acc_ap(u)
                    r = P.op("dve", lambda e, a=a, u=u: e.reciprocal(out=rdn[:, u:u + 1], in_=a[:, 64:65]), waits=[pv_last] + (free_yp if (u == 0 and h % 2 == 0) else []))
                    ev = P.op("dve", lambda e, a=a, u=u, h=h: e.scalar_tensor_tensor(
                        out=ypair[:, u, (h % 2) * 64:(h % 2) * 64 + 64], in0=a[:, 0:64], scalar=rdn[:, u:u + 1],
                        in1=GNall[:, u, h * 64:(h + 1) * 64], op0=ALU.mult, op1=ALU.mult), waits=[r, lg])
                for k in range(3):
                    B.release(accb[k], [ev])
                if h % 2 == 1:
                    hp = h // 2
                    tt = None
                    evs = []
                    for u4 in range(4):
                        q = 5 + (u4 % 2)
                        for k in range(4):
                            u = u4 * 4 + k
                            tt = mm(B.ap[q][:, k * 128:(k + 1) * 128], ypair[:, u, :], ident[:, :], True, True,
                                    waits=([ev] + B.waits(q)) if k == 0 else (), inc=(k == 3))
                        e2 = P.op("act", lambda e, q=q, u4=u4: e.activation(out=YTst[:, u4 * 512:(u4 + 1) * 512], in_=B.ap[q][:, :], func=AF.Copy),
                                  waits=[tt] + (free_yst if u4 == 0 else []))
                        B.release(q, [e2])
                        evs.append(e2)
                    free_yp = [tt]
                    s = P.dma("pool", "st2", lambda e, hp=hp: e.dma_start(out=YT_s[1024 + hp * 128:1024 + (hp + 1) * 128, :], in_=YTst[:, :]), waits=evs)
                    free_yst = [s]
            P.barrier()

        if stop_after not in ("A", "B", "C") and not (stop_after or "").startswith("Bn"):
            ab.reset(); af.reset()
            KTh = ab.take(TK)
            kpeT = ab.take(TK)
            Vh = ab.take(NT * 129).rearrange("p (j e) -> p j e", j=NT)
            Qn = [ab.take(OWN) for _ in range(2)]
            Qp = [ab.take(OWN) for _ in range(2)]
            GMh = [ab.take(16 * 128).rearrange("p (j f) -> p j f", j=16) for _ in range(2)]
            PTd = [ab.take(512) for _ in range(4)]
            ybf = [ab.take(128) for _ in range(2)]
            YTd = [ab.take(1024) for _ in range(2)]
            rdd = af.take(16)
            lp = P.dma("sp", "m0", lambda e: e.dma_start(out=kpeT[0:64, :], in_=KPE_s))
            zp = P.op("pool", lambda e: e.memset(kpeT[64:128, :], 0.0))
            for hb_ in range(2):
                zp = P.op("pool", lambda e, hb_=hb_: e.memset(Qp[hb_][64:128, :], 0.0))
            NGK = 8
            gtiles = [list(range(16 * g, 16 * g + 16)) for g in range(NGK)]
            gtiles[7].append(128)
            accb = [0, 1, 2]

            def accd(i):
                return B.ap[accb[i // 3]][:, (i % 3) * 129:(i % 3) * 129 + 129]

            kv_free = [[] for _ in range(NGK)]
            kv_ld = [None] * NGK
            q_free = [[], []]
            free_ptd = [[], [], [], []]
            free_yb = [[], []]
            free_ytd = [[], []]
            gm_free = [[], []]
            GMv = GM_s.rearrange("(j p) f -> p j f", p=128)
            ucnt = 0
            pcount = 0
            for h in range(H):
                hb = h % 2
                lqn = P.dma("sp", "lD%d" % hb, lambda e, hb=hb, h=h: e.dma_start(out=Qn[hb][:, :], in_=QT_s[h]), waits=q_free[hb])
                lqp = P.dma("sp", "lD%d" % hb, lambda e, hb=hb, h=h: e.dma_start(out=Qp[hb][0:64, :], in_=QPT_s[h]))
                lgm = P.dma("sp", "lD%d" % hb, lambda e, hb=hb, h=h: e.dma_start(out=GMh[hb][:, :, :], in_=GMv[:, :, h * 128:(h + 1) * 128]), waits=gm_free[hb])
                for g in range(NGK):
                    ta, tb = gtiles[g][0], gtiles[g][-1] + 1
                    t0, t1 = ta * 128, min(tb * 128, TK)
                    P.dma("sp", "kv%d" % g, lambda e, h=h, t0=t0, t1=t1: e.dma_start(out=KTh[:, t0:t1], in_=KT_s[h, :, t0:t1]), waits=kv_free[g])
                    kv_ld[g] = P.dma("sp", "kv%d" % g, lambda e, h=h, ta=ta, tb=tb: e.dma_start(out=Vh[:, ta:tb, :], in_=V_s[h, :, ta:tb, :]))
                for qc in range(2):
                    z = None
                    for k in range(3):
                        z = P.op("dve", lambda e, k=k: e.memset(B.ap[accb[k]][:, 0:387], 0.0), waits=B.waits(accb[k]))
                    units = [(j, s) for j in range(NT) for s in range(2)]
                    pend = []

                    def issue_pv(item):
                        (j, s, pi, ex, nk) = item
                        t = None
                        for i in range(4):
                            t = mm(accd(4 * s + i), PTd[pi][0:nk, i * 128:(i + 1) * 128], Vh[0:nk, j, :], False, False,
                                   waits=[ex, z] if i == 0 else (), inc=(i == 3), skip=True)
                        free_ptd[pi] = [t]
                        if qc == 1 and s == 1 and (j % 16 == 15 or j == 128) and not (j == 127):
                            kv_free[min(j // 16, 7)] = [t]
                        return t

                    pv_last = None
                    for (j, s) in units:
                        nk = 128 if j < 128 else 16
                        g = min(j // 16, 7)
                        sb_ = 3 + (ucnt % 3)
                        pi = ucnt % 4
                        ucnt += 1
                        pS = B.ap[sb_]
                        q0 = qc * 1024 + s * 512
                        w0 = B.waits(sb_)
                        if s == 0 and (j % 16 == 0) and j < 128:
                            w0 = w0 + [kv_ld[g]]
                        if j == 0 and s == 0:
                            w0 = w0 + [lgm, lp, zp]
                        mm(pS[0:nk, :], KTh[:, j * 128:j * 128 + nk], Qn[hb][:, q0:q0 + 512], True, False, waits=w0)
                        ts = mm(pS[0:nk, :], kpeT[:, j * 128:j * 128 + nk], Qp[hb][:, q0:q0 + 512], False, True, inc=True)
                        ex = P.op("act", lambda e, pi=pi, nk=nk, pS=pS: e.activation(out=PTd[pi][0:nk, :], in_=pS[0:nk, :], func=AF.Exp, scale=MLA_SCALE),
                                  waits=[ts] + free_ptd[pi])
                        B.release(sb_, [ex])
                        pend.append((j, s, pi, ex, nk))
                        if len(pend) > 2:
                            pv_last = issue_pv(pend.pop(0))
                    while pend:
                        pv_last = issue_pv(pend.pop(0))
                    if qc == 1:
                        q_free[hb] = [pv_last]
                    for i in range(8):
                        a = accd(i)
                        u = qc * 8 + i
                        yb = ybf[i % 2]
                        r = P.op("dve", lambda e, a=a, i=i: e.reciprocal(out=rdd[:, i:i + 1], in_=a[:, 128:129]), waits=[pv_last])
                        ev = P.op("dve", lambda e, a=a, i=i, u=u, yb=yb, hb=hb: e.scalar_tensor_tensor(
                            out=yb[:, :], in0=a[:, 0:128], scalar=rdd[:, i:i + 1], in1=GMh[hb][:, u, :], op0=ALU.mult, op1=ALU.mult),
                            waits=[r, lgm] + free_yb[i % 2])
                        q = 6 + (i // 4) % 2
                        if i % 4 == 0:
                            wq_ = B.waits(q)
                        tt = mm(B.ap[q][:, (i % 4) * 128:(i % 4 + 1) * 128], yb[:, :], ident[:, :], True, True,
                                waits=[ev] + (wq_ if i % 4 == 0 else []), inc=True)
                        free_yb[i % 2] = [tt]
                        if i % 4 == 3:
                            yk = pcount % 2
                            e2 = P.op("act", lambda e, q=q, yk=yk, i=i: e.activation(
                                out=YTd[yk][:, (i // 4) * 512:(i // 4 + 1) * 512], in_=B.ap[q][:, :], func=AF.Copy),
                                waits=[tt] + (free_ytd[yk] if i == 3 else []))
                            B.release(q, [e2])
                            if i == 7:
                                s_ = P.dma("pool", "sD%d" % yk, lambda e, yk=yk, h=h, qc=qc: e.dma_start(
                                    out=YT_s[h * 128:(h + 1) * 128, qc * 1024:(qc + 1) * 1024], in_=YTd[yk][:, :]), waits=[e2])
                                free_ytd[yk] = [s_]
                    for k in range(3):
                        B.release(accb[k], [ev])
                    if qc == 1:
                        gm_free[hb] = [ev]
                    pcount += 1
            P.barrier()

        if stop_after is None:
            ab.reset(); af.reset()
            wst = [af.take(2048), af.take(2048)]
            wst_state["free"] = [[], []]
            wo = ab.take(16 * 1024).rearrange("p (c f) -> p c f", c=16)
            yT = ab.take(16 * OWN).rearrange("p (c t) -> p c t", c=16)
            xo = [af.take(1024) for _ in range(2)]
            oo = [af.take(1024) for _ in range(2)]
            ly = P.dma("sp", "m1", lambda e: e.dma_start(out=yT[:, :, :], in_=YT_s.rearrange("(c p) t -> p c t", p=128)))
            tw = load_w(wo, w_out, 16, 0, 1024, None, wst)
            free_xo = [[], []]
            free_oo = [[], []]
            fin = None
            for ti in range(16):
                tp = ti % 2
                lx = P.dma("sp", "lE%d" % tp, lambda e, tp=tp, ti=ti: e.dma_start(out=xo[tp][:, :], in_=xown[ti * 128:(ti + 1) * 128, :]), waits=free_xo[tp])
                for half in range(2):
                    q = (ti * 2 + half) % 8
                    t = None
                    for c in range(16):
                        t = mm(B.ap[q][:, :], yT[:, c, ti * 128:(ti + 1) * 128], wo[:, c, half * 512:(half + 1) * 512], c == 0, c == 15,
                               waits=([ly, tw] + B.waits(q)) if c == 0 else (), inc=(c == 15))
                    a = P.op("dve", lambda e, q=q, tp=tp, half=half: e.tensor_tensor(
                        out=oo[tp][:, half * 512:(half + 1) * 512], in0=B.ap[q][:, :], in1=xo[tp][:, half * 512:(half + 1) * 512], op=ALU.add),
                        waits=[t, lx] + (free_oo[tp] if half == 0 else []))
                    B.release(q, [a])
                free_xo[tp] = [a]
                fin = P.dma("pool", "fin%d" % tp, lambda e, tp=tp, ti=ti: e.dma_start(out=out[ti * 128:(ti + 1) * 128, :], in_=oo[tp][:, :]), waits=[a])
                free_oo[tp] = [fin]

        fin_waits = [(s, P.cnt[s]) for s in P.dma_sems if s in P.cnt]
        P._push("pool", lambda e: e.memset(identf[:, 0:1], 0.0), fin_waits + [(s, v) for s, v in P.cnt.items() if s in Prog.CE], None, 1)

        for nme in P.cnt:
            P.sems[nme] = es.enter_context(nc.semaphore(nme))
        block = es.enter_context(nc.Block())

        @block.sync
        def _(e):
            P.replay("sp", e)

        @block.tensor
        def _(e):
            P.replay("pe", e)

        @block.scalar
        def _(e):
            P.replay("act", e)

        @block.vector
        def _(e):
            P.replay("dve", e)

        @block.gpsimd
        def _(e):
            P.replay("pool", e)
    return nc


def _rope_table(pos):
    inv_freq = (10000.0 ** (-(np.arange(0, 64, 2, dtype=np.float32) / 64))).astype(np.float32)
    ang = pos.astype(np.float32)[:, None] * inv_freq[None, :]
    c, s = np.cos(ang).astype(np.float32), np.sin(ang).astype(np.float32)
    return np.ascontiguousarray(np.concatenate([c, c, -s, s], axis=1))


def prepare_inputs(x, meta_tokens, norm_w, w_in, q_lat_norm_w, kv_lat_norm_w, w_uq, w_ukv,
                   mla_qn_w, mla_qpe_w, mla_kn_w, mla_kpe_w, na_q_norm_w, na_k_norm_w,
                   na_rel_bias, na_meta_bias, w_out):
    f = lambda a: np.ascontiguousarray(np.asarray(a, dtype=np.float32))
    x = f(x)[0]
    meta = f(meta_tokens)
    xall = np.concatenate([x, meta], axis=0)
    xT = np.ascontiguousarray(xall.T)
    posk = np.concatenate([np.arange(SEQ) + NMETA, np.arange(NMETA)])
    csk = _rope_table(posk)
    rb = f(na_rel_bias)[0]
    cols = np.arange(GRID_W)
    c0 = np.clip(cols - 8, 0, GRID_W - 16)
    eb = np.full((NH, 2, 64, 16, 64), NEG, np.float32)
    for jj in range(2):
        for irel in range(16):
            dr = jj + 7 - irel
            if not (-7 <= dr <= 7):
                continue
            for cq in range(64):
                ck = np.arange(c0[cq], c0[cq] + 16)
                eb[:, jj, ck, irel, cq] = rb[:, dr + 7, ck - cq + 15]
    eb = np.ascontiguousarray(eb.reshape(NH, 128, 1024))
    konehot = np.zeros((WROWS, WTOK), np.float32)
    for j in range(WROWS):
        konehot[j, j * 64:(j + 1) * 64] = 1.0
    shared = {
        "xT": xT, "w_in": f(w_in)[0], "w_uq": f(w_uq)[0], "w_ukv": f(w_ukv)[0], "w_out": f(w_out)[0],
        "normw": np.ascontiguousarray(f(norm_w)[0].reshape(8, 128).T),
        "qlw": np.ascontiguousarray(f(q_lat_norm_w)[0].reshape(2, 128).T),
        "kvlw": np.ascontiguousarray(f(kv_lat_norm_w)[0].reshape(2, 128).T),
        "qnw": f(mla_qn_w)[0][None, :], "knw": f(mla_kn_w)[0][None, :],
        "qpew": f(mla_qpe_w)[0][None, :], "kpew": f(mla_kpe_w)[0][None, :],
        "naqw": f(na_q_norm_w)[0][None, :], "nakw": f(na_k_norm_w)[0][None, :],
        "csk": csk, "ebias": eb, "konehot": konehot,
        "metab": np.ascontiguousarray(f(na_meta_bias)[0].T),
    }
    in_maps = []
    for c in range(NCORES):
        xw = np.zeros((WALL, D), np.float32)
        for j in range(WROWS):
            gr = 32 * c - 4 + j
            if 0 <= gr < ROWS:
                xw[j * 64:(j + 1) * 64] = x[gr * 64:(gr + 1) * 64]
        xw[WTOK:] = meta
        qm = np.zeros((WROWS, OWN), np.float32)
        for i in range(32):
            for j in range(WROWS):
                if not na_valid(c, i, j):
                    qm[j, i * 64:(i + 1) * 64] = NEG
        m = dict(shared)
        m["xTw"] = np.ascontiguousarray(xw.T)
        m["xown"] = np.ascontiguousarray(x[c * OWN:(c + 1) * OWN])
        m["csq"] = _rope_table(np.arange(c * OWN, (c + 1) * OWN) + NMETA)
        m["qmask"] = qm
        in_maps.append(m)
    return in_maps


def kernel(**inputs):
    in_maps = prepare_inputs(**inputs)
    nc = build_program()
    res = run_bass_kernel_spmd(nc, in_maps, core_ids=list(range(NCORES)))
    outs = [np.asarray(r["out"], dtype=np.float32) for r in res.results]
    return np.concatenate(outs, axis=0)[None, :, :]
```

```python
import numpy as np
from contextlib import ExitStack
import concourse.bass as bass
import concourse.mybir as mybir
from concourse.bass_utils import run_bass_kernel_spmd

F32 = mybir.dt.float32
BF16 = mybir.dt.bfloat16
AF = mybir.ActivationFunctionType
ALU = mybir.AluOpType
AX = mybir.AxisListType

NCORES = 8
D = 1024
SEQ = 16384
NMETA = 16
TK = SEQ + NMETA
NT = 129
OWN = 2048
WROWS = 40
WTOK = WROWS * 64
WALL = WTOK + NMETA
H = 8
NH = 16
EPS = 1e-6
GRID_W = 64
ROWS = 256
NEG = -30000.0
MLA_SCALE = 192 ** -0.5
NA_SCALE = 0.125


class Prog:
    CE = ("pe", "act", "dve", "pool")

    def __init__(self):
        self.ops = {e: [] for e in ("pe", "act", "dve", "pool", "sp")}
        self.sems = {}
        self.cnt = {}
        self.waited = {e: {} for e in self.ops}
        self.pending = {e: [] for e in self.ops}
        self.dma_sems = []

    def _push(self, eng, fn, waits, inc, n):
        ws = list(self.pending[eng]) + [w for w in waits if w is not None]
        self.pending[eng] = []
        mx = {}
        for (s, v) in ws:
            if v > mx.get(s, 0):
                mx[s] = v
        fw = []
        for s, v in mx.items():
            if self.waited[eng].get(s, 0) >= v:
                continue
            self.waited[eng][s] = v
            fw.append((s, v))
        tok = None
        if inc is not None:
            self.cnt[inc] = self.cnt.get(inc, 0) + n
            tok = (inc, self.cnt[inc])
        self.ops[eng].append((fn, tuple(fw), inc, n))
        return tok

    def op(self, eng, fn, waits=(), inc=True):
        return self._push(eng, fn, waits, eng if inc else None, 1)

    def dma(self, eng, sem, fn, waits=()):
        if sem not in self.dma_sems:
            self.dma_sems.append(sem)
        return self._push(eng, fn, waits, sem, 16)

    def barrier(self):
        toks = [(s, v) for s, v in self.cnt.items() if s != "cc"]
        for e in self.pending:
            self.pending[e] = list(toks)

    def replay(self, eng, e):
        for fn, waits, inc, n in self.ops[eng]:
            for (s, v) in waits:
                e.wait_ge(self.sems[s], v)
            ins = fn(e)
            if inc is not None:
                ins.then_inc(self.sems[inc], n)


class Arena:
    def __init__(self, t, cols):
        self.t, self.cols, self.off = t, cols, 0

    def reset(self):
        self.off = 0

    def take(self, n):
        a = self.off
        self.off += (n + 7) // 8 * 8
        assert self.off <= self.cols, (self.off, self.cols)
        return self.t[:, a:a + n]


class Banks:
    def __init__(self, aps):
        self.ap = aps
        self.free = [[] for _ in aps]

    def waits(self, i):
        return list(self.free[i])

    def release(self, i, toks):
        self.free[i] = [t for t in toks if t is not None]


def na_valid(c, i, j):
    r = 32 * c + i
    kr = 32 * c - 4 + j
    r0 = min(max(r - 4, 0), ROWS - 8)
    return (0 <= kr < ROWS) and (r0 <= kr < r0 + 8)


def na_pieces():
    out = []
    for m in range(20):
        rows = [i for i in range(32) if any(na_valid(c, i, j) for c in range(NCORES) for j in (2 * m, 2 * m + 1))]
        lo, hi = min(rows) & ~1, max(rows) | 1
        assert 0 <= lo - 2 * m + 11 and hi - 2 * m + 11 <= 15, (m, lo, hi)
        pcs = []
        r = lo
        while r <= hi:
            r2 = min(r + 8, hi + 1)
            pcs.append((r * 64, r2 * 64))
            r = r2
        out.append(pcs)
    return out


def build_program(dbg=False, stop_after=None):
    nc = bass.Bass("TRN2", target_bir_lowering=False)

    def din(name, shape, dt=F32):
        return nc.dram_tensor(name, list(shape), dt, kind="ExternalInput").ap()

    xTw = din("xTw", [D, WALL])
    xown = din("xown", [OWN, D])
    w_in = din("w_in", [D, 5696])
    w_uq = din("w_uq", [256, 1536])
    w_ukv = din("w_ukv", [256, 2048])
    w_out = din("w_out", [2048, D])
    normw = din("normw", [128, 8])
    qlw = din("qlw", [128, 2])
    kvlw = din("kvlw", [128, 2])
    qnw = din("qnw", [1, 128])
    knw = din("knw", [1, 128])
    qpew = din("qpew", [1, 64])
    kpew = din("kpew", [1, 64])
    naqw = din("naqw", [1, 64])
    nakw = din("nakw", [1, 64])
    csm = din("csm", [NMETA, 128])
    csq = din("csq", [OWN, 128])
    ebias = din("ebias", [NH, 128, 1024])
    qmask = din("qmask", [WROWS, OWN])
    konehot = din("konehot", [WROWS, WTOK])
    metab = din("metab", [NMETA, NH])
    out = nc.dram_tensor("out", [OWN, D], F32, kind="ExternalOutput").ap()

    skind = "ExternalOutput" if dbg else "Internal"

    def scratch(name, shape, dt=BF16):
        return nc.dram_tensor(name, list(shape), dt, kind=skind).ap()

    def cscratch(name, shape, dt=BF16):
        return nc.dram_tensor(name, list(shape), dt).ap()

    KT_part = cscratch("KT_part", [H * 128, OWN])
    V_part = cscratch("V_part", [H * 128, 16 * 129])
    KPE_part = cscratch("KPE_part", [64, OWN])
    KT_all = cscratch("KT_all", [NCORES * H * 128, OWN])
    V_all = cscratch("V_all", [NCORES * H * 128, 16 * 129])
    KPE_all = cscratch("KPE_all", [NCORES * 64, OWN])
    KT_m = scratch("KT_m", [H, 128, NMETA])
    V_m = scratch("V_m", [H, 128, 1, 129])
    KPE_m = scratch("KPE_m", [64, NMETA])
    QT_s = scratch("QT_s", [H, 128, OWN])
    QPT_s = scratch("QPT_s", [H, 64, OWN])
    GM_s = scratch("GM_s", [OWN, 1024])
    GN_s = scratch("GN_s", [OWN, 1024])
    NQT_s = scratch("NQT_s", [NH, 64, OWN])
    NKT_s = scratch("NKT_s", [NH, 64, WALL])
    NV_s = scratch("NV_s", [21 * 128, NH, 65])
    YT_s = scratch("YT_s", [2048, OWN])

    P = Prog()
    with ExitStack() as es:
        AB_COLS = 73728
        AF_COLS = 14848
        abt = es.enter_context(nc.sbuf_tensor("arena_bf", [128, AB_COLS], BF16))
        aft = es.enter_context(nc.sbuf_tensor("arena_f", [128, AF_COLS], F32))
        ident = es.enter_context(nc.sbuf_tensor("ident", [128, 128], BF16))
        identf = es.enter_context(nc.sbuf_tensor("identf", [128, 128], F32))
        ones = es.enter_context(nc.sbuf_tensor("ones", [128, 8], BF16))
        consts = es.enter_context(nc.sbuf_tensor("consts", [128, 1024], F32))
        pb = [es.enter_context(nc.psum_tensor("pb%d" % i, [128, 512], F32)) for i in range(8)]
        ab = Arena(abt, AB_COLS)
        af = Arena(aft, AF_COLS)
        B = Banks([p[:, :] for p in pb])

        qkw_t = consts[:, 0:128]
        qpew_t = consts[:, 384:448]
        kpew_t = consts[:, 448:512]
        naqw_t = consts[:, 512:576]
        nakw_t = consts[:, 576:640]
        normw_t = consts[:, 640:648]
        qlw_t = consts[:, 648:650]
        kvlw_t = consts[:, 650:652]
        metab_t = consts[:, 656:672]

        P.op("pool", lambda e: e.memset(identf[:], 0.0))
        i1 = P.op("pool", lambda e: e.affine_select(out=identf[:], in_=identf[:], pattern=[[-1, 128]],
                                                    compare_op=ALU.not_equal, fill=1.0, base=0, channel_multiplier=1))
        P.op("dve", lambda e: e.tensor_copy(out=ident[:], in_=identf[:]), waits=[i1])
        P.op("dve", lambda e: e.memset(ones[:], 1.0))
        lt = None
        for (dst, src) in [(consts[:, 128:256], qnw), (consts[:, 256:384], knw), (qpew_t, qpew), (kpew_t, kpew),
                           (naqw_t, naqw), (nakw_t, nakw)]:
            lt = P.dma("sp", "ld0", lambda e, dst=dst, src=src: e.dma_start(out=dst, in_=src.partition_broadcast(128)))
        for (dst, src) in [(normw_t, normw), (qlw_t, qlw), (kvlw_t, kvlw)]:
            lt = P.dma("sp", "ld0", lambda e, dst=dst, src=src: e.dma_start(out=dst, in_=src))
        lt = P.dma("sp", "ld0", lambda e: e.dma_start(out=metab_t[0:16, :], in_=metab))
        P.op("dve", lambda e: e.tensor_tensor(out=qkw_t, in0=consts[:, 128:256], in1=consts[:, 256:384], op=ALU.mult), waits=[lt])
        P.barrier()

        wst_state = {"k": 0, "free": [[], []]}

        def load_w(dst3, src2d, nch, c0, c1, rowscale, wst):
            last = None
            for ch in range(nch):
                for a in range(c0, c1, 2048):
                    b_ = min(a + 2048, c1)
                    k = wst_state["k"] % 2
                    wst_state["k"] += 1
                    stg = wst[k][:, 0:b_ - a]
                    t = P.dma("sp", "lw%d" % k, lambda e, stg=stg, ch=ch, a=a, b_=b_: e.dma_start(
                        out=stg, in_=src2d[ch * 128:(ch + 1) * 128, a:b_]), waits=wst_state["free"][k])
                    dsl = dst3[:, ch, a - c0:b_ - c0]
                    if rowscale is not None:
                        if k == 0:
                            t2 = P.op("dve", lambda e, dsl=dsl, stg=stg, ch=ch: e.tensor_scalar(
                                out=dsl, in0=stg, scalar1=rowscale[:, ch:ch + 1], scalar2=None, op0=ALU.mult), waits=[t])
                        else:
                            t2 = P.op("act", lambda e, dsl=dsl, stg=stg, ch=ch: e.activation(
                                out=dsl, in_=stg, func=AF.Copy, scale=rowscale[:, ch:ch + 1]), waits=[t])
                    else:
                        if k == 0:
                            t2 = P.op("dve", lambda e, dsl=dsl, stg=stg: e.tensor_copy(out=dsl, in_=stg), waits=[t])
                        else:
                            t2 = P.op("act", lambda e, dsl=dsl, stg=stg: e.activation(out=dsl, in_=stg, func=AF.Copy), waits=[t])
                    wst_state["free"][k] = [t2]
                    last = t2
            return last

        def mm(out_ap, lhsT, rhs, start, stop, waits=(), inc=False, skip=False):
            if skip:
                return P.op("pe", lambda e: e.matmul(out_ap, lhsT=lhsT, rhs=rhs, start=start, stop=stop, skip_group_check=True),
                            waits=waits, inc=inc)
            return P.op("pe", lambda e: e.matmul(out_ap, lhsT=lhsT, rhs=rhs, start=start, stop=stop), waits=waits, inc=inc)

        scr_free = {}

        def run(gen):
            try:
                while True:
                    next(gen)
            except StopIteration as ex_:
                return ex_.value

        def interleave(gens):
            gens = list(gens)
            while gens:
                for gi in list(gens):
                    try:
                        next(gi)
                    except StopIteration:
                        gens.remove(gi)

        def headnorm(*a, **k):
            return run(headnorm_g(*a, **k))

        def headnorm_g(src3, n, nh, hd, outp, sq, tmp, stt, pre, wtile, rope_cs, waits, tmp2=None, key=None):
            waits = list(waits) + scr_free.get(key, [])
            ss = stt[:n, 0:nh]
            sd = stt[:n, nh:2 * nh]
            rr = stt[:n, 2 * nh:3 * nh]
            a = P.op("act", lambda e: e.activation(out=sq, in_=src3, func=AF.Square), waits=waits)
            yield
            d = P.op("dve", lambda e: e.tensor_reduce(out=ss, in_=sq, axis=AX.X, op=ALU.add), waits=[a])
            if pre is not None:
                d = P.op("dve", lambda e: e.tensor_scalar(out=ss, in0=ss, scalar1=pre[1], scalar2=None, op0=ALU.mult), waits=[d])
            yield
            a = P.op("act", lambda e: e.activation(out=sd, in_=ss, func=AF.Sqrt, bias=EPS, scale=1.0 / hd), waits=[d])
            yield
            d = P.op("dve", lambda e: e.reciprocal(out=rr, in_=sd), waits=[a])
            if pre is not None:
                d = P.op("dve", lambda e: e.tensor_scalar(out=rr, in0=rr, scalar1=pre[0], scalar2=None, op0=ALU.mult), waits=[d])
            rb = rr.unsqueeze(2).to_broadcast([n, nh, hd])
            if wtile is None and rope_cs is None:
                tok = P.op("dve", lambda e: e.tensor_tensor(out=outp, in0=src3, in1=rb, op=ALU.mult), waits=[d])
                scr_free[key] = [tok]
                yield
                return tok
            d = P.op("dve", lambda e: e.tensor_tensor(out=tmp, in0=src3, in1=rb, op=ALU.mult), waits=[d])
            wb = wtile[:n, :].unsqueeze(1).to_broadcast([n, nh, hd])
            if rope_cs is None:
                tok = P.op("dve", lambda e: e.tensor_tensor(out=outp, in0=tmp, in1=wb, op=ALU.mult), waits=[d])
                scr_free[key] = [tok]
                yield
                return tok
            d = P.op("dve", lambda e: e.tensor_tensor(out=tmp, in0=tmp, in1=wb, op=ALU.mult), waits=[d])
            hh = hd // 2
            cc = rope_cs[:n, 0:hd].unsqueeze(1).to_broadcast([n, nh, hd])
            nsin = rope_cs[:n, hd:hd + hh].unsqueeze(1).to_broadcast([n, nh, hh])
            psin = rope_cs[:n, hd + hh:2 * hd].unsqueeze(1).to_broadcast([n, nh, hh])
            d1 = P.op("dve", lambda e: e.tensor_tensor(out=tmp2[:, :, 0:hh], in0=tmp[:, :, hh:hd], in1=nsin, op=ALU.mult), waits=[d])
            d2 = P.op("dve", lambda e: e.tensor_tensor(out=tmp2[:, :, hh:hd], in0=tmp[:, :, 0:hh], in1=psin, op=ALU.mult), waits=[d])
            d3 = P.op("dve", lambda e: e.tensor_tensor(out=tmp, in0=tmp, in1=cc, op=ALU.mult), waits=[d2])
            tok = P.op("dve", lambda e: e.tensor_tensor(out=outp, in0=tmp, in1=tmp2, op=ALU.add), waits=[d3])
            scr_free[key] = [tok]
            yield
            return tok

        def rstd_from_ss(*a, **k):
            return run(rstd_from_ss_g(*a, **k))

        def rstd_from_ss_g(ss_ps, n, stt, waits):
            a = P.op("act", lambda e: e.activation(out=stt[:n, 0:1], in_=ss_ps, func=AF.Sqrt, bias=EPS, scale=1.0 / D), waits=waits)
            yield
            d = P.op("dve", lambda e: e.reciprocal(out=stt[:n, 1:2], in_=stt[:n, 0:1]), waits=[a])
            d = P.op("dve", lambda e: e.tensor_tensor(out=stt[:n, 2:3], in0=stt[:n, 1:2], in1=stt[:n, 1:2], op=ALU.mult), waits=[d])
            yield
            return (stt[:n, 1:2], stt[:n, 2:3]), d

        ab.reset(); af.reset()
        wst = [af.take(2048), af.take(2048)]
        xt = [af.take(4096).rearrange("p (c t) -> p c t", c=8) for _ in range(2)]
        cs = [af.take(512).rearrange("p (j f) -> p j f", j=4) for _ in range(2)]
        stA = [af.take(16) for _ in range(2)]
        stK = [af.take(64) for _ in range(2)]
        sqA_ = [af.take(320) for _ in range(2)]
        tmpA_ = [af.take(64) for _ in range(2)]
        tmpB_ = [af.take(64) for _ in range(2)]
        wkv = ab.take(8 * 320).rearrange("p (c f) -> p c f", c=8)
        wukv = ab.take(2 * 2048).rearrange("p (c f) -> p c f", c=2)
        xb = [ab.take(4096).rearrange("p (c t) -> p c t", c=8) for _ in range(2)]
        xsq = [ab.take(4096).rearrange("p (c t) -> p c t", c=8) for _ in range(2)]
        cn = [ab.take(256) for _ in range(2)]
        kpe_b = [ab.take(64) for _ in range(2)]
        cTt = [ab.take(256).rearrange("p (c t) -> p c t", c=2) for _ in range(2)]
        khat = [ab.take(1024).rearrange("p (h d) -> p h d", h=8) for _ in range(2)]
        KTst = [ab.take(4096).rearrange("p (h t) -> p h t", h=8) for _ in range(2)]
        Vst = [ab.take(8 * 4 * 129).rearrange("p (h j e) -> p h j e", h=8, j=4) for _ in range(2)]
        kpest = [ab.take(512) for _ in range(2)]

        load_w(wkv, w_in, 8, 256, 576, normw_t, wst)
        tw = load_w(wukv, w_ukv, 2, 0, 2048, kvlw_t, wst)
        for b_ in range(2):
            P.op("pool", lambda e, b_=b_: e.memset(Vst[b_][:, :, :, :], 1.0))

        xTv = xTw.rearrange("(c p) t -> p c t", p=128)
        KTv = KT_part.rearrange("(h d) t -> d h t", d=128)
        Vv = V_part.rearrange("(h p) (j e) -> p h j e", p=128, e=129)
        KTmv = KT_m.rearrange("h d t -> d h t")
        Vmv = V_m.rearrange("h p j e -> p h j e")
        NGA = 5
        NFULL = 4
        gfree_x = [[], []]
        gfree_xb = [[], []]
        gfree_st = [[], []]
        tcount = 0
        for g in range(NGA):
            b_ = g % 2
            t0 = (256 + g * 512) if g < NFULL else WTOK
            ng = 512 if g < NFULL else 16
            ntile = 4 if g < NFULL else 1
            l1 = P.dma("sp", "lA%d" % b_, lambda e, b_=b_, t0=t0, ng=ng: e.dma_start(out=xt[b_][:, :, 0:ng], in_=xTv[:, :, t0:t0 + ng]),
                       waits=gfree_x[b_])
            if g < NFULL:
                l2 = P.dma("sp", "lA%d" % b_, lambda e, b_=b_, g=g: e.dma_start(
                    out=cs[b_][:, :, :], in_=csq[g * 512:(g + 1) * 512, :].rearrange("(j p) f -> p j f", p=128)))
            else:
                l2 = P.dma("sp", "lA%d" % b_, lambda e, b_=b_: e.dma_start(out=cs[b_][0:16, 0, :], in_=csm))
            c1 = P.op("act", lambda e, b_=b_, ng=ng: e.activation(out=xb[b_][:, :, 0:ng], in_=xt[b_][:, :, 0:ng], func=AF.Copy),
                      waits=[l2] + gfree_xb[b_])
            c2 = P.op("pool", lambda e, b_=b_, ng=ng: e.tensor_tensor(out=xsq[b_][:, :, 0:ng], in0=xt[b_][:, :, 0:ng],
                                                                      in1=xt[b_][:, :, 0:ng], op=ALU.mult), waits=[l2] + gfree_xb[b_])
            res = {"last_pe": None, "last_rope": None, "stage": []}

            def tileA(j, n, tp, b_=b_, l2=l2, c1=c1, c2=c2, res=res):
                sqA, tmpA, tmpB = sqA_[tp], tmpA_[tp], tmpB_[tp]
                sl = slice(j * 128, j * 128 + n)
                pk = B.ap[tp]
                for c in range(8):
                    mm(pk[:n, 0:320], xb[b_][:, c, sl], wkv[:, c, :], c == 0, c == 7, waits=([c1] + B.waits(tp)) if c == 0 else ())
                for c in range(8):
                    tk = mm(pk[:n, 384:385], xsq[b_][:, c, sl], ones[:, 0:1], c == 0, c == 7, waits=[c2] if c == 0 else (), inc=(c == 7))
                yield
                st = stA[tp]
                (r1, r1sq), d = yield from rstd_from_ss_g(pk[:n, 384:385], n, st, [tk])
                dc = yield from headnorm_g(pk[:n, 0:256].rearrange("p (h d) -> p h d", h=1), n, 1, 256, cn[tp][:n, :].rearrange("p (h d) -> p h d", h=1),
                                           sqA[:n, 0:256].rearrange("p (h d) -> p h d", h=1), None, st[:, 4:8], (r1, r1sq), None, None, [tk, d],
                                           key="s0_%d" % tp)
                dk = yield from headnorm_g(pk[:n, 256:320].rearrange("p (h d) -> p h d", h=1), n, 1, 64, kpe_b[tp][:n, :].rearrange("p (h d) -> p h d", h=1),
                                           sqA[:n, 256:320].rearrange("p (h d) -> p h d", h=1), tmpA[:n, :].rearrange("p (h d) -> p h d", h=1),
                                           st[:, 8:12], (r1, r1sq), kpew_t, cs[b_][:, j, :], [tk, d, l2, dc],
                                           tmp2=tmpB[:n, :].rearrange("p (h d) -> p h d", h=1), key="s1_%d" % tp)
                B.release(tp, [dc, dk])
                res["last_rope"] = dk
                ptb = 2 if tp == 0 else 7
                pt = B.ap[ptb]
                mm(pt[:, 0:n], cn[tp][:n, 0:128], ident[:n, :n], True, True, waits=[dc, dk] + B.waits(ptb))
                mm(pt[:, 128:128 + n], cn[tp][:n, 128:256], ident[:n, :n], True, True)
                tt = mm(pt[0:64, 256:256 + n], kpe_b[tp][:n, 0:64], ident[:n, :n], True, True, inc=True)
                yield
                e1 = P.op("dve", lambda e: e.tensor_copy(
                    out=cTt[tp][:, :, 0:n], in_=pt[:, 0:256].rearrange("p (c t) -> p c t", c=2)[:, :, 0:n]), waits=[tt])
                yield
                e2 = P.op("act", lambda e: e.activation(out=kpest[b_][0:64, sl], in_=pt[0:64, 256:256 + n], func=AF.Copy),
                          waits=[tt, e1] + gfree_st[b_])
                B.release(ptb, [e1, e2])
                res["stage"].append(e2)
                stk = stK[tp]
                kh = khat[tp]
                t2 = None
                for gp in range(4):
                    q = 3 + tp
                    pu = B.ap[q]
                    mm(pu[:n, :], cTt[tp][:, 0, 0:n], wukv[:, 0, gp * 512:(gp + 1) * 512], True, False, waits=[e1] + B.waits(q))
                    tu = mm(pu[:n, :], cTt[tp][:, 1, 0:n], wukv[:, 1, gp * 512:(gp + 1) * 512], False, True, inc=True)
                    yield
                    pu4 = pu[:n, :].rearrange("p (h two d) -> p h two d", h=2, two=2)
                    dkk = yield from headnorm_g(pu4[:, :, 0, :], n, 2, 128, kh[:n, 2 * gp:2 * gp + 2, :], sqA[:n, 0:256].rearrange("p (h d) -> p h d", h=2),
                                                None, stk[:, 8 * gp:8 * gp + 8], None, None, None, [tu], key="s0_%d" % tp)
                    av = P.op("act", lambda e, gp=gp, pu4=pu4: e.activation(
                        out=Vst[b_][:n, 2 * gp:2 * gp + 2, j, 0:128], in_=pu4[:, :, 1, :], func=AF.Copy),
                        waits=[tu, dkk] + gfree_st[b_])
                    B.release(q, [dkk, av])
                    res["stage"].append(av)
                    q2 = 5 + tp
                    p2 = B.ap[q2]
                    mm(p2[:, 0:n], kh[:n, 2 * gp, :], ident[:n, :n], True, True, waits=[dkk] + B.waits(q2))
                    t2 = mm(p2[:, 128:128 + n], kh[:n, 2 * gp + 1, :], ident[:n, :n], True, True, inc=True)
                    yield
                    e3 = P.op("dve", lambda e, gp=gp, p2=p2: e.tensor_copy(
                        out=KTst[b_][:, 2 * gp:2 * gp + 2, sl], in_=p2[:, 0:256].rearrange("p (h t) -> p h t", h=2)[:, :, 0:n]),
                        waits=[t2] + gfree_st[b_])
                    B.release(q2, [e3])
                    res["stage"].append(e3)
                    yield
                res["last_pe"] = t2

            if g < NFULL:
                interleave([tileA(0, 128, 0), tileA(1, 128, 1)])
                interleave([tileA(2, 128, 0), tileA(3, 128, 1)])
            else:
                interleave([tileA(0, 16, 0)])
            last_pe = res["last_pe"]
            last_rope = res["last_rope"]
            stage_toks = res["stage"]
            gfree_x[b_] = [c1, c2, last_rope]
            gfree_xb[b_] = [last_pe]
            if g < NFULL:
                s1 = P.dma("pool", "sA%d" % b_, lambda e, b_=b_, g=g: e.dma_start(out=KTv[:, :, g * 512:(g + 1) * 512], in_=KTst[b_][:, :, :]),
                           waits=stage_toks)
                s2 = P.dma("pool", "sA%d" % b_, lambda e, b_=b_, g=g: e.dma_start(out=Vv[:, :, 4 * g:4 * g + 4, :], in_=Vst[b_][:, :, :, :]))
                s3 = P.dma("pool", "sA%d" % b_, lambda e, b_=b_, g=g: e.dma_start(out=KPE_part[:, g * 512:(g + 1) * 512], in_=kpest[b_][0:64, :]))
            else:
                s1 = P.dma("pool", "sA%d" % b_, lambda e, b_=b_: e.dma_start(out=KTmv[:, :, :], in_=KTst[b_][:, :, 0:16]), waits=stage_toks)
                s2 = P.dma("pool", "sA%d" % b_, lambda e, b_=b_: e.dma_start(out=Vmv[0:16, :, 0:1, :], in_=Vst[b_][0:16, :, 0:1, :]))
                s3 = P.dma("pool", "sA%d" % b_, lambda e, b_=b_: e.dma_start(out=KPE_m[:, :], in_=kpest[b_][0:64, 0:16]))
            gfree_st[b_] = [s3]
            a_store_toks = [gfree_st[0][0] if gfree_st[0] else None, gfree_st[1][0] if gfree_st[1] else None]
        rgrp = [list(range(NCORES))]
        P._push("pool", lambda e: e.collective_compute("AllGather", ALU.bypass, replica_groups=rgrp, ins=[KT_part], outs=[KT_all]),
                a_store_toks, "cc", 1)
        P._push("pool", lambda e: e.collective_compute("AllGather", ALU.bypass, replica_groups=rgrp, ins=[V_part], outs=[V_all]), [], "cc", 1)
        cc_tok = P._push("pool", lambda e: e.collective_compute("AllGather", ALU.bypass, replica_groups=rgrp, ins=[KPE_part], outs=[KPE_all]),
                         [], "cc", 1)
        P.barrier()

        if stop_after != "A":
            ab.reset(); af.reset()
            wst = [af.take(2048), af.take(2048)]
            wst_state["free"] = [[], []]
            xtB = [af.take(1024).rearrange("p (c t) -> p c t", c=8) for _ in range(2)]
            csB = [af.take(128) for _ in range(2)]
            stB = [af.take(16) for _ in range(2)]
            stH = [af.take(64) for _ in range(8)]
            sqB = [af.take(512) for _ in range(2)]
            tmB = [af.take(512) for _ in range(2)]
            tm2 = [af.take(512) for _ in range(2)]
            wq = ab.take(8 * 256).rearrange("p (c f) -> p c f", c=8)
            wgm = ab.take(8 * 1024).rearrange("p (c f) -> p c f", c=8)
            wnq = ab.take(8 * 1024).rearrange("p (c f) -> p c f", c=8)
            wnk = ab.take(8 * 1024).rearrange("p (c f) -> p c f", c=8)
            wnv = ab.take(8 * 1024).rearrange("p (c f) -> p c f", c=8)
            wgn = ab.take(8 * 1024).rearrange("p (c f) -> p c f", c=8)
            wuq = ab.take(2 * 1536).rearrange("p (c f) -> p c f", c=2)
            xbB = [ab.take(1024).rearrange("p (c t) -> p c t", c=8) for _ in range(2)]
            xsqB = [ab.take(1024).rearrange("p (c t) -> p c t", c=8) for _ in range(2)]
            nkh = [ab.take(1024) for _ in range(2)]
            nqh = nkh
            NKst = [ab.take(8 * 128).rearrange("p (h t) -> p h t", h=8) for _ in range(2)]
            NQst = [ab.take(8 * 128).rearrange("p (h t) -> p h t", h=8) for _ in range(2)]
            NVst = [ab.take(16 * 65).rearrange("p (h e) -> p h e", h=16) for _ in range(2)]
            qlb = [ab.take(256) for _ in range(2)]
            qlT = [ab.take(256).rearrange("p (c t) -> p c t", c=2) for _ in range(2)]
            qnb = [ab.take(1024).rearrange("p (h d) -> p h d", h=8) for _ in range(2)]
            qpb = [ab.take(512).rearrange("p (h d) -> p h d", h=8) for _ in range(2)]
            QnSt = [ab.take(8 * 128).rearrange("p (h t) -> p h t", h=8) for _ in range(2)]
            QpSt = [ab.take(4 * 128).rearrange("p (h t) -> p h t", h=4) for _ in range(2)]
            GMst = [ab.take(1024) for _ in range(2)]
            GNst = [ab.take(1024) for _ in range(2)]
            load_w(wq, w_in, 8, 0, 256, normw_t, wst)
            load_w(wgm, w_in, 8, 576, 1600, normw_t, wst)
            load_w(wnq, w_in, 8, 1600, 2624, normw_t, wst)
            load_w(wnk, w_in, 8, 2624, 3648, normw_t, wst)
            load_w(wnv, w_in, 8, 3648, 4672, normw_t, wst)
            load_w(wgn, w_in, 8, 4672, 5696, normw_t, wst)
            load_w(wuq, w_uq, 2, 0, 1536, qlw_t, wst)
            for b_ in range(2):
                P.op("pool", lambda e, b_=b_: e.memset(NVst[b_][:, :, :], 1.0))
            xTwv = xTw.rearrange("(c p) t -> p c t", p=128)
            NKTv = NKT_s.rearrange("(hp two) d t -> (two d) hp t", two=2)
            NQTv = NQT_s.rearrange("(hp two) d t -> (two d) hp t", two=2)
            QTnv = QT_s.rearrange("h d t -> d h t")
            QTpv = QPT_s.rearrange("(hp two) r t -> (two r) hp t", two=2)
            free_x = [[], []]
            free_xb = [[], []]
            free_st = [[], []]
            free_cs = [[], []]
            bk = {"i": 0}

            def nextbank():
                i = bk["i"] % 8
                bk["i"] += 1
                return i

            def proj(n, tp, wmat, col0, ncols, c1tok):
                q = nextbank()
                t = None
                for c in range(8):
                    t = mm(B.ap[q][:n, 0:ncols], xbB[tp][:, c, 0:n], wmat[:, c, col0:col0 + ncols], c == 0, c == 7,
                           waits=([c1tok] + B.waits(q)) if c == 0 else (), inc=(c == 7))
                return q, t

            def transposes_out(src_bf, n, nblk, stage, st_waits, src_waits):
                toks = []
                for k0 in range(0, nblk, 4):
                    q = nextbank()
                    kk = min(4, nblk - k0)
                    t = None
                    for k in range(kk):
                        t = mm(B.ap[q][:, k * 128:k * 128 + n], src_bf[:n, (k0 + k) * 128:(k0 + k + 1) * 128], ident[:n, :n], True, True,
                               waits=(B.waits(q) + list(src_waits)) if k == 0 else (), inc=(k == kk - 1))
                    ev = P.op("act", lambda e, q=q, k0=k0, kk=kk, n=n: e.activation(
                        out=stage[:, k0:k0 + kk, 0:n], in_=B.ap[q][:, 0:kk * 128].rearrange("p (h t) -> p h t", h=kk)[:, :, 0:n], func=AF.Copy),
                        waits=[t] + (st_waits if k0 == 0 else []))
                    B.release(q, [ev])
                    toks.append(ev)
                return toks

            nwt = int(stop_after[2:]) if (stop_after or "").startswith("Bn") else 21
            for wt in range(nwt):
                tp = wt % 2
                n = 128 if wt < 20 else 16
                own = 2 <= wt < 18
                ti = wt - 2
                tok0 = wt * 128
                l1 = P.dma("sp", "lB%d" % tp, lambda e, tp=tp, n=n, tok0=tok0: e.dma_start(out=xtB[tp][:, :, 0:n], in_=xTwv[:, :, tok0:tok0 + n]),
                           waits=free_x[tp])
                l2 = None
                lb = l1
                if own:
                    l2 = P.dma("sp", "lB%d" % tp, lambda e, tp=tp, ti=ti: e.dma_start(out=csB[tp][:, :], in_=csq[ti * 128:(ti + 1) * 128, :]),
                               waits=free_cs[tp])
                    lb = l2
                c1 = P.op("act", lambda e, tp=tp, n=n: e.activation(out=xbB[tp][:, :, 0:n], in_=xtB[tp][:, :, 0:n], func=AF.Copy),
                          waits=[lb] + free_xb[tp])
                c2 = P.op("pool", lambda e, tp=tp, n=n: e.tensor_tensor(out=xsqB[tp][:, :, 0:n], in0=xtB[tp][:, :, 0:n], in1=xtB[tp][:, :, 0:n],
                                                                        op=ALU.mult), waits=[lb] + free_xb[tp])
                qs = nextbank()
                tk = None
                for c in range(8):
                    tk = mm(B.ap[qs][:n, 0:1], xsqB[tp][:, c, 0:n], ones[:, 0:1], c == 0, c == 7, waits=([c2] + B.waits(qs)) if c == 0 else (),
                            inc=(c == 7))
                st = stB[tp]
                (r1, r1sq), d = rstd_from_ss(B.ap[qs][:n, 0:1], n, st, [tk])
                B.release(qs, [d])
                stage_toks = []
                for half in range(2):
                    q, t = proj(n, tp, wnk, half * 512, 512, c1)
                    dk = headnorm(B.ap[q][:n, :].rearrange("p (h d) -> p h d", h=8), n, 8, 64,
                                  nkh[tp][:n, half * 512:(half + 1) * 512].rearrange("p (h d) -> p h d", h=8),
                                  sqB[half][:n, :].rearrange("p (h d) -> p h d", h=8), tmB[half][:n, :].rearrange("p (h d) -> p h d", h=8),
                                  stH[half], (r1, r1sq), nakw_t, None, [t, d], key="h%d" % half)
                    B.release(q, [dk])
                evs = transposes_out(nkh[tp], n, 8, NKst[tp], free_st[tp], [dk])
                stage_toks += evs
                for half in range(2):
                    q, t = proj(n, tp, wnv, half * 512, 512, c1)
                    av = P.op("act", lambda e, q=q, n=n, tp=tp, half=half, r1=r1: e.activation(
                        out=NVst[tp][:n, half * 8:(half + 1) * 8, 0:64], in_=B.ap[q][:n, :].rearrange("p (h d) -> p h d", h=8),
                        func=AF.Copy, scale=r1), waits=[t, d] + free_st[tp])
                    B.release(q, [av])
                    stage_toks.append(av)
                last_pe_x = None
                if own:
                    q, t = proj(n, tp, wq, 0, 256, c1)
                    dq = headnorm(B.ap[q][:n, 0:256].rearrange("p (h d) -> p h d", h=1), n, 1, 256, qlb[tp][:n, :].rearrange("p (h d) -> p h d", h=1),
                                  sqB[0][:n, 0:256].rearrange("p (h d) -> p h d", h=1), None, stH[2], (r1, r1sq), None, None, [t, d], key="h0")
                    B.release(q, [dq])
                    q = nextbank()
                    mm(B.ap[q][:, 0:128], qlb[tp][:, 0:128], ident[:, :], True, True, waits=[dq] + B.waits(q))
                    t = mm(B.ap[q][:, 128:256], qlb[tp][:, 128:256], ident[:, :], True, True, inc=True)
                    eq = P.op("dve", lambda e, q=q, tp=tp: e.tensor_copy(out=qlT[tp][:, :, :], in_=B.ap[q][:, 0:256].rearrange("p (c t) -> p c t", c=2)),
                              waits=[t])
                    B.release(q, [eq])
                    for gq in range(4):
                        q = nextbank()
                        mm(B.ap[q][:, 0:384], qlT[tp][:, 0, :], wuq[:, 0, gq * 384:(gq + 1) * 384], True, False, waits=[eq] + B.waits(q))
                        t = mm(B.ap[q][:, 0:384], qlT[tp][:, 1, :], wuq[:, 1, gq * 384:(gq + 1) * 384], False, True, inc=True)
                        v3 = B.ap[q][:, 0:384].rearrange("p (h d) -> p h d", h=2)
                        d1 = headnorm(v3[:, :, 0:128], 128, 2, 128, qnb[tp][:, 2 * gq:2 * gq + 2, :], sqB[0][:, 0:256].rearrange("p (h d) -> p h d", h=2),
                                      tmB[0][:, 0:256].rearrange("p (h d) -> p h d", h=2), stH[3], None, qkw_t, None, [t], key="h0")
                        d2 = headnorm(v3[:, :, 128:192], 128, 2, 64, qpb[tp][:, 2 * gq:2 * gq + 2, :], sqB[1][:, 0:128].rearrange("p (h d) -> p h d", h=2),
                                      tmB[1][:, 0:128].rearrange("p (h d) -> p h d", h=2), stH[4], None, qpew_t, csB[tp], [t, l2, d1],
                                      tmp2=tm2[1][:, 0:128].rearrange("p (h d) -> p h d", h=2), key="h1")
                        B.release(q, [d1, d2])
                    free_cs[tp] = [d2]
                    evs = transposes_out(qnb[tp][:, :, :].rearrange("p h d -> p (h d)"), 128, 8, QnSt[tp], free_st[tp], [d1, d2])
                    stage_toks += evs
                    evs = transposes_out(qpb[tp][:, :, :].rearrange("p h d -> p (h d)"), 128, 4, QpSt[tp], free_st[tp], [d1, d2])
                    stage_toks += evs
                    for (wmat, gst) in ((wgm, GMst), (wgn, GNst)):
                        for half in range(2):
                            q, t = proj(n, tp, wmat, half * 512, 512, c1)
                            ag = P.op("act", lambda e, q=q, tp=tp, half=half, gst=gst, r1=r1: e.activation(
                                out=gst[tp][:, half * 512:(half + 1) * 512], in_=B.ap[q][:, :], func=AF.Silu, scale=r1),
                                waits=[t, d] + free_st[tp])
                            B.release(q, [ag])
                            stage_toks.append(ag)
                    for half in range(2):
                        q, t = proj(n, tp, wnq, half * 512, 512, c1)
                        dk = headnorm(B.ap[q][:, :].rearrange("p (h d) -> p h d", h=8), 128, 8, 64,
                                      nqh[tp][:, half * 512:(half + 1) * 512].rearrange("p (h d) -> p h d", h=8),
                                      sqB[half][:, :].rearrange("p (h d) -> p h d", h=8), tmB[half][:, :].rearrange("p (h d) -> p h d", h=8),
                                      stH[5 + half], (r1, r1sq), naqw_t, None, [t, d], key="h%d" % half)
                        B.release(q, [dk])
                        last_pe_x = t
                    evs = transposes_out(nqh[tp], 128, 8, NQst[tp], free_st[tp], [dk])
                    stage_toks += evs
                else:
                    last_pe_x = t
                free_x[tp] = [c1, c2]
                free_xb[tp] = [last_pe_x] if last_pe_x is not None else []
                s = P.dma("pool", "sB%d" % tp, lambda e, tp=tp, n=n, tok0=tok0: e.dma_start(out=NKTv[:, :, tok0:tok0 + n], in_=NKst[tp][:, :, 0:n]),
                          waits=stage_toks)
                s = P.dma("pool", "sB%d" % tp, lambda e, tp=tp, n=n, tok0=tok0: e.dma_start(out=NV_s[tok0:tok0 + n, :, :], in_=NVst[tp][0:n, :, :]))
                if own:
                    q0 = ti * 128
                    s = P.dma("pool", "sB%d" % tp, lambda e, tp=tp, q0=q0: e.dma_start(out=QTnv[:, :, q0:q0 + 128], in_=QnSt[tp][:, :, :]))
                    s = P.dma("pool", "sB%d" % tp, lambda e, tp=tp, q0=q0: e.dma_start(out=QTpv[:, :, q0:q0 + 128], in_=QpSt[tp][:, :, :]))
                    s = P.dma("pool", "sB%d" % tp, lambda e, tp=tp, q0=q0: e.dma_start(out=NQTv[:, :, q0:q0 + 128], in_=NQst[tp][:, :, :]))
                    s = P.dma("pool", "sB%d" % tp, lambda e, tp=tp, q0=q0: e.dma_start(out=GM_s[q0:q0 + 128, :], in_=GMst[tp][:, :]))
                    s = P.dma("pool", "sB%d" % tp, lambda e, tp=tp, q0=q0: e.dma_start(out=GN_s[q0:q0 + 128, :], in_=GNst[tp][:, :]))
                free_st[tp] = [s]
            P.barrier()

        if stop_after not in ("A", "B") and not (stop_after or "").startswith("Bn"):
            ab.reset(); af.reset()
            NKA = [ab.take(WALL) for _ in range(2)]
            NQA = [ab.take(OWN) for _ in range(2)]
            NVall = ab.take(21 * 16 * 65).rearrange("p (j h e) -> p j h e", j=21, h=16)
            GNall = ab.take(16 * 1024).rearrange("p (j f) -> p j f", j=16)
            PTc = [ab.take(512) for _ in range(4)]
            ypair = ab.take(16 * 128).rearrange("p (u f) -> p u f", u=16)
            YTst = ab.take(OWN)
            EB = [af.take(1024) for _ in range(2)]
            Ef = [af.take(512) for _ in range(2)]
            rdn = af.take(16)
            mstage = af.take(WTOK)
            pieces = na_pieces()
            lA = P.dma("sp", "m0", lambda e: e.dma_start(out=mstage[64:104, 0:WTOK], in_=konehot))
            for b_ in range(2):
                P.op("dve", lambda e, b_=b_: e.memset(NKA[b_][64:128, :], 0.0))
                P.op("dve", lambda e, b_=b_: e.memset(NQA[b_][64:128, :], 0.0))
                lk = P.op("dve", lambda e, b_=b_: e.tensor_copy(out=NKA[b_][64:104, 0:WTOK], in_=mstage[64:104, 0:WTOK]), waits=[lA])
            lB = P.dma("sp", "m1", lambda e: e.dma_start(out=mstage[64:104, 0:OWN], in_=qmask), waits=[lk])
            for b_ in range(2):
                lq = P.op("dve", lambda e, b_=b_: e.tensor_copy(out=NQA[b_][64:104, :], in_=mstage[64:104, 0:OWN]), waits=[lB])
            lv = P.dma("sp", "m2", lambda e: e.dma_start(out=NVall[:, :, :, :], in_=NV_s.rearrange("(j p) h e -> p j h e", p=128)))
            lg = P.dma("sp", "m3", lambda e: e.dma_start(out=GNall[:, :, :], in_=GN_s.rearrange("(j p) f -> p j f", p=128)))
            accb = [0, 1, 2]

            def acc_ap(u):
                return B.ap[accb[u // 7]][:, (u % 7) * 65:(u % 7) * 65 + 65]

            free_nk = [[], []]
            free_eb = [[], []]
            free_pt = [[], [], [], []]
            free_ef = [[], []]
            free_yp = []
            free_yst = []
            ucnt = 0
            for h in range(NH):
                b_ = h % 2
                lk = P.dma("sp", "lC%d" % b_, lambda e, b_=b_, h=h: e.dma_start(out=NKA[b_][0:64, :], in_=NKT_s[h]), waits=free_nk[b_])
                lq2 = P.dma("sp", "lC%d" % b_, lambda e, b_=b_, h=h: e.dma_start(out=NQA[b_][0:64, :], in_=NQT_s[h]))
                le = P.dma("sp", "lC%d" % b_, lambda e, b_=b_, h=h: e.dma_start(out=EB[b_][:, :], in_=ebias[h]), waits=free_eb[b_])
                ae = P.op("act", lambda e, b_=b_: e.activation(out=EB[b_][:, :], in_=EB[b_][:, :], func=AF.Exp), waits=[le])
                z = None
                for k in range(3):
                    z = P.op("dve", lambda e, k=k: e.memset(B.ap[accb[k]][:, 0:455], 0.0), waits=B.waits(accb[k]))
                pv_last = None
                first = True
                pendc = []

                def issue_pv_c(item):
                    (m_, q0_, q1_, pi_, ready_, nk_) = item
                    t_ = None
                    for u in range(q0_ // 128, q1_ // 128):
                        o = u * 128 - q0_
                        t_ = mm(acc_ap(u), PTc[pi_][0:nk_, o:o + 128], NVall[0:nk_, m_, h, :], False, False,
                                waits=[ready_] if u == q0_ // 128 else (), inc=(u == q1_ // 128 - 1), skip=True)
                    free_pt[pi_] = [t_]
                    return t_

                for m in range(21):
                    nk = 128 if m < 20 else 16
                    pcs = pieces[m] if m < 20 else [(0, 512), (512, 1024), (1024, 1536), (1536, 2048)]
                    for (q0, q1) in pcs:
                        nq = q1 - q0
                        sb_ = (3, 4, 7)[ucnt % 3]
                        pS = B.ap[sb_]
                        pi = ucnt % 4
                        fi = ucnt % 2
                        ucnt += 1
                        w0 = B.waits(sb_) + ([le, lq, lv, z] if first else [])
                        first = False
                        if m < 20:
                            ts = mm(pS[:, 0:nq], NKA[b_][:, m * 128:(m + 1) * 128], NQA[b_][:, q0:q1], True, True, waits=w0, inc=True)
                            a1 = P.op("act", lambda e, fi=fi, nq=nq, pS=pS: e.activation(out=Ef[fi][:, 0:nq], in_=pS[:, 0:nq], func=AF.Exp, scale=NA_SCALE),
                                      waits=[ts] + free_ef[fi])
                            rel0 = (q0 // 64) - 2 * m + 11
                            d1 = P.op("dve", lambda e, fi=fi, pi=pi, nq=nq, rel0=rel0, b_=b_: e.tensor_tensor(
                                out=PTc[pi][:, 0:nq], in0=Ef[fi][:, 0:nq], in1=EB[b_][:, rel0 * 64:rel0 * 64 + nq], op=ALU.mult),
                                waits=[a1, ae] + free_pt[pi])
                            B.release(sb_, [a1])
                            free_ef[fi] = [d1]
                            ready = d1
                        else:
                            ts = mm(pS[0:16, 0:nq], NKA[b_][:, WTOK:WALL], NQA[b_][:, q0:q1], True, True, waits=w0, inc=True)
                            a1 = P.op("act", lambda e, pi=pi, nq=nq, pS=pS, h=h: e.activation(
                                out=PTc[pi][0:16, 0:nq], in_=pS[0:16, 0:nq], func=AF.Exp, scale=NA_SCALE, bias=metab_t[0:16, h:h + 1]),
                                waits=[ts] + free_pt[pi])
                            B.release(sb_, [a1])
                            ready = a1
                        pendc.append((m, q0, q1, pi, ready, nk))
                        if len(pendc) > 2:
                            pv_last = issue_pv_c(pendc.pop(0))
                while pendc:
                    pv_last = issue_pv_c(pendc.pop(0))
                free_nk[b_] = [pv_last]
                free_eb[b_] = [pv_last]
                ev = None
                for u in range(16):
                    a = acc_ap(u)
                    r = P.op("dve", lambda e, a=a, u=u: e.reciprocal(out=rdn[:, u:u + 1], in_=a[:, 64:65]), waits=[pv_last] + (free_yp if (u == 0 and h % 2 == 0) else []))
                    ev = P.op("dve", lambda e, a=a, u=u, h=h: e.scalar_tensor_tensor(
                        out=ypair[:, u, (h % 2) * 64:(h % 2) * 64 + 64], in0=a[:, 0:64], scalar=rdn[:, u:u + 1],
                        in1=GNall[:, u, h * 64:(h + 1) * 64], op0=ALU.mult, op1=ALU.mult), waits=[r, lg])
                for k in range(3):
                    B.release(accb[k], [ev])
                if h % 2 == 1:
                    hp = h // 2
                    tt = None
                    evs = []
                    for u4 in range(4):
                        q = 5 + (u4 % 2)
                        for k in range(4):
                            u = u4 * 4 + k
                            tt = mm(B.ap[q][:, k * 128:(k + 1) * 128], ypair[:, u, :], ident[:, :], True, True,
                                    waits=([ev] + B.waits(q)) if k == 0 else (), inc=(k == 3))
                        e2 = P.op("act", lambda e, q=q, u4=u4: e.activation(out=YTst[:, u4 * 512:(u4 + 1) * 512], in_=B.ap[q][:, :], func=AF.Copy),
                                  waits=[tt] + (free_yst if u4 == 0 else []))
                        B.release(q, [e2])
                        evs.append(e2)
                    free_yp = [tt]
                    s = P.dma("pool", "st2", lambda e, hp=hp: e.dma_start(out=YT_s[1024 + hp * 128:1024 + (hp + 1) * 128, :], in_=YTst[:, :]), waits=evs)
                    free_yst = [s]
            P.barrier()

        if stop_after not in ("A", "B", "C") and not (stop_after or "").startswith("Bn"):
            ab.reset(); af.reset()
            KTh = ab.take(TK)
            kpeT = ab.take(TK)
            Vh = ab.take(NT * 129).rearrange("p (j e) -> p j e", j=NT)
            Qn = [ab.take(OWN) for _ in range(2)]
            Qp = [ab.take(OWN) for _ in range(2)]
            GMh = [ab.take(16 * 128).rearrange("p (j f) -> p j f", j=16) for _ in range(2)]
            PTd = [ab.take(512) for _ in range(4)]
            ybf = [ab.take(128) for _ in range(2)]
            YTd = [ab.take(1024) for _ in range(2)]
            rdd = af.take(16)
            P.dma("sp", "m0", lambda e: e.dma_start(out=kpeT[0:64, 0:SEQ].rearrange("p (r t) -> p r t", r=NCORES),
                                                    in_=KPE_all.rearrange("(r k) t -> k r t", k=64)), waits=[cc_tok])
            lp = P.dma("sp", "m0", lambda e: e.dma_start(out=kpeT[0:64, SEQ:TK], in_=KPE_m))
            zp = P.op("pool", lambda e: e.memset(kpeT[64:128, :], 0.0))
            for hb_ in range(2):
                zp = P.op("pool", lambda e, hb_=hb_: e.memset(Qp[hb_][64:128, :], 0.0))
            NGK = 8
            gtiles = [list(range(16 * g, 16 * g + 16)) for g in range(NGK)]
            gtiles[7].append(128)
            accb = [0, 1, 2]

            def accd(i):
                return B.ap[accb[i // 3]][:, (i % 3) * 129:(i % 3) * 129 + 129]

            kv_free = [[] for _ in range(NGK)]
            kv_ld = [None] * NGK
            q_free = [[], []]
            free_ptd = [[], [], [], []]
            free_yb = [[], []]
            free_ytd = [[], []]
            gm_free = [[], []]
            GMv = GM_s.rearrange("(j p) f -> p j f", p=128)
            ucnt = 0
            pcount = 0
            for h in range(H):
                hb = h % 2
                lqn = P.dma("sp", "lD%d" % hb, lambda e, hb=hb, h=h: e.dma_start(out=Qn[hb][:, :], in_=QT_s[h]), waits=q_free[hb])
                lqp = P.dma("sp", "lD%d" % hb, lambda e, hb=hb, h=h: e.dma_start(out=Qp[hb][0:64, :], in_=QPT_s[h]))
                lgm = P.dma("sp", "lD%d" % hb, lambda e, hb=hb, h=h: e.dma_start(out=GMh[hb][:, :, :], in_=GMv[:, :, h * 128:(h + 1) * 128]), waits=gm_free[hb])
                for g in range(NGK):
                    r0 = g * H * 128 + h * 128
                    P.dma("sp", "kv%d" % g, lambda e, g=g, r0=r0: e.dma_start(out=KTh[:, g * OWN:(g + 1) * OWN], in_=KT_all[r0:r0 + 128, :]),
                          waits=kv_free[g] + [cc_tok])
                    kv_ld[g] = P.dma("sp", "kv%d" % g, lambda e, g=g, r0=r0: e.dma_start(
                        out=Vh[:, 16 * g:16 * g + 16, :], in_=V_all[r0:r0 + 128, :].rearrange("p (j e) -> p j e", e=129)))
                    if g == NGK - 1:
                        P.dma("sp", "kv%d" % g, lambda e, h=h: e.dma_start(out=KTh[:, SEQ:TK], in_=KT_m[h]))
                        kv_ld[g] = P.dma("sp", "kv%d" % g, lambda e, h=h: e.dma_start(out=Vh[0:16, 128, :], in_=V_m[h, 0:16, 0, :]))
                for qc in range(2):
                    z = None
                    for k in range(3):
                        z = P.op("dve", lambda e, k=k: e.memset(B.ap[accb[k]][:, 0:387], 0.0), waits=B.waits(accb[k]))
                    units = [(j, s) for j in range(NT) for s in range(2)]
                    pend = []

                    def issue_pv(item):
                        (j, s, pi, ex, nk) = item
                        t = None
                        for i in range(4):
                            t = mm(accd(4 * s + i), PTd[pi][0:nk, i * 128:(i + 1) * 128], Vh[0:nk, j, :], False, False,
                                   waits=[ex, z] if i == 0 else (), inc=(i == 3), skip=True)
                        free_ptd[pi] = [t]
                        if qc == 1 and s == 1 and (j % 16 == 15 or j == 128) and not (j == 127):
                            kv_free[min(j // 16, 7)] = [t]
                        return t

                    pv_last = None
                    for (j, s) in units:
                        nk = 128 if j < 128 else 16
                        g = min(j // 16, 7)
                        sb_ = 3 + (ucnt % 3)
                        pi = ucnt % 4
                        ucnt += 1
                        pS = B.ap[sb_]
                        q0 = qc * 1024 + s * 512
                        w0 = B.waits(sb_)
                        if s == 0 and (j % 16 == 0) and j < 128:
                            w0 = w0 + [kv_ld[g]]
                        if j == 0 and s == 0:
                            w0 = w0 + [lgm, lp, zp]
                        mm(pS[0:nk, :], KTh[:, j * 128:j * 128 + nk], Qn[hb][:, q0:q0 + 512], True, False, waits=w0)
                        ts = mm(pS[0:nk, :], kpeT[:, j * 128:j * 128 + nk], Qp[hb][:, q0:q0 + 512], False, True, inc=True)
                        ex = P.op("act", lambda e, pi=pi, nk=nk, pS=pS: e.activation(out=PTd[pi][0:nk, :], in_=pS[0:nk, :], func=AF.Exp, scale=MLA_SCALE),
                                  waits=[ts] + free_ptd[pi])
                        B.release(sb_, [ex])
                        pend.append((j, s, pi, ex, nk))
                        if len(pend) > 2:
                            pv_last = issue_pv(pend.pop(0))
                    while pend:
                        pv_last = issue_pv(pend.pop(0))
                    if qc == 1:
                        q_free[hb] = [pv_last]
                    for i in range(8):
                        a = accd(i)
                        u = qc * 8 + i
                        yb = ybf[i % 2]
                        r = P.op("dve", lambda e, a=a, i=i: e.reciprocal(out=rdd[:, i:i + 1], in_=a[:, 128:129]), waits=[pv_last])
                        ev = P.op("dve", lambda e, a=a, i=i, u=u, yb=yb, hb=hb: e.scalar_tensor_tensor(
                            out=yb[:, :], in0=a[:, 0:128], scalar=rdd[:, i:i + 1], in1=GMh[hb][:, u, :], op0=ALU.mult, op1=ALU.mult),
                            waits=[r, lgm] + free_yb[i % 2])
                        q = 6 + (i // 4) % 2
                        if i % 4 == 0:
                            wq_ = B.waits(q)
                        tt = mm(B.ap[q][:, (i % 4) * 128:(i % 4 + 1) * 128], yb[:, :], ident[:, :], True, True,
                                waits=[ev] + (wq_ if i % 4 == 0 else []), inc=True)
                        free_yb[i % 2] = [tt]
                        if i % 4 == 3:
                            yk = pcount % 2
                            e2 = P.op("act", lambda e, q=q, yk=yk, i=i: e.activation(
                                out=YTd[yk][:, (i // 4) * 512:(i // 4 + 1) * 512], in_=B.ap[q][:, :], func=AF.Copy),
                                waits=[tt] + (free_ytd[yk] if i == 3 else []))
                            B.release(q, [e2])
                            if i == 7:
                                s_ = P.dma("pool", "sD%d" % yk, lambda e, yk=yk, h=h, qc=qc: e.dma_start(
                                    out=YT_s[h * 128:(h + 1) * 128, qc * 1024:(qc + 1) * 1024], in_=YTd[yk][:, :]), waits=[e2])
                                free_ytd[yk] = [s_]
                    for k in range(3):
                        B.release(accb[k], [ev])
                    if qc == 1:
                        gm_free[hb] = [ev]
                    pcount += 1
            P.barrier()

        if stop_after is None:
            ab.reset(); af.reset()
            wst = [af.take(2048), af.take(2048)]
            wst_state["free"] = [[], []]
            wo = ab.take(16 * 1024).rearrange("p (c f) -> p c f", c=16)
            yT = ab.take(16 * OWN).rearrange("p (c t) -> p c t", c=16)
            xo = [af.take(1024) for _ in range(2)]
            oo = [af.take(1024) for _ in range(2)]
            ly = P.dma("sp", "m1", lambda e: e.dma_start(out=yT[:, :, :], in_=YT_s.rearrange("(c p) t -> p c t", p=128)))
            tw = load_w(wo, w_out, 16, 0, 1024, None, wst)
            free_xo = [[], []]
            free_oo = [[], []]
            fin = None
            for ti in range(16):
                tp = ti % 2
                lx = P.dma("sp", "lE%d" % tp, lambda e, tp=tp, ti=ti: e.dma_start(out=xo[tp][:, :], in_=xown[ti * 128:(ti + 1) * 128, :]), waits=free_xo[tp])
                for half in range(2):
                    q = (ti * 2 + half) % 8
                    t = None
                    for c in range(16):
                        t = mm(B.ap[q][:, :], yT[:, c, ti * 128:(ti + 1) * 128], wo[:, c, half * 512:(half + 1) * 512], c == 0, c == 15,
                               waits=([ly, tw] + B.waits(q)) if c == 0 else (), inc=(c == 15))
                    a = P.op("dve", lambda e, q=q, tp=tp, half=half: e.tensor_tensor(
                        out=oo[tp][:, half * 512:(half + 1) * 512], in0=B.ap[q][:, :], in1=xo[tp][:, half * 512:(half + 1) * 512], op=ALU.add),
                        waits=[t, lx] + (free_oo[tp] if half == 0 else []))
                    B.release(q, [a])
                free_xo[tp] = [a]
                fin = P.dma("pool", "fin%d" % tp, lambda e, tp=tp, ti=ti: e.dma_start(out=out[ti * 128:(ti + 1) * 128, :], in_=oo[tp][:, :]), waits=[a])
                free_oo[tp] = [fin]

        fin_waits = [(s, P.cnt[s]) for s in P.dma_sems if s in P.cnt]
        P._push("pool", lambda e: e.memset(identf[:, 0:1], 0.0), fin_waits + [(s, v) for s, v in P.cnt.items() if s in Prog.CE], None, 1)

        for nme in P.cnt:
            P.sems[nme] = es.enter_context(nc.semaphore(nme))
        block = es.enter_context(nc.Block())

        @block.sync
        def _(e):
            P.replay("sp", e)

        @block.tensor
        def _(e):
            P.replay("pe", e)

        @block.scalar
        def _(e):
            P.replay("act", e)

        @block.vector
        def _(e):
            P.replay("dve", e)

        @block.gpsimd
        def _(e):
            P.replay("pool", e)
    return nc


def _rope_table(pos):
    inv_freq = (10000.0 ** (-(np.arange(0, 64, 2, dtype=np.float32) / 64))).astype(np.float32)
    ang = pos.astype(np.float32)[:, None] * inv_freq[None, :]
    c, s = np.cos(ang).astype(np.float32), np.sin(ang).astype(np.float32)
    return np.ascontiguousarray(np.concatenate([c, c, -s, s], axis=1))


def prepare_inputs(x, meta_tokens, norm_w, w_in, q_lat_norm_w, kv_lat_norm_w, w_uq, w_ukv,
                   mla_qn_w, mla_qpe_w, mla_kn_w, mla_kpe_w, na_q_norm_w, na_k_norm_w,
                   na_rel_bias, na_meta_bias, w_out):
    f = lambda a: np.ascontiguousarray(np.asarray(a, dtype=np.float32))
    x = f(x)[0]
    meta = f(meta_tokens)
    csm = _rope_table(np.arange(NMETA))
    rb = f(na_rel_bias)[0]
    cols = np.arange(GRID_W)
    c0 = np.clip(cols - 8, 0, GRID_W - 16)
    eb = np.full((NH, 2, 64, 16, 64), NEG, np.float32)
    for jj in range(2):
        for irel in range(16):
            dr = jj + 7 - irel
            if not (-7 <= dr <= 7):
                continue
            for cq in range(64):
                ck = np.arange(c0[cq], c0[cq] + 16)
                eb[:, jj, ck, irel, cq] = rb[:, dr + 7, ck - cq + 15]
    eb = np.ascontiguousarray(eb.reshape(NH, 128, 1024))
    konehot = np.zeros((WROWS, WTOK), np.float32)
    for j in range(WROWS):
        konehot[j, j * 64:(j + 1) * 64] = 1.0
    shared = {
        "w_in": f(w_in)[0], "w_uq": f(w_uq)[0], "w_ukv": f(w_ukv)[0], "w_out": f(w_out)[0],
        "normw": np.ascontiguousarray(f(norm_w)[0].reshape(8, 128).T),
        "qlw": np.ascontiguousarray(f(q_lat_norm_w)[0].reshape(2, 128).T),
        "kvlw": np.ascontiguousarray(f(kv_lat_norm_w)[0].reshape(2, 128).T),
        "qnw": f(mla_qn_w)[0][None, :], "knw": f(mla_kn_w)[0][None, :],
        "qpew": f(mla_qpe_w)[0][None, :], "kpew": f(mla_kpe_w)[0][None, :],
        "naqw": f(na_q_norm_w)[0][None, :], "nakw": f(na_k_norm_w)[0][None, :],
        "csm": csm, "ebias": eb, "konehot": konehot,
        "metab": np.ascontiguousarray(f(na_meta_bias)[0].T),
    }
    in_maps = []
    for c in range(NCORES):
        xw = np.zeros((WALL, D), np.float32)
        for j in range(WROWS):
            gr = 32 * c - 4 + j
            if 0 <= gr < ROWS:
                xw[j * 64:(j + 1) * 64] = x[gr * 64:(gr + 1) * 64]
        xw[WTOK:] = meta
        qm = np.zeros((WROWS, OWN), np.float32)
        for i in range(32):
            for j in range(WROWS):
                if not na_valid(c, i, j):
                    qm[j, i * 64:(i + 1) * 64] = NEG
        m = dict(shared)
        m["xTw"] = np.ascontiguousarray(xw.T)
        m["xown"] = np.ascontiguousarray(x[c * OWN:(c + 1) * OWN])
        m["csq"] = _rope_table(np.arange(c * OWN, (c + 1) * OWN) + NMETA)
        m["qmask"] = qm
        in_maps.append(m)
    return in_maps


def kernel(**inputs):
    in_maps = prepare_inputs(**inputs)
    nc = build_program()
    res = run_bass_kernel_spmd(nc, in_maps, core_ids=list(range(NCORES)))
    outs = [np.asarray(r["out"], dtype=np.float32) for r in res.results]
    return np.concatenate(outs, axis=0)[None, :, :]
```

```python
import numpy as np
from contextlib import ExitStack
import concourse.bass as bass
import concourse.mybir as mybir
from concourse.bass_utils import run_bass_kernel_spmd

F32 = mybir.dt.float32
BF16 = mybir.dt.bfloat16
AF = mybir.ActivationFunctionType
ALU = mybir.AluOpType
AX = mybir.AxisListType

NCORES = 8
D = 1024
SEQ = 16384
NMETA = 16
TK = SEQ + NMETA
NT = 129
OWN = 2048
WROWS = 40
WTOK = WROWS * 64
WALL = WTOK + NMETA
H = 8
NH = 16
EPS = 1e-6
GRID_W = 64
ROWS = 256
NEG = -30000.0
MLA_SCALE = 192 ** -0.5
NA_SCALE = 0.125


class Prog:
    CE = ("pe", "act", "dve", "pool")
    SERIAL = True

    def __init__(self):
        self.ops = {e: [] for e in ("pe", "act", "dve", "pool", "sp")}
        self.sems = {}
        self.cnt = {}
        self.waited = {e: {} for e in self.ops}
        self.pending = {e: [] for e in self.ops}
        self.dma_sems = []

    def _push(self, eng, fn, waits, inc, n):
        ws = list(self.pending[eng]) + [w for w in waits if w is not None]
        self.pending[eng] = []
        if self.SERIAL and eng in ("act", "dve", "pool") and self.cnt.get(eng, 0) > 0:
            ws.append((eng, self.cnt[eng]))
        mx = {}
        for (s, v) in ws:
            if v > mx.get(s, 0):
                mx[s] = v
        fw = []
        for s, v in mx.items():
            if self.waited[eng].get(s, 0) >= v:
                continue
            self.waited[eng][s] = v
            fw.append((s, v))
        tok = None
        if inc is not None:
            self.cnt[inc] = self.cnt.get(inc, 0) + n
            tok = (inc, self.cnt[inc])
        self.ops[eng].append((fn, tuple(fw), inc, n))
        return tok

    def op(self, eng, fn, waits=(), inc=True):
        return self._push(eng, fn, waits, eng if inc else None, 1)

    def dma(self, eng, sem, fn, waits=()):
        if sem not in self.dma_sems:
            self.dma_sems.append(sem)
        return self._push(eng, fn, waits, sem, 16)

    def barrier(self, with_cc=False):
        toks = [(s, v) for s, v in self.cnt.items() if (s != "cc" or with_cc)]
        for e in self.pending:
            self.pending[e] = list(toks)

    def replay(self, eng, e):
        for fn, waits, inc, n in self.ops[eng]:
            for (s, v) in waits:
                e.wait_ge(self.sems[s], v)
            ins = fn(e)
            if inc is not None:
                ins.then_inc(self.sems[inc], n)


class Arena:
    def __init__(self, t, cols):
        self.t, self.cols, self.off = t, cols, 0

    def reset(self):
        self.off = 0

    def take(self, n):
        a = self.off
        self.off += (n + 7) // 8 * 8
        assert self.off <= self.cols, (self.off, self.cols)
        return self.t[:, a:a + n]


class Banks:
    def __init__(self, aps):
        self.ap = aps
        self.free = [[] for _ in aps]

    def waits(self, i):
        return list(self.free[i])

    def release(self, i, toks):
        self.free[i] = [t for t in toks if t is not None]


def na_valid(c, i, j):
    r = 32 * c + i
    kr = 32 * c - 4 + j
    r0 = min(max(r - 4, 0), ROWS - 8)
    return (0 <= kr < ROWS) and (r0 <= kr < r0 + 8)


def na_pieces():
    out = []
    for m in range(20):
        rows = [i for i in range(32) if any(na_valid(c, i, j) for c in range(NCORES) for j in (2 * m, 2 * m + 1))]
        lo, hi = min(rows) & ~1, max(rows) | 1
        assert 0 <= lo - 2 * m + 11 and hi - 2 * m + 11 <= 15, (m, lo, hi)
        pcs = []
        r = lo
        while r <= hi:
            r2 = min(r + 8, hi + 1)
            pcs.append((r * 64, r2 * 64))
            r = r2
        out.append(pcs)
    return out


def build_program(dbg=False, stop_after=None):
    nc = bass.Bass("TRN2", target_bir_lowering=False)

    def din(name, shape, dt=F32):
        return nc.dram_tensor(name, list(shape), dt, kind="ExternalInput").ap()

    xTw = din("xTw", [D, WALL])
    xown = din("xown", [OWN, D])
    w_in = din("w_in", [D, 5696])
    w_uq = din("w_uq", [256, 1536])
    w_ukv = din("w_ukv", [256, 2048])
    w_out = din("w_out", [2048, D])
    normw = din("normw", [128, 8])
    qlw = din("qlw", [128, 2])
    kvlw = din("kvlw", [128, 2])
    qnw = din("qnw", [1, 128])
    knw = din("knw", [1, 128])
    qpew = din("qpew", [1, 64])
    kpew = din("kpew", [1, 64])
    naqw = din("naqw", [1, 64])
    nakw = din("nakw", [1, 64])
    csm = din("csm", [NMETA, 128])
    csq = din("csq", [OWN, 128])
    ebias = din("ebias", [NH, 128, 1024])
    qmask = din("qmask", [WROWS, OWN])
    konehot = din("konehot", [WROWS, WTOK])
    metab = din("metab", [NMETA, NH])
    out = nc.dram_tensor("out", [OWN, D], F32, kind="ExternalOutput").ap()

    skind = "ExternalOutput" if dbg else "Internal"

    def scratch(name, shape, dt=BF16):
        return nc.dram_tensor(name, list(shape), dt, kind=skind).ap()

    def cscratch(name, shape, dt=BF16):
        return nc.dram_tensor(name, list(shape), dt).ap()

    KT_part = cscratch("KT_part", [H * 128 + 64, OWN])
    V_part = cscratch("V_part", [H * 128, 16 * 129])
    KPE_part = cscratch("KPE_part", [64, OWN])
    KT_all = cscratch("KT_all", [NCORES * (H * 128 + 64), OWN])
    V_all = cscratch("V_all", [NCORES * H * 128, 16 * 129])
    KPE_all = cscratch("KPE_all", [NCORES * 64, OWN])
    KT_m = scratch("KT_m", [H, 128, NMETA])
    V_m = scratch("V_m", [H, 128, 1, 129])
    KPE_m = scratch("KPE_m", [64, NMETA])
    QT_s = scratch("QT_s", [H, 128, OWN])
    QPT_s = scratch("QPT_s", [H, 64, OWN])
    GM_s = scratch("GM_s", [OWN, 1024])
    GN_s = scratch("GN_s", [OWN, 1024])
    NQT_s = scratch("NQT_s", [NH, 64, OWN])
    NKT_s = scratch("NKT_s", [NH, 64, WALL])
    NV_s = scratch("NV_s", [21 * 128, NH, 65])
    YT_s = scratch("YT_s", [2048, OWN])

    P = Prog()
    with ExitStack() as es:
        AB_COLS = 73728
        AF_COLS = 14848
        abt = es.enter_context(nc.sbuf_tensor("arena_bf", [128, AB_COLS], BF16))
        aft = es.enter_context(nc.sbuf_tensor("arena_f", [128, AF_COLS], F32))
        ident = es.enter_context(nc.sbuf_tensor("ident", [128, 128], BF16))
        identf = es.enter_context(nc.sbuf_tensor("identf", [128, 128], F32))
        ones = es.enter_context(nc.sbuf_tensor("ones", [128, 8], BF16))
        consts = es.enter_context(nc.sbuf_tensor("consts", [128, 1024], F32))
        pb = [es.enter_context(nc.psum_tensor("pb%d" % i, [128, 512], F32)) for i in range(8)]
        ab = Arena(abt, AB_COLS)
        af = Arena(aft, AF_COLS)
        B = Banks([p[:, :] for p in pb])

        qkw_t = consts[:, 0:128]
        qpew_t = consts[:, 384:448]
        kpew_t = consts[:, 448:512]
        naqw_t = consts[:, 512:576]
        nakw_t = consts[:, 576:640]
        normw_t = consts[:, 640:648]
        qlw_t = consts[:, 648:650]
        kvlw_t = consts[:, 650:652]
        metab_t = consts[:, 656:672]

        P.op("pool", lambda e: e.memset(identf[:], 0.0))
        i1 = P.op("pool", lambda e: e.affine_select(out=identf[:], in_=identf[:], pattern=[[-1, 128]],
                                                    compare_op=ALU.not_equal, fill=1.0, base=0, channel_multiplier=1))
        P.op("dve", lambda e: e.tensor_copy(out=ident[:], in_=identf[:]), waits=[i1])
        P.op("dve", lambda e: e.memset(ones[:], 1.0))
        lt = None
        for (dst, src) in [(consts[:, 128:256], qnw), (consts[:, 256:384], knw), (qpew_t, qpew), (kpew_t, kpew),
                           (naqw_t, naqw), (nakw_t, nakw)]:
            lt = P.dma("sp", "ld0", lambda e, dst=dst, src=src: e.dma_start(out=dst, in_=src.partition_broadcast(128)))
        for (dst, src) in [(normw_t, normw), (qlw_t, qlw), (kvlw_t, kvlw)]:
            lt = P.dma("sp", "ld0", lambda e, dst=dst, src=src: e.dma_start(out=dst, in_=src))
        lt = P.dma("sp", "ld0", lambda e: e.dma_start(out=metab_t[0:16, :], in_=metab))
        P.op("dve", lambda e: e.tensor_tensor(out=qkw_t, in0=consts[:, 128:256], in1=consts[:, 256:384], op=ALU.mult), waits=[lt])
        P.barrier()

        wst_state = {"k": 0, "free": [[], []]}

        def load_w(dst3, src2d, nch, c0, c1, rowscale, wst):
            last = None
            for ch in range(nch):
                for a in range(c0, c1, 2048):
                    b_ = min(a + 2048, c1)
                    k = wst_state["k"] % 2
                    wst_state["k"] += 1
                    stg = wst[k][:, 0:b_ - a]
                    t = P.dma("sp", "lw%d" % k, lambda e, stg=stg, ch=ch, a=a, b_=b_: e.dma_start(
                        out=stg, in_=src2d[ch * 128:(ch + 1) * 128, a:b_]), waits=wst_state["free"][k])
                    dsl = dst3[:, ch, a - c0:b_ - c0]
                    if rowscale is not None:
                        if k == 0:
                            t2 = P.op("dve", lambda e, dsl=dsl, stg=stg, ch=ch: e.tensor_scalar(
                                out=dsl, in0=stg, scalar1=rowscale[:, ch:ch + 1], scalar2=None, op0=ALU.mult), waits=[t])
                        else:
                            t2 = P.op("act", lambda e, dsl=dsl, stg=stg, ch=ch: e.activation(
                                out=dsl, in_=stg, func=AF.Copy, scale=rowscale[:, ch:ch + 1]), waits=[t])
                    else:
                        if k == 0:
                            t2 = P.op("dve", lambda e, dsl=dsl, stg=stg: e.tensor_copy(out=dsl, in_=stg), waits=[t])
                        else:
                            t2 = P.op("act", lambda e, dsl=dsl, stg=stg: e.activation(out=dsl, in_=stg, func=AF.Copy), waits=[t])
                    wst_state["free"][k] = [t2]
                    last = t2
                    P.pending["pe"] = [w for w in P.pending["pe"] if w[0] != t2[0]] + [t2]
            return last

        def mm(out_ap, lhsT, rhs, start, stop, waits=(), inc=False, skip=False):
            if skip:
                return P.op("pe", lambda e: e.matmul(out_ap, lhsT=lhsT, rhs=rhs, start=start, stop=stop, skip_group_check=True),
                            waits=waits, inc=inc)
            return P.op("pe", lambda e: e.matmul(out_ap, lhsT=lhsT, rhs=rhs, start=start, stop=stop), waits=waits, inc=inc)

        scr_free = {}

        def run(gen):
            try:
                while True:
                    next(gen)
            except StopIteration as ex_:
                return ex_.value

        def interleave(gens):
            gens = list(gens)
            while gens:
                for gi in list(gens):
                    try:
                        next(gi)
                    except StopIteration:
                        gens.remove(gi)

        def headnorm(*a, **k):
            return run(headnorm_g(*a, **k))

        def headnorm_g(src3, n, nh, hd, outp, sq, tmp, stt, pre, wtile, rope_cs, waits, tmp2=None, key=None):
            waits = list(waits) + scr_free.get(key, [])
            ss = stt[:n, 0:nh]
            sd = stt[:n, nh:2 * nh]
            rr = stt[:n, 2 * nh:3 * nh]
            a = P.op("act", lambda e: e.activation(out=sq, in_=src3, func=AF.Square), waits=waits)
            yield
            d = P.op("dve", lambda e: e.tensor_reduce(out=ss, in_=sq, axis=AX.X, op=ALU.add), waits=[a])
            if pre is not None:
                d = P.op("dve", lambda e: e.tensor_scalar(out=ss, in0=ss, scalar1=pre[1], scalar2=None, op0=ALU.mult), waits=[d])
            yield
            a = P.op("act", lambda e: e.activation(out=sd, in_=ss, func=AF.Sqrt, bias=EPS, scale=1.0 / hd), waits=[d])
            yield
            d = P.op("dve", lambda e: e.reciprocal(out=rr, in_=sd), waits=[a])
            if pre is not None:
                d = P.op("dve", lambda e: e.tensor_scalar(out=rr, in0=rr, scalar1=pre[0], scalar2=None, op0=ALU.mult), waits=[d])
            rb = rr.unsqueeze(2).to_broadcast([n, nh, hd])
            if wtile is None and rope_cs is None:
                tok = P.op("dve", lambda e: e.tensor_tensor(out=outp, in0=src3, in1=rb, op=ALU.mult), waits=[d])
                scr_free[key] = [tok]
                yield
                return tok
            d = P.op("dve", lambda e: e.tensor_tensor(out=tmp, in0=src3, in1=rb, op=ALU.mult), waits=[d])
            wb = wtile[:n, :].unsqueeze(1).to_broadcast([n, nh, hd])
            if rope_cs is None:
                tok = P.op("dve", lambda e: e.tensor_tensor(out=outp, in0=tmp, in1=wb, op=ALU.mult), waits=[d])
                scr_free[key] = [tok]
                yield
                return tok
            d = P.op("dve", lambda e: e.tensor_tensor(out=tmp, in0=tmp, in1=wb, op=ALU.mult), waits=[d])
            hh = hd // 2
            cc = rope_cs[:n, 0:hd].unsqueeze(1).to_broadcast([n, nh, hd])
            nsin = rope_cs[:n, hd:hd + hh].unsqueeze(1).to_broadcast([n, nh, hh])
            psin = rope_cs[:n, hd + hh:2 * hd].unsqueeze(1).to_broadcast([n, nh, hh])
            d1 = P.op("dve", lambda e: e.tensor_tensor(out=tmp2[:, :, 0:hh], in0=tmp[:, :, hh:hd], in1=nsin, op=ALU.mult), waits=[d])
            d2 = P.op("dve", lambda e: e.tensor_tensor(out=tmp2[:, :, hh:hd], in0=tmp[:, :, 0:hh], in1=psin, op=ALU.mult), waits=[d])
            d3 = P.op("dve", lambda e: e.tensor_tensor(out=tmp, in0=tmp, in1=cc, op=ALU.mult), waits=[d2])
            tok = P.op("dve", lambda e: e.tensor_tensor(out=outp, in0=tmp, in1=tmp2, op=ALU.add), waits=[d3])
            scr_free[key] = [tok]
            yield
            return tok

        def rstd_from_ss(*a, **k):
            return run(rstd_from_ss_g(*a, **k))

        def rstd_from_ss_g(ss_ps, n, stt, waits):
            a = P.op("act", lambda e: e.activation(out=stt[:n, 0:1], in_=ss_ps, func=AF.Sqrt, bias=EPS, scale=1.0 / D), waits=waits)
            yield
            d = P.op("dve", lambda e: e.reciprocal(out=stt[:n, 1:2], in_=stt[:n, 0:1]), waits=[a])
            d = P.op("dve", lambda e: e.tensor_tensor(out=stt[:n, 2:3], in0=stt[:n, 1:2], in1=stt[:n, 1:2], op=ALU.mult), waits=[d])
            yield
            return (stt[:n, 1:2], stt[:n, 2:3]), d

        ab.reset(); af.reset()
        wst = [af.take(2048), af.take(2048)]
        xt = [af.take(4096).rearrange("p (c t) -> p c t", c=8) for _ in range(2)]
        cs = [af.take(512).rearrange("p (j f) -> p j f", j=4) for _ in range(2)]
        stA = [af.take(16) for _ in range(2)]
        stK = [af.take(64) for _ in range(2)]
        sqA_ = [af.take(320) for _ in range(2)]
        tmpA_ = [af.take(64) for _ in range(2)]
        tmpB_ = [af.take(64) for _ in range(2)]
        wkv = ab.take(8 * 320).rearrange("p (c f) -> p c f", c=8)
        wukv = ab.take(2 * 2048).rearrange("p (c f) -> p c f", c=2)
        xb = [ab.take(4096).rearrange("p (c t) -> p c t", c=8) for _ in range(2)]
        xsq = [ab.take(4096).rearrange("p (c t) -> p c t", c=8) for _ in range(2)]
        cn = [ab.take(256) for _ in range(2)]
        kpe_b = [ab.take(64) for _ in range(2)]
        cTt = [ab.take(256).rearrange("p (c t) -> p c t", c=2) for _ in range(2)]
        khat = [ab.take(1024).rearrange("p (h d) -> p h d", h=8) for _ in range(2)]
        KTst = [ab.take(4096).rearrange("p (h t) -> p h t", h=8) for _ in range(2)]
        Vst = [ab.take(8 * 4 * 129).rearrange("p (h j e) -> p h j e", h=8, j=4) for _ in range(2)]
        kpest = [ab.take(512) for _ in range(2)]

        load_w(wkv, w_in, 8, 256, 576, normw_t, wst)
        tw = load_w(wukv, w_ukv, 2, 0, 2048, kvlw_t, wst)
        for b_ in range(2):
            P.op("pool", lambda e, b_=b_: e.memset(Vst[b_][:, :, :, :], 1.0))

        xTv = xTw.rearrange("(c p) t -> p c t", p=128)
        KTv = KT_part[0:H * 128].rearrange("(h d) t -> d h t", d=128)
        Vv = V_part.rearrange("(h p) (j e) -> p h j e", p=128, e=129)
        KTmv = KT_m.rearrange("h d t -> d h t")
        Vmv = V_m.rearrange("h p j e -> p h j e")
        NGA = 5
        NFULL = 4
        gfree_x = [[], []]
        gfree_xb = [[], []]
        gfree_st = [[], []]
        tcount = 0
        for g in range(NGA):
            b_ = g % 2
            t0 = (256 + g * 512) if g < NFULL else WTOK
            ng = 512 if g < NFULL else 16
            ntile = 4 if g < NFULL else 1
            l1 = P.dma("sp", "lA%d" % b_, lambda e, b_=b_, t0=t0, ng=ng: e.dma_start(out=xt[b_][:, :, 0:ng], in_=xTv[:, :, t0:t0 + ng]),
                       waits=gfree_x[b_])
            if g < NFULL:
                l2 = P.dma("sp", "lA%d" % b_, lambda e, b_=b_, g=g: e.dma_start(
                    out=cs[b_][:, :, :], in_=csq[g * 512:(g + 1) * 512, :].rearrange("(j p) f -> p j f", p=128)))
            else:
                l2 = P.dma("sp", "lA%d" % b_, lambda e, b_=b_: e.dma_start(out=cs[b_][0:16, 0, :], in_=csm))
            c1 = P.op("act", lambda e, b_=b_, ng=ng: e.activation(out=xb[b_][:, :, 0:ng], in_=xt[b_][:, :, 0:ng], func=AF.Copy),
                      waits=[l2] + gfree_xb[b_])
            c2 = P.op("pool", lambda e, b_=b_, ng=ng: e.tensor_tensor(out=xsq[b_][:, :, 0:ng], in0=xt[b_][:, :, 0:ng],
                                                                      in1=xt[b_][:, :, 0:ng], op=ALU.mult), waits=[l2] + gfree_xb[b_])
            res = {"last_pe": None, "last_rope": None, "stage": []}

            def tileA(j, n, tp, b_=b_, l2=l2, c1=c1, c2=c2, res=res):
                sqA, tmpA, tmpB = sqA_[tp], tmpA_[tp], tmpB_[tp]
                sl = slice(j * 128, j * 128 + n)
                pk = B.ap[tp]
                for c in range(8):
                    mm(pk[:n, 0:320], xb[b_][:, c, sl], wkv[:, c, :], c == 0, c == 7, waits=([c1] + B.waits(tp)) if c == 0 else ())
                for c in range(8):
                    tk = mm(pk[:n, 384:385], xsq[b_][:, c, sl], ones[:, 0:1], c == 0, c == 7, waits=[c2] if c == 0 else (), inc=(c == 7))
                yield
                st = stA[tp]
                (r1, r1sq), d = yield from rstd_from_ss_g(pk[:n, 384:385], n, st, [tk])
                dc = yield from headnorm_g(pk[:n, 0:256].rearrange("p (h d) -> p h d", h=1), n, 1, 256, cn[tp][:n, :].rearrange("p (h d) -> p h d", h=1),
                                           sqA[:n, 0:256].rearrange("p (h d) -> p h d", h=1), None, st[:, 4:8], (r1, r1sq), None, None, [tk, d],
                                           key="s0_%d" % tp)
                dk = yield from headnorm_g(pk[:n, 256:320].rearrange("p (h d) -> p h d", h=1), n, 1, 64, kpe_b[tp][:n, :].rearrange("p (h d) -> p h d", h=1),
                                           sqA[:n, 256:320].rearrange("p (h d) -> p h d", h=1), tmpA[:n, :].rearrange("p (h d) -> p h d", h=1),
                                           st[:, 8:12], (r1, r1sq), kpew_t, cs[b_][:, j, :], [tk, d, l2, dc],
                                           tmp2=tmpB[:n, :].rearrange("p (h d) -> p h d", h=1), key="s1_%d" % tp)
                B.release(tp, [dc, dk])
                res["last_rope"] = dk
                ptb = 2 if tp == 0 else 7
                pt = B.ap[ptb]
                mm(pt[:, 0:n], cn[tp][:n, 0:128], ident[:n, :n], True, True, waits=[dc, dk] + B.waits(ptb))
                mm(pt[:, 128:128 + n], cn[tp][:n, 128:256], ident[:n, :n], True, True)
                tt = mm(pt[0:64, 256:256 + n], kpe_b[tp][:n, 0:64], ident[:n, :n], True, True, inc=True)
                yield
                e1 = P.op("dve", lambda e: e.tensor_copy(
                    out=cTt[tp][:, :, 0:n], in_=pt[:, 0:256].rearrange("p (c t) -> p c t", c=2)[:, :, 0:n]), waits=[tt])
                yield
                e2 = P.op("act", lambda e: e.activation(out=kpest[b_][0:64, sl], in_=pt[0:64, 256:256 + n], func=AF.Copy),
                          waits=[tt, e1] + gfree_st[b_])
                B.release(ptb, [e1, e2])
                res["stage"].append(e2)
                stk = stK[tp]
                kh = khat[tp]
                t2 = None
                for gp in range(4):
                    q = 3 + tp
                    pu = B.ap[q]
                    mm(pu[:n, :], cTt[tp][:, 0, 0:n], wukv[:, 0, gp * 512:(gp + 1) * 512], True, False, waits=[e1] + B.waits(q))
                    tu = mm(pu[:n, :], cTt[tp][:, 1, 0:n], wukv[:, 1, gp * 512:(gp + 1) * 512], False, True, inc=True)
                    yield
                    pu4 = pu[:n, :].rearrange("p (h two d) -> p h two d", h=2, two=2)
                    dkk = yield from headnorm_g(pu4[:, :, 0, :], n, 2, 128, kh[:n, 2 * gp:2 * gp + 2, :], sqA[:n, 0:256].rearrange("p (h d) -> p h d", h=2),
                                                None, stk[:, 8 * gp:8 * gp + 8], None, None, None, [tu], key="s0_%d" % tp)
                    av = P.op("act", lambda e, gp=gp, pu4=pu4: e.activation(
                        out=Vst[b_][:n, 2 * gp:2 * gp + 2, j, 0:128], in_=pu4[:, :, 1, :], func=AF.Copy),
                        waits=[tu, dkk] + gfree_st[b_])
                    B.release(q, [dkk, av])
                    res["stage"].append(av)
                    q2 = 5 + tp
                    p2 = B.ap[q2]
                    mm(p2[:, 0:n], kh[:n, 2 * gp, :], ident[:n, :n], True, True, waits=[dkk] + B.waits(q2))
                    t2 = mm(p2[:, 128:128 + n], kh[:n, 2 * gp + 1, :], ident[:n, :n], True, True, inc=True)
                    yield
                    e3 = P.op("dve", lambda e, gp=gp, p2=p2: e.tensor_copy(
                        out=KTst[b_][:, 2 * gp:2 * gp + 2, sl], in_=p2[:, 0:256].rearrange("p (h t) -> p h t", h=2)[:, :, 0:n]),
                        waits=[t2] + gfree_st[b_])
                    B.release(q2, [e3])
                    res["stage"].append(e3)
                    yield
                res["last_pe"] = t2

            if g < NFULL:
                interleave([tileA(0, 128, 0), tileA(1, 128, 1)])
                interleave([tileA(2, 128, 0), tileA(3, 128, 1)])
            else:
                interleave([tileA(0, 16, 0)])
            last_pe = res["last_pe"]
            last_rope = res["last_rope"]
            stage_toks = res["stage"]
            gfree_x[b_] = [c1, c2, last_rope]
            gfree_xb[b_] = [last_pe]
            if g < NFULL:
                s1 = P.dma("pool", "sA%d" % b_, lambda e, b_=b_, g=g: e.dma_start(out=KTv[:, :, g * 512:(g + 1) * 512], in_=KTst[b_][:, :, :]),
                           waits=stage_toks)
                s2 = P.dma("pool", "sA%d" % b_, lambda e, b_=b_, g=g: e.dma_start(out=Vv[:, :, 4 * g:4 * g + 4, :], in_=Vst[b_][:, :, :, :]))
                s3 = P.dma("pool", "sA%d" % b_, lambda e, b_=b_, g=g: e.dma_start(out=KT_part[H * 128:H * 128 + 64, g * 512:(g + 1) * 512], in_=kpest[b_][0:64, :]))
            else:
                s1 = P.dma("pool", "sA%d" % b_, lambda e, b_=b_: e.dma_start(out=KTmv[:, :, :], in_=KTst[b_][:, :, 0:16]), waits=stage_toks)
                s2 = P.dma("pool", "sA%d" % b_, lambda e, b_=b_: e.dma_start(out=Vmv[0:16, :, 0:1, :], in_=Vst[b_][0:16, :, 0:1, :]))
                s3 = P.dma("pool", "sA%d" % b_, lambda e, b_=b_: e.dma_start(out=KPE_m[:, :], in_=kpest[b_][0:64, 0:16]))
            gfree_st[b_] = [s3]
            a_store_toks = [gfree_st[0][0] if gfree_st[0] else None, gfree_st[1][0] if gfree_st[1] else None]
        cc_tok = None
        if stop_after is None:
            rgrp = [list(range(NCORES))]
            P._push("pool", lambda e: e.collective_compute("AllGather", ALU.bypass, replica_groups=rgrp, ins=[KT_part], outs=[KT_all]),
                    a_store_toks, "cc", 1)
            cc_tok = P._push("pool", lambda e: e.collective_compute("AllGather", ALU.bypass, replica_groups=rgrp, ins=[V_part], outs=[V_all]), [], "cc", 1)
        P.barrier(with_cc=True)

        if stop_after != "A":
            ab.reset(); af.reset()
            wst = [af.take(2048), af.take(2048)]
            wst_state["free"] = [[], []]
            xtB = [af.take(1024).rearrange("p (c t) -> p c t", c=8) for _ in range(2)]
            csB = [af.take(128) for _ in range(4)]
            stB = [af.take(16) for _ in range(2)]
            stH_ = [[af.take(64) for _ in range(8)] for _ in range(2)]
            sqB_ = [[af.take(512) for _ in range(2)] for _ in range(2)]
            tmB_ = [[af.take(512) for _ in range(2)] for _ in range(2)]
            tm2_ = [[af.take(512) for _ in range(2)] for _ in range(2)]
            wq = ab.take(8 * 256).rearrange("p (c f) -> p c f", c=8)
            wgm = ab.take(8 * 1024).rearrange("p (c f) -> p c f", c=8)
            wnq = ab.take(8 * 1024).rearrange("p (c f) -> p c f", c=8)
            wnk = ab.take(8 * 1024).rearrange("p (c f) -> p c f", c=8)
            wnv = ab.take(8 * 1024).rearrange("p (c f) -> p c f", c=8)
            wgn = ab.take(8 * 1024).rearrange("p (c f) -> p c f", c=8)
            wuq = ab.take(2 * 1536).rearrange("p (c f) -> p c f", c=2)
            xbB = [ab.take(1024).rearrange("p (c t) -> p c t", c=8) for _ in range(2)]
            xsqB = [ab.take(1024).rearrange("p (c t) -> p c t", c=8) for _ in range(2)]
            nkh = [ab.take(1024) for _ in range(2)]
            nqh = nkh
            NKst = [ab.take(8 * 128).rearrange("p (h t) -> p h t", h=8) for _ in range(2)]
            NQst = [ab.take(8 * 128).rearrange("p (h t) -> p h t", h=8) for _ in range(2)]
            NVst = [ab.take(16 * 65).rearrange("p (h e) -> p h e", h=16) for _ in range(2)]
            qlb = [ab.take(256) for _ in range(2)]
            qlT = [ab.take(256).rearrange("p (c t) -> p c t", c=2) for _ in range(2)]
            qnb = [ab.take(1024).rearrange("p (h d) -> p h d", h=8) for _ in range(2)]
            qpb = [ab.take(512).rearrange("p (h d) -> p h d", h=8) for _ in range(2)]
            QnSt = [ab.take(8 * 128).rearrange("p (h t) -> p h t", h=8) for _ in range(2)]
            QpSt = [ab.take(4 * 128).rearrange("p (h t) -> p h t", h=4) for _ in range(2)]
            GMst = [ab.take(1024) for _ in range(2)]
            GNst = [ab.take(1024) for _ in range(2)]
            load_w(wq, w_in, 8, 0, 256, normw_t, wst)
            load_w(wgm, w_in, 8, 576, 1600, normw_t, wst)
            load_w(wnq, w_in, 8, 1600, 2624, normw_t, wst)
            load_w(wnk, w_in, 8, 2624, 3648, normw_t, wst)
            load_w(wnv, w_in, 8, 3648, 4672, normw_t, wst)
            load_w(wgn, w_in, 8, 4672, 5696, normw_t, wst)
            load_w(wuq, w_uq, 2, 0, 1536, qlw_t, wst)
            for b_ in range(2):
                P.op("dve", lambda e, b_=b_: e.memset(NVst[b_][:, :, :], 1.0))
            xTwv = xTw.rearrange("(c p) t -> p c t", p=128)
            NKTv = NKT_s.rearrange("(hp two) d t -> (two d) hp t", two=2)
            NQTv = NQT_s.rearrange("(hp two) d t -> (two d) hp t", two=2)
            QTnv = QT_s.rearrange("h d t -> d h t")
            QTpv = QPT_s.rearrange("(hp two) r t -> (two r) hp t", two=2)
            free_x = [[], []]
            free_xb = [[], []]
            free_st = [[], []]
            free_cs = [[], [], [], []]
            bk = {"i": 0, "held": set()}

            def nextbank():
                while True:
                    i = bk["i"] % 8
                    bk["i"] += 1
                    if i not in bk["held"]:
                        bk["held"].add(i)
                        return i

            def relbank(q, toks):
                B.release(q, toks)
                bk["held"].discard(q)

            def proj(n, tp, wmat, col0, ncols, c1tok):
                q = nextbank()
                t = None
                for c in range(8):
                    t = mm(B.ap[q][:n, 0:ncols], xbB[tp][:, c, 0:n], wmat[:, c, col0:col0 + ncols], c == 0, c == 7,
                           waits=([c1tok] + B.waits(q)) if c == 0 else (), inc=(c == 7))
                return q, t

            def transposes_out_g(src_bf, n, nblk, stage, st_waits, src_waits):
                toks = []
                for k0 in range(0, nblk, 4):
                    q = nextbank()
                    kk = min(4, nblk - k0)
                    t = None
                    for k in range(kk):
                        t = mm(B.ap[q][:, k * 128:k * 128 + n], src_bf[:n, (k0 + k) * 128:(k0 + k + 1) * 128], ident[:n, :n], True, True,
                               waits=(B.waits(q) + list(src_waits)) if k == 0 else (), inc=(k == kk - 1))
                    yield
                    ev = P.op("act", lambda e, q=q, k0=k0, kk=kk, n=n: e.activation(
                        out=stage[:, k0:k0 + kk, 0:n], in_=B.ap[q][:, 0:kk * 128].rearrange("p (h t) -> p h t", h=kk)[:, :, 0:n], func=AF.Copy),
                        waits=[t] + list(st_waits))
                    relbank(q, [ev])
                    toks.append(ev)
                    yield
                return toks

            loads = {}

            def emit_loads(wt):
                tp = wt % 2
                n = 128 if wt < 20 else 16
                own = 2 <= wt < 18
                tok0 = wt * 128
                l1 = P.dma("sp", "lB%d" % tp, lambda e: e.dma_start(out=xtB[tp][:, :, 0:n], in_=xTwv[:, :, tok0:tok0 + n]), waits=free_x[tp])
                lb = l1
                if own:
                    ti = wt - 2
                    lb = P.dma("sp", "lB%d" % tp, lambda e: e.dma_start(out=csB[wt % 4][:, :], in_=csq[ti * 128:(ti + 1) * 128, :]), waits=free_cs[wt % 4])
                loads[wt] = lb

            stores = {}

            def tileB(wt):
                tp = wt % 2
                n = 128 if wt < 20 else 16
                own = 2 <= wt < 18
                ti = wt - 2
                stH, sqB, tmB, tm2 = stH_[tp], sqB_[tp], tmB_[tp], tm2_[tp]
                lb = loads[wt]
                c1 = P.op("act", lambda e: e.activation(out=xbB[tp][:, :, 0:n], in_=xtB[tp][:, :, 0:n], func=AF.Copy), waits=[lb] + free_xb[tp])
                c2 = P.op("act", lambda e: e.activation(out=xsqB[tp][:, :, 0:n], in_=xtB[tp][:, :, 0:n], func=AF.Square), waits=[lb] + free_xb[tp])
                free_x[tp] = [c1, c2]
                yield
                qs = nextbank()
                tk = None
                for c in range(8):
                    tk = mm(B.ap[qs][:n, 0:1], xsqB[tp][:, c, 0:n], ones[:, 0:1], c == 0, c == 7, waits=([c2] + B.waits(qs)) if c == 0 else (),
                            inc=(c == 7))
                yield
                st = stB[tp]
                (r1, r1sq), d = yield from rstd_from_ss_g(B.ap[qs][:n, 0:1], n, st, [tk])
                relbank(qs, [d])
                stage_toks = []
                dk = None
                for half in range(2):
                    q, t = proj(n, tp, wnk, half * 512, 512, c1)
                    yield
                    dk = yield from headnorm_g(B.ap[q][:n, :].rearrange("p (h d) -> p h d", h=8), n, 8, 64,
                                               nkh[tp][:n, half * 512:(half + 1) * 512].rearrange("p (h d) -> p h d", h=8),
                                               sqB[half][:n, :].rearrange("p (h d) -> p h d", h=8), tmB[half][:n, :].rearrange("p (h d) -> p h d", h=8),
                                               stH[half], (r1, r1sq), nakw_t, None, [t, d], key="h%d_%d" % (half, tp))
                    relbank(q, [dk])
                evs = yield from transposes_out_g(nkh[tp], n, 8, NKst[tp], free_st[tp], [dk])
                stage_toks += evs
                t = None
                for half in range(2):
                    q, t = proj(n, tp, wnv, half * 512, 512, c1)
                    yield
                    av = P.op("act", lambda e, q=q, half=half: e.activation(
                        out=NVst[tp][:n, half * 8:(half + 1) * 8, 0:64], in_=B.ap[q][:n, :].rearrange("p (h d) -> p h d", h=8),
                        func=AF.Copy, scale=r1), waits=[t, d] + free_st[tp])
                    relbank(q, [av])
                    stage_toks.append(av)
                    yield
                last_pe_x = t
                if own:
                    q, t = proj(n, tp, wq, 0, 256, c1)
                    yield
                    dq = yield from headnorm_g(B.ap[q][:n, 0:256].rearrange("p (h d) -> p h d", h=1), n, 1, 256,
                                               qlb[tp][:n, :].rearrange("p (h d) -> p h d", h=1),
                                               sqB[0][:n, 0:256].rearrange("p (h d) -> p h d", h=1), None, stH[2], (r1, r1sq), None, None, [t, d],
                                               key="h0_%d" % tp)
                    relbank(q, [dq])
                    q = nextbank()
                    mm(B.ap[q][:, 0:128], qlb[tp][:, 0:128], ident[:, :], True, True, waits=[dq] + B.waits(q))
                    t = mm(B.ap[q][:, 128:256], qlb[tp][:, 128:256], ident[:, :], True, True, inc=True)
                    yield
                    eq = P.op("dve", lambda e, q=q: e.tensor_copy(out=qlT[tp][:, :, :], in_=B.ap[q][:, 0:256].rearrange("p (c t) -> p c t", c=2)),
                              waits=[t])
                    relbank(q, [eq])
                    yield
                    d1 = d2 = None
                    for gq in range(4):
                        q = nextbank()
                        mm(B.ap[q][:, 0:384], qlT[tp][:, 0, :], wuq[:, 0, gq * 384:(gq + 1) * 384], True, False, waits=[eq] + B.waits(q))
                        t = mm(B.ap[q][:, 0:384], qlT[tp][:, 1, :], wuq[:, 1, gq * 384:(gq + 1) * 384], False, True, inc=True)
                        yield
                        v3 = B.ap[q][:, 0:384].rearrange("p (h d) -> p h d", h=2)
                        d1 = yield from headnorm_g(v3[:, :, 0:128], 128, 2, 128, qnb[tp][:, 2 * gq:2 * gq + 2, :],
                                                   sqB[0][:, 0:256].rearrange("p (h d) -> p h d", h=2),
                                                   tmB[0][:, 0:256].rearrange("p (h d) -> p h d", h=2), stH[3], None, qkw_t, None, [t], key="h0_%d" % tp)
                        d2 = yield from headnorm_g(v3[:, :, 128:192], 128, 2, 64, qpb[tp][:, 2 * gq:2 * gq + 2, :],
                                                   sqB[1][:, 0:128].rearrange("p (h d) -> p h d", h=2),
                                                   tmB[1][:, 0:128].rearrange("p (h d) -> p h d", h=2), stH[4], None, qpew_t, csB[wt % 4], [t, lb, d1],
                                                   tmp2=tm2[1][:, 0:128].rearrange("p (h d) -> p h d", h=2), key="h1_%d" % tp)
                        relbank(q, [d1, d2])
                    free_cs[wt % 4] = [d2]
                    evs = yield from transposes_out_g(qnb[tp][:, :, :].rearrange("p h d -> p (h d)"), 128, 8, QnSt[tp], free_st[tp], [d1, d2])
                    stage_toks += evs
                    evs = yield from transposes_out_g(qpb[tp][:, :, :].rearrange("p h d -> p (h d)"), 128, 4, QpSt[tp], free_st[tp], [d1, d2])
                    stage_toks += evs
                    for (wmat, gst) in ((wgm, GMst), (wgn, GNst)):
                        for half in range(2):
                            q, t = proj(n, tp, wmat, half * 512, 512, c1)
                            yield
                            ag = P.op("act", lambda e, q=q, half=half, gst=gst: e.activation(
                                out=gst[tp][:, half * 512:(half + 1) * 512], in_=B.ap[q][:, :], func=AF.Silu, scale=r1),
                                waits=[t, d] + free_st[tp])
                            relbank(q, [ag])
                            stage_toks.append(ag)
                            yield
                    for half in range(2):
                        q, t = proj(n, tp, wnq, half * 512, 512, c1)
                        yield
                        dk = yield from headnorm_g(B.ap[q][:, :].rearrange("p (h d) -> p h d", h=8), 128, 8, 64,
                                                   nqh[tp][:, half * 512:(half + 1) * 512].rearrange("p (h d) -> p h d", h=8),
                                                   sqB[half][:, :].rearrange("p (h d) -> p h d", h=8), tmB[half][:, :].rearrange("p (h d) -> p h d", h=8),
                                                   stH[5 + half], (r1, r1sq), naqw_t, None, [t, d], key="h%d_%d" % (half, tp))
                        relbank(q, [dk])
                        last_pe_x = t
                    evs = yield from transposes_out_g(nqh[tp], 128, 8, NQst[tp], free_st[tp], [dk])
                    stage_toks += evs
                free_xb[tp] = [last_pe_x]
                stores[wt] = stage_toks

            def emit_stores(wt):
                tp = wt % 2
                n = 128 if wt < 20 else 16
                own = 2 <= wt < 18
                tok0 = wt * 128
                s_ = P.dma("pool", "sB%d" % tp, lambda e: e.dma_start(out=NKTv[:, :, tok0:tok0 + n], in_=NKst[tp][:, :, 0:n]), waits=stores[wt])
                s_ = P.dma("pool", "sB%d" % tp, lambda e: e.dma_start(out=NV_s[tok0:tok0 + n, :, :], in_=NVst[tp][0:n, :, :]))
                if own:
                    q0 = (wt - 2) * 128
                    s_ = P.dma("pool", "sB%d" % tp, lambda e: e.dma_start(out=QTnv[:, :, q0:q0 + 128], in_=QnSt[tp][:, :, :]))
                    s_ = P.dma("pool", "sB%d" % tp, lambda e: e.dma_start(out=QTpv[:, :, q0:q0 + 128], in_=QpSt[tp][:, :, :]))
                    s_ = P.dma("pool", "sB%d" % tp, lambda e: e.dma_start(out=NQTv[:, :, q0:q0 + 128], in_=NQst[tp][:, :, :]))
                    s_ = P.dma("pool", "sB%d" % tp, lambda e: e.dma_start(out=GM_s[q0:q0 + 128, :], in_=GMst[tp][:, :]))
                    s_ = P.dma("pool", "sB%d" % tp, lambda e: e.dma_start(out=GN_s[q0:q0 + 128, :], in_=GNst[tp][:, :]))
                free_st[tp] = [s_]

            nwt = int(stop_after[2:]) if (stop_after or "").startswith("Bn") else 21
            pairs = [list(range(a, min(a + 2, nwt))) for a in range(0, nwt, 2)]
            for wt in pairs[0]:
                emit_loads(wt)
            for pi_, pr in enumerate(pairs):
                gens = [tileB(wt) for wt in pr]
                for gi in gens:
                    next(gi)
                if pi_ + 1 < len(pairs):
                    for wt in pairs[pi_ + 1]:
                        emit_loads(wt)
                interleave(gens)
                for wt in pr:
                    emit_stores(wt)
            P.barrier()

        if stop_after not in ("A", "B") and not (stop_after or "").startswith("Bn"):
            ab.reset(); af.reset()
            NKA = [ab.take(WALL) for _ in range(2)]
            NQA = [ab.take(OWN) for _ in range(2)]
            NVall = ab.take(21 * 16 * 65).rearrange("p (j h e) -> p j h e", j=21, h=16)
            GNall = ab.take(16 * 1024).rearrange("p (j f) -> p j f", j=16)
            PTc = [ab.take(512) for _ in range(4)]
            ypair = ab.take(16 * 128).rearrange("p (u f) -> p u f", u=16)
            YTst = ab.take(OWN)
            EB = [af.take(1024) for _ in range(2)]
            Ef = [af.take(512) for _ in range(2)]
            rdn = af.take(16)
            mstage = af.take(WTOK)
            pieces = na_pieces()
            lA = P.dma("sp", "m0", lambda e: e.dma_start(out=mstage[64:104, 0:WTOK], in_=konehot))
            for b_ in range(2):
                P.op("dve", lambda e, b_=b_: e.memset(NKA[b_][64:128, :], 0.0))
                P.op("dve", lambda e, b_=b_: e.memset(NQA[b_][64:128, :], 0.0))
                lk = P.op("dve", lambda e, b_=b_: e.tensor_copy(out=NKA[b_][64:104, 0:WTOK], in_=mstage[64:104, 0:WTOK]), waits=[lA])
            lB = P.dma("sp", "m1", lambda e: e.dma_start(out=mstage[64:104, 0:OWN], in_=qmask), waits=[lk])
            for b_ in range(2):
                lq = P.op("dve", lambda e, b_=b_: e.tensor_copy(out=NQA[b_][64:104, :], in_=mstage[64:104, 0:OWN]), waits=[lB])
            P.dma("sp", "m2", lambda e: e.dma_start(out=NVall[:, 0:20, :, :], in_=NV_s[0:WTOK].rearrange("(j p) h e -> p j h e", p=128)))
            lv = P.dma("sp", "m2", lambda e: e.dma_start(out=NVall[0:16, 20, :, :], in_=NV_s[WTOK:WALL]))
            lg = P.dma("sp", "m3", lambda e: e.dma_start(out=GNall[:, :, :], in_=GN_s.rearrange("(j p) f -> p j f", p=128)))
            accb = [0, 1, 2]

            def acc_ap(u):
                return B.ap[accb[u // 7]][:, (u % 7) * 65:(u % 7) * 65 + 65]

            free_nk = [[], []]
            free_eb = [[], []]
            free_pt = [[], [], [], []]
            free_ef = [[], []]
            free_yp = []
            free_yst = []
            ucnt = 0
            cload = {}

            def emit_cloads(h):
                b_ = h % 2
                P.dma("sp", "lC%d" % b_, lambda e: e.dma_start(out=NKA[b_][0:64, :], in_=NKT_s[h]), waits=free_nk[b_])
                P.dma("sp", "lC%d" % b_, lambda e: e.dma_start(out=NQA[b_][0:64, :], in_=NQT_s[h]))
                le_ = P.dma("sp", "lC%d" % b_, lambda e: e.dma_start(out=EB[b_][:, :], in_=ebias[h]), waits=free_eb[b_])
                ae_ = P.op("act", lambda e: e.activation(out=EB[b_][:, :], in_=EB[b_][:, :], func=AF.Exp), waits=[le_])
                cload[h] = (le_, ae_)

            emit_cloads(0)
            for h in range(NH):
                b_ = h % 2
                le, ae = cload[h]
                z = None
                for k in range(3):
                    z = P.op("dve", lambda e, k=k: e.memset(B.ap[accb[k]][:, 0:455], 0.0), waits=B.waits(accb[k]))
                pv_last = None
                first = True
                pendc = []

                def issue_pv_c(item):
                    (m_, q0_, q1_, pi_, ready_, nk_) = item
                    t_ = None
                    for u in range(q0_ // 128, q1_ // 128):
                        o = u * 128 - q0_
                        t_ = mm(acc_ap(u), PTc[pi_][0:nk_, o:o + 128], NVall[0:nk_, m_, h, :], False, False,
                                waits=[ready_] if u == q0_ // 128 else (), inc=(u == q1_ // 128 - 1), skip=True)
                    free_pt[pi_] = [t_]
                    return t_

                for m in range(21):
                    nk = 128 if m < 20 else 16
                    pcs = pieces[m] if m < 20 else [(0, 512), (512, 1024), (1024, 1536), (1536, 2048)]
                    for (q0, q1) in pcs:
                        nq = q1 - q0
                        sb_ = (3, 4, 7)[ucnt % 3]
                        pS = B.ap[sb_]
                        pi = ucnt % 4
                        fi = ucnt % 2
                        ucnt += 1
                        w0 = B.waits(sb_) + ([le, lq, lv, z] if first else [])
                        first = False
                        if m < 20:
                            ts = mm(pS[:, 0:nq], NKA[b_][:, m * 128:(m + 1) * 128], NQA[b_][:, q0:q1], True, True, waits=w0, inc=True)
                            a1 = P.op("act", lambda e, fi=fi, nq=nq, pS=pS: e.activation(out=Ef[fi][:, 0:nq], in_=pS[:, 0:nq], func=AF.Exp, scale=NA_SCALE),
                                      waits=[ts] + free_ef[fi])
                            rel0 = (q0 // 64) - 2 * m + 11
                            d1 = P.op("dve", lambda e, fi=fi, pi=pi, nq=nq, rel0=rel0, b_=b_: e.tensor_tensor(
                                out=PTc[pi][:, 0:nq], in0=Ef[fi][:, 0:nq], in1=EB[b_][:, rel0 * 64:rel0 * 64 + nq], op=ALU.mult),
                                waits=[a1, ae] + free_pt[pi])
                            B.release(sb_, [a1])
                            free_ef[fi] = [d1]
                            ready = d1
                        else:
                            ts = mm(pS[0:16, 0:nq], NKA[b_][:, WTOK:WALL], NQA[b_][:, q0:q1], True, True, waits=w0, inc=True)
                            a1 = P.op("act", lambda e, pi=pi, nq=nq, pS=pS, h=h: e.activation(
                                out=PTc[pi][0:16, 0:nq], in_=pS[0:16, 0:nq], func=AF.Exp, scale=NA_SCALE, bias=metab_t[0:16, h:h + 1]),
                                waits=[ts] + free_pt[pi])
                            B.release(sb_, [a1])
                            ready = a1
                        pendc.append((m, q0, q1, pi, ready, nk))
                        if len(pendc) > 2:
                            pv_last = issue_pv_c(pendc.pop(0))
                while pendc:
                    pv_last = issue_pv_c(pendc.pop(0))
                free_nk[b_] = [pv_last]
                free_eb[b_] = [pv_last]
                if h + 1 < NH:
                    emit_cloads(h + 1)
                ev = None
                for u in range(16):
                    a = acc_ap(u)
                    r = P.op("dve", lambda e, a=a, u=u: e.reciprocal(out=rdn[:, u:u + 1], in_=a[:, 64:65]), waits=[pv_last] + (free_yp if (u == 0 and h % 2 == 0) else []))
                    ev = P.op("dve", lambda e, a=a, u=u, h=h: e.scalar_tensor_tensor(
                        out=ypair[:, u, (h % 2) * 64:(h % 2) * 64 + 64], in0=a[:, 0:64], scalar=rdn[:, u:u + 1],
                        in1=GNall[:, u, h * 64:(h + 1) * 64], op0=ALU.mult, op1=ALU.mult), waits=[r, lg])
                for k in range(3):
                    B.release(accb[k], [ev])
                if h % 2 == 1:
                    hp = h // 2
                    tt = None
                    evs = []
                    for u4 in range(4):
                        q = 5 + (u4 % 2)
                        for k in range(4):
                            u = u4 * 4 + k
                            tt = mm(B.ap[q][:, k * 128:(k + 1) * 128], ypair[:, u, :], ident[:, :], True, True,
                                    waits=([ev] + B.waits(q)) if k == 0 else (), inc=(k == 3))
                        e2 = P.op("act", lambda e, q=q, u4=u4: e.activation(out=YTst[:, u4 * 512:(u4 + 1) * 512], in_=B.ap[q][:, :], func=AF.Copy),
                                  waits=[tt] + (free_yst if u4 == 0 else []))
                        B.release(q, [e2])
                        evs.append(e2)
                    free_yp = [tt]
                    s = P.dma("sp", "st2", lambda e, hp=hp: e.dma_start(out=YT_s[1024 + hp * 128:1024 + (hp + 1) * 128, :], in_=YTst[:, :]), waits=evs)
                    free_yst = [s]
            P.barrier()

        if stop_after not in ("A", "B", "C") and not (stop_after or "").startswith("Bn"):
            ab.reset(); af.reset()
            KTh = ab.take(TK)
            kpeT = ab.take(TK)
            Vh = ab.take(NT * 129).rearrange("p (j e) -> p j e", j=NT)
            Qn = [ab.take(OWN) for _ in range(2)]
            Qp = [ab.take(OWN) for _ in range(2)]
            GMh = [ab.take(16 * 128).rearrange("p (j f) -> p j f", j=16) for _ in range(2)]
            PTd = [ab.take(512) for _ in range(4)]
            ybf = [ab.take(128) for _ in range(2)]
            YTd = [ab.take(1024) for _ in range(2)]
            rdd = af.take(16)
            P.dma("sp", "m0", lambda e: e.dma_start(out=kpeT[0:64, 0:SEQ].rearrange("p (r t) -> p r t", r=NCORES),
                                                    in_=KT_all.rearrange("(r k) t -> k r t", k=H * 128 + 64)[H * 128:H * 128 + 64]), waits=[cc_tok])
            lp = P.dma("sp", "m0", lambda e: e.dma_start(out=kpeT[0:64, SEQ:TK], in_=KPE_m))
            zp = P.op("pool", lambda e: e.memset(kpeT[64:128, :], 0.0))
            for hb_ in range(2):
                zp = P.op("pool", lambda e, hb_=hb_: e.memset(Qp[hb_][64:128, :], 0.0))
            NGK = 8
            gtiles = [list(range(16 * g, 16 * g + 16)) for g in range(NGK)]
            gtiles[7].append(128)
            accb = [0, 1, 2]

            def accd(i):
                return B.ap[accb[i // 3]][:, (i % 3) * 129:(i % 3) * 129 + 129]

            kv_free = [[] for _ in range(NGK)]
            kv_ld = [None] * NGK
            q_free = [[], []]
            free_ptd = [[], [], [], []]
            free_yb = [[], []]
            free_ytd = [[], []]
            gm_free = [[], []]
            GMv = GM_s.rearrange("(j p) f -> p j f", p=128)
            ucnt = 0
            pcount = 0
            for h in range(H):
                hb = h % 2
                lqn = P.dma("sp", "lD%d" % hb, lambda e, hb=hb, h=h: e.dma_start(out=Qn[hb][:, :], in_=QT_s[h]), waits=q_free[hb])
                lqp = P.dma("sp", "lD%d" % hb, lambda e, hb=hb, h=h: e.dma_start(out=Qp[hb][0:64, :], in_=QPT_s[h]))
                lgm = P.dma("sp", "lD%d" % hb, lambda e, hb=hb, h=h: e.dma_start(out=GMh[hb][:, :, :], in_=GMv[:, :, h * 128:(h + 1) * 128]), waits=gm_free[hb])
                for g in range(NGK):
                    r0 = g * H * 128 + h * 128
                    rk = g * (H * 128 + 64) + h * 128
                    P.dma("sp", "kv%d" % g, lambda e, g=g, rk=rk: e.dma_start(out=KTh[:, g * OWN:(g + 1) * OWN], in_=KT_all[rk:rk + 128, :]),
                          waits=kv_free[g] + [cc_tok])
                    kv_ld[g] = P.dma("sp", "kv%d" % g, lambda e, g=g, r0=r0: e.dma_start(
                        out=Vh[:, 16 * g:16 * g + 16, :], in_=V_all[r0:r0 + 128, :].rearrange("p (j e) -> p j e", e=129)))
                    if g == NGK - 1:
                        P.dma("sp", "kv%d" % g, lambda e, h=h: e.dma_start(out=KTh[:, SEQ:TK], in_=KT_m[h]))
                        kv_ld[g] = P.dma("sp", "kv%d" % g, lambda e, h=h: e.dma_start(out=Vh[0:16, 128, :], in_=V_m[h, 0:16, 0, :]))
                for qc in range(2):
                    z = None
                    for k in range(3):
                        z = P.op("dve", lambda e, k=k: e.memset(B.ap[accb[k]][:, 0:387], 0.0), waits=B.waits(accb[k]))
                    units = [(j, s) for j in range(NT) for s in range(2)]
                    pend = []

                    def issue_pv(item):
                        (j, s, pi, ex, nk) = item
                        t = None
                        for i in range(4):
                            t = mm(accd(4 * s + i), PTd[pi][0:nk, i * 128:(i + 1) * 128], Vh[0:nk, j, :], False, False,
                                   waits=[ex, z] if i == 0 else (), inc=(i == 3), skip=True)
                        free_ptd[pi] = [t]
                        if qc == 1 and s == 1 and (j % 16 == 15 or j == 128) and not (j == 127):
                            kv_free[min(j // 16, 7)] = [t]
                        return t

                    pv_last = None
                    for (j, s) in units:
                        nk = 128 if j < 128 else 16
                        g = min(j // 16, 7)
                        sb_ = 3 + (ucnt % 3)
                        pi = ucnt % 4
                        ucnt += 1
                        pS = B.ap[sb_]
                        q0 = qc * 1024 + s * 512
                        w0 = B.waits(sb_)
                        if s == 0 and (j % 16 == 0) and j < 128:
                            w0 = w0 + [kv_ld[g]]
                        if j == 0 and s == 0:
                            w0 = w0 + [lgm, lp, zp]
                        mm(pS[0:nk, :], KTh[:, j * 128:j * 128 + nk], Qn[hb][:, q0:q0 + 512], True, False, waits=w0)
                        ts = mm(pS[0:nk, :], kpeT[:, j * 128:j * 128 + nk], Qp[hb][:, q0:q0 + 512], False, True, inc=True)
                        ex = P.op("act", lambda e, pi=pi, nk=nk, pS=pS: e.activation(out=PTd[pi][0:nk, :], in_=pS[0:nk, :], func=AF.Exp, scale=MLA_SCALE),
                                  waits=[ts] + free_ptd[pi])
                        B.release(sb_, [ex])
                        pend.append((j, s, pi, ex, nk))
                        if len(pend) > 2:
                            pv_last = issue_pv(pend.pop(0))
                    while pend:
                        pv_last = issue_pv(pend.pop(0))
                    if qc == 1:
                        q_free[hb] = [pv_last]
                    for i in range(8):
                        a = accd(i)
                        u = qc * 8 + i
                        yb = ybf[i % 2]
                        r = P.op("dve", lambda e, a=a, i=i: e.reciprocal(out=rdd[:, i:i + 1], in_=a[:, 128:129]), waits=[pv_last])
                        ev = P.op("dve", lambda e, a=a, i=i, u=u, yb=yb, hb=hb: e.scalar_tensor_tensor(
                            out=yb[:, :], in0=a[:, 0:128], scalar=rdd[:, i:i + 1], in1=GMh[hb][:, u, :], op0=ALU.mult, op1=ALU.mult),
                            waits=[r, lgm] + free_yb[i % 2])
                        q = 6 + (i // 4) % 2
                        if i % 4 == 0:
                            wq_ = B.waits(q)
                        tt = mm(B.ap[q][:, (i % 4) * 128:(i % 4 + 1) * 128], yb[:, :], ident[:, :], True, True,
                                waits=[ev] + (wq_ if i % 4 == 0 else []), inc=True)
                        free_yb[i % 2] = [tt]
                        if i % 4 == 3:
                            yk = pcount % 2
                            e2 = P.op("act", lambda e, q=q, yk=yk, i=i: e.activation(
                                out=YTd[yk][:, (i // 4) * 512:(i // 4 + 1) * 512], in_=B.ap[q][:, :], func=AF.Copy),
                                waits=[tt] + (free_ytd[yk] if i == 3 else []))
                            B.release(q, [e2])
                            if i == 7:
                                s_ = P.dma("pool", "sD%d" % yk, lambda e, yk=yk, h=h, qc=qc: e.dma_start(
                                    out=YT_s[h * 128:(h + 1) * 128, qc * 1024:(qc + 1) * 1024], in_=YTd[yk][:, :]), waits=[e2])
                                free_ytd[yk] = [s_]
                    for k in range(3):
                        B.release(accb[k], [ev])
                    if qc == 1:
                        gm_free[hb] = [ev]
                    pcount += 1
            P.barrier()

        if stop_after is None:
            ab.reset(); af.reset()
            wst = [af.take(2048), af.take(2048)]
            wst_state["free"] = [[], []]
            wo = ab.take(16 * 1024).rearrange("p (c f) -> p c f", c=16)
            yT = ab.take(16 * OWN).rearrange("p (c t) -> p c t", c=16)
            xo = [af.take(1024) for _ in range(2)]
            oo = [af.take(1024) for _ in range(2)]
            ly = P.dma("sp", "m1", lambda e: e.dma_start(out=yT[:, :, :], in_=YT_s.rearrange("(c p) t -> p c t", p=128)))
            tw = load_w(wo, w_out, 16, 0, 1024, None, wst)
            free_xo = [[], []]
            free_oo = [[], []]
            fin = None
            for ti in range(16):
                tp = ti % 2
                lx = P.dma("sp", "lE%d" % tp, lambda e, tp=tp, ti=ti: e.dma_start(out=xo[tp][:, :], in_=xown[ti * 128:(ti + 1) * 128, :]), waits=free_xo[tp])
                for half in range(2):
                    q = (ti * 2 + half) % 8
                    t = None
                    for c in range(16):
                        t = mm(B.ap[q][:, :], yT[:, c, ti * 128:(ti + 1) * 128], wo[:, c, half * 512:(half + 1) * 512], c == 0, c == 15,
                               waits=([ly, tw] + B.waits(q)) if c == 0 else (), inc=(c == 15))
                    a = P.op("dve", lambda e, q=q, tp=tp, half=half: e.tensor_tensor(
                        out=oo[tp][:, half * 512:(half + 1) * 512], in0=B.ap[q][:, :], in1=xo[tp][:, half * 512:(half + 1) * 512], op=ALU.add),
                        waits=[t, lx] + (free_oo[tp] if half == 0 else []))
                    B.release(q, [a])
                free_xo[tp] = [a]
                fin = P.dma("pool", "fin%d" % tp, lambda e, tp=tp, ti=ti: e.dma_start(out=out[ti * 128:(ti + 1) * 128, :], in_=oo[tp][:, :]), waits=[a])
                free_oo[tp] = [fin]

        fin_waits = [(s, P.cnt[s]) for s in P.dma_sems if s in P.cnt]
        P._push("pool", lambda e: e.memset(identf[:, 0:1], 0.0), fin_waits + [(s, v) for s, v in P.cnt.items() if s in Prog.CE], None, 1)

        for nme in P.cnt:
            P.sems[nme] = es.enter_context(nc.semaphore(nme))
        block = es.enter_context(nc.Block())

        @block.sync
        def _(e):
            P.replay("sp", e)

        @block.tensor
        def _(e):
            P.replay("pe", e)

        @block.scalar
        def _(e):
            P.replay("act", e)

        @block.vector
        def _(e):
            P.replay("dve", e)

        @block.gpsimd
        def _(e):
            P.replay("pool", e)
    return nc


def _rope_table(pos):
    inv_freq = (10000.0 ** (-(np.arange(0, 64, 2, dtype=np.float32) / 64))).astype(np.float32)
    ang = pos.astype(np.float32)[:, None] * inv_freq[None, :]
    c, s = np.cos(ang).astype(np.float32), np.sin(ang).astype(np.float32)
    return np.ascontiguousarray(np.concatenate([c, c, -s, s], axis=1))


def prepare_inputs(x, meta_tokens, norm_w, w_in, q_lat_norm_w, kv_lat_norm_w, w_uq, w_ukv,
                   mla_qn_w, mla_qpe_w, mla_kn_w, mla_kpe_w, na_q_norm_w, na_k_norm_w,
                   na_rel_bias, na_meta_bias, w_out):
    f = lambda a: np.ascontiguousarray(np.asarray(a, dtype=np.float32))
    x = f(x)[0]
    meta = f(meta_tokens)
    csm = _rope_table(np.arange(NMETA))
    rb = f(na_rel_bias)[0]
    cols = np.arange(GRID_W)
    c0 = np.clip(cols - 8, 0, GRID_W - 16)
    eb = np.full((NH, 2, 64, 16, 64), NEG, np.float32)
    for jj in range(2):
        for irel in range(16):
            dr = jj + 7 - irel
            if not (-7 <= dr <= 7):
                continue
            for cq in range(64):
                ck = np.arange(c0[cq], c0[cq] + 16)
                eb[:, jj, ck, irel, cq] = rb[:, dr + 7, ck - cq + 15]
    eb = np.ascontiguousarray(eb.reshape(NH, 128, 1024))
    konehot = np.zeros((WROWS, WTOK), np.float32)
    for j in range(WROWS):
        konehot[j, j * 64:(j + 1) * 64] = 1.0
    shared = {
        "w_in": f(w_in)[0], "w_uq": f(w_uq)[0], "w_ukv": f(w_ukv)[0], "w_out": f(w_out)[0],
        "normw": np.ascontiguousarray(f(norm_w)[0].reshape(8, 128).T),
        "qlw": np.ascontiguousarray(f(q_lat_norm_w)[0].reshape(2, 128).T),
        "kvlw": np.ascontiguousarray(f(kv_lat_norm_w)[0].reshape(2, 128).T),
        "qnw": f(mla_qn_w)[0][None, :], "knw": f(mla_kn_w)[0][None, :],
        "qpew": f(mla_qpe_w)[0][None, :], "kpew": f(mla_kpe_w)[0][None, :],
        "naqw": f(na_q_norm_w)[0][None, :], "nakw": f(na_k_norm_w)[0][None, :],
        "csm": csm, "ebias": eb, "konehot": konehot,
        "metab": np.ascontiguousarray(f(na_meta_bias)[0].T),
    }
    in_maps = []
    for c in range(NCORES):
        xw = np.zeros((WALL, D), np.float32)
        for j in range(WROWS):
            gr = 32 * c - 4 + j
            if 0 <= gr < ROWS:
                xw[j * 64:(j + 1) * 64] = x[gr * 64:(gr + 1) * 64]
        xw[WTOK:] = meta
        qm = np.zeros((WROWS, OWN), np.float32)
        for i in range(32):
            for j in range(WROWS):
                if not na_valid(c, i, j):
                    qm[j, i * 64:(i + 1) * 64] = NEG
        m = dict(shared)
        m["xTw"] = np.ascontiguousarray(xw.T)
        m["xown"] = np.ascontiguousarray(x[c * OWN:(c + 1) * OWN])
        m["csq"] = _rope_table(np.arange(c * OWN, (c + 1) * OWN) + NMETA)
        m["qmask"] = qm
        in_maps.append(m)
    return in_maps


def kernel(**inputs):
    in_maps = prepare_inputs(**inputs)
    nc = build_program()
    res = run_bass_kernel_spmd(nc, in_maps, core_ids=list(range(NCORES)))
    outs = [np.asarray(r["out"], dtype=np.float32) for r in res.results]
    return np.concatenate(outs, axis=0)[None, :, :]
```
